# Optimizing a Trainium2 kernel written in Bass

```python
import math
import jax, jax.numpy as jnp
from jax import lax
import numpy as np

D_MODEL = 1024
BATCH = 8
SEQ = 4096
DEPTH = 2

GRID_W = 64
HEAD_DIM = 64
BLOCK = 128
A_HEADS = 8
A_KV = 2
B_HEADS = 8
B_KV = 2
WINDOW = 128
C_HEADS = 16
NA_ROWS = 8
NA_COLS = 16
MEM_LEN = 256
M_HEADS = 4
M_HEAD_DIM = 128
MEM_W = M_HEADS * M_HEAD_DIM
REL_BUCKETS = 32
REL_MAX_DIST = 128
ROPE_THETA = 10000.0
EPS = 1e-6
MIX_EVEN = A_HEADS * HEAD_DIM + B_HEADS * HEAD_DIM + MEM_W
MIX_ODD = C_HEADS * HEAD_DIM + MEM_W
SPLIT_EVEN = [A_HEADS * HEAD_DIM, A_KV * HEAD_DIM, A_KV * HEAD_DIM,
              B_HEADS * HEAD_DIM, B_KV * HEAD_DIM, B_KV * HEAD_DIM,
              MEM_W, MIX_EVEN]
SPLIT_ODD = [C_HEADS * HEAD_DIM, C_HEADS * HEAD_DIM, C_HEADS * HEAD_DIM,
             MEM_W, MIX_ODD]
IN_EVEN = sum(SPLIT_EVEN)
IN_ODD = sum(SPLIT_ODD)
N_EVEN = (DEPTH + 1) // 2
N_ODD = DEPTH // 2

kernel_name = "hybrid_grid_encoder_block"


def rmsnorm(x, g):
    xf = x.astype(jnp.float32)
    y = xf * lax.rsqrt(jnp.mean(xf * xf, axis=-1, keepdims=True) + EPS)
    return (y * g.astype(jnp.float32)).astype(x.dtype)


def split_cols(z, sizes):
    offs = [int(v) for v in np.cumsum(sizes)[:-1]]
    return jnp.split(z, offs, axis=-1)


def _rotate_axis(xa, ang):
    x1, x2 = jnp.split(xa, 2, axis=-1)
    c = jnp.cos(ang)[None, :, None, :]
    s = jnp.sin(ang)[None, :, None, :]
    return jnp.concatenate([x1 * c - x2 * s, x1 * s + x2 * c], axis=-1)


def rope_2d(x, row, col):
    half = x.shape[-1] // 2
    nf = half // 2
    freqs = jnp.power(ROPE_THETA, -jnp.arange(nf, dtype=jnp.float32) / nf)
    xf = x.astype(jnp.float32)
    ang_r = row.astype(jnp.float32)[:, None] * freqs
    ang_c = col.astype(jnp.float32)[:, None] * freqs
    out = jnp.concatenate([_rotate_axis(xf[..., :half], ang_r),
                           _rotate_axis(xf[..., half:], ang_c)], axis=-1)
    return out.astype(x.dtype)


def t5_bucket(rel):
    nb = REL_BUCKETS // 2
    max_exact = nb // 2
    ret = jnp.where(rel > 0, nb, 0)
    n = jnp.abs(rel)
    nf = jnp.maximum(n, 1).astype(jnp.float32)
    large = max_exact + (jnp.log(nf / max_exact) / math.log(REL_MAX_DIST / max_exact)
                         * (nb - max_exact)).astype(jnp.int32)
    large = jnp.minimum(large, nb - 1)
    return ret + jnp.where(n < max_exact, n, large)


def global_gqa(q, k, v):
    B, S, Hq, dh = q.shape
    Hkv = k.shape[2]
    G = Hq // Hkv
    nb = S // BLOCK
    scale = dh ** -0.5
    qb = q.reshape(B, nb, BLOCK, Hkv, G, dh).transpose(1, 0, 2, 3, 4, 5)

    def one(qblk):
        s = jnp.einsum('bqkgd,bskd->bkgqs', qblk, k).astype(jnp.float32) * scale
        p = jax.nn.softmax(s, axis=-1).astype(v.dtype)
        return jnp.einsum('bkgqs,bskd->bqkgd', p, v)

    o = lax.map(one, qb)
    return o.transpose(1, 0, 2, 3, 4, 5).reshape(B, S, Hq * dh)


def window_gqa(q, k, v, sink, rel_bias):
    B, S, Hq, dh = q.shape
    Hkv = k.shape[2]
    G = Hq // Hkv
    nb = S // BLOCK
    span = BLOCK + 2 * WINDOW
    scale = dh ** -0.5
    kp = jnp.pad(k, ((0, 0), (WINDOW, WINDOW), (0, 0), (0, 0)))
    vp = jnp.pad(v, ((0, 0), (WINDOW, WINDOW), (0, 0), (0, 0)))
    qpos = jnp.arange(BLOCK)
    kpos = jnp.arange(span) - WINDOW
    rel = kpos[None, :] - qpos[:, None]
    band = jnp.abs(rel) <= WINDOW
    bias = rel_bias.astype(jnp.float32)[t5_bucket(rel)]
    bias = bias.transpose(2, 0, 1).reshape(Hkv, G, BLOCK, span)
    sk = sink.astype(jnp.float32).reshape(Hkv, G)[None, :, :, None, None]
    qb = q.reshape(B, nb, BLOCK, Hkv, G, dh).transpose(1, 0, 2, 3, 4, 5)

    def one(args):
        i, qblk = args
        start = i * BLOCK
        kb = lax.dynamic_slice_in_dim(kp, start, span, axis=1)
        vb = lax.dynamic_slice_in_dim(vp, start, span, axis=1)
        apos = start + kpos
        valid = band & (apos >= 0)[None, :] & (apos < S)[None, :]
        s = jnp.einsum('bqkgd,bskd->bkgqs', qblk, kb).astype(jnp.float32) * scale + bias
        s = jnp.where(valid, s, -jnp.inf)
        m = jnp.maximum(jnp.max(s, axis=-1, keepdims=True), sk)
        p = jnp.exp(s - m)
        denom = jnp.sum(p, axis=-1, keepdims=True) + jnp.exp(sk - m)
        return jnp.einsum('bkgqs,bskd->bqkgd', (p / denom).astype(v.dtype), vb)

    o = lax.map(one, (jnp.arange(nb), qb))
    return o.transpose(1, 0, 2, 3, 4, 5).reshape(B, S, Hq * dh)


def neighborhood_attn(q, k, v, rpb):
    B, S, H, dh = q.shape
    rows = S // GRID_W
    kr = min(NA_ROWS, rows)
    scale = dh ** -0.5
    qg = q.reshape(B, rows, GRID_W, H, dh)
    kg = k.reshape(B, rows, GRID_W, H, dh)
    vg = v.reshape(B, rows, GRID_W, H, dh)
    col = jnp.arange(GRID_W)
    cs = jnp.clip(col - NA_COLS // 2, 0, GRID_W - NA_COLS)
    colmask = (col[None, :] >= cs[:, None]) & (col[None, :] < cs[:, None] + NA_COLS)
    dc = jnp.clip(col[None, :] - col[:, None] + NA_COLS - 1, 0, 2 * NA_COLS - 2)
    rpb_cols = rpb.astype(jnp.float32)[:, :, dc]

    def one(r):
        rs = jnp.clip(r - kr // 2, 0, rows - kr)
        kblk = lax.dynamic_slice_in_dim(kg, rs, kr, axis=1)
        vblk = lax.dynamic_slice_in_dim(vg, rs, kr, axis=1)
        qblk = lax.dynamic_index_in_dim(qg, r, axis=1, keepdims=False)
        dr = rs + jnp.arange(kr) - r + NA_ROWS - 1
        bias = jnp.take(rpb_cols, dr, axis=1).transpose(0, 2, 1, 3)
        s = jnp.einsum('bqhd,bikhd->bhqik', qblk, kblk).astype(jnp.float32) * scale + bias
        s = jnp.where(colmask[:, None, :], s, -jnp.inf)
        p = jax.nn.softmax(s.reshape(B, H, GRID_W, kr * GRID_W), axis=-1)
        p = p.reshape(B, H, GRID_W, kr, GRID_W).astype(v.dtype)
        return jnp.einsum('bhqik,bikhd->bqhd', p, vblk)

    o = lax.map(one, jnp.arange(rows))
    return o.transpose(1, 0, 2, 3, 4).reshape(B, S, H * dh)


def memory_attn(q, mk, mv):
    B, S, Hm, dm = q.shape
    s = jnp.einsum('bshd,bmhd->bhsm', q, mk).astype(jnp.float32) * (dm ** -0.5)
    p = jax.nn.softmax(s, axis=-1).astype(mv.dtype)
    return jnp.einsum('bhsm,bmhd->bshd', p, mv).reshape(B, S, Hm * dm)


def setup_inputs(seed: int = 0) -> dict:
    key = jax.random.key(seed)
    ks = jax.random.split(key, 16)
    f32 = jnp.float32
    nrm = lambda k, shp: jax.random.normal(k, shp, dtype=f32)
    return {
        "x": nrm(ks[0], (BATCH, SEQ, D_MODEL)),
        "mem": nrm(ks[1], (BATCH, MEM_LEN, D_MODEL)),
        "norm_gain": 1.0 + 0.05 * nrm(ks[2], (DEPTH, D_MODEL)),
        "mem_norm_gain": 1.0 + 0.05 * nrm(ks[3], (D_MODEL,)),
        "w_in_even": nrm(ks[4], (N_EVEN, D_MODEL, IN_EVEN)) * D_MODEL ** -0.5,
        "w_out_even": nrm(ks[5], (N_EVEN, MIX_EVEN, D_MODEL)) * MIX_EVEN ** -0.5,
        "q_norm_a": 1.0 + 0.05 * nrm(ks[6], (N_EVEN, HEAD_DIM)),
        "k_norm_a": 1.0 + 0.05 * nrm(ks[7], (N_EVEN, HEAD_DIM)),
        "sink_b": 0.5 * nrm(ks[8], (N_EVEN, B_HEADS)),
        "rel_bias": 0.5 * nrm(ks[9], (REL_BUCKETS, B_HEADS)),
        "w_in_odd": nrm(ks[10], (N_ODD, D_MODEL, IN_ODD)) * D_MODEL ** -0.5,
        "w_out_odd": nrm(ks[11], (N_ODD, MIX_ODD, D_MODEL)) * MIX_ODD ** -0.5,
        "rpb_c": 0.5 * nrm(ks[12], (N_ODD, C_HEADS, 2 * NA_ROWS - 1, 2 * NA_COLS - 1)),
        "w_mem_kv": nrm(ks[13], (DEPTH, D_MODEL, 2 * MEM_W)) * D_MODEL ** -0.5,
        "final_norm_gain": 1.0 + 0.05 * nrm(ks[14], (D_MODEL,)),
    }


def reference(x, mem, norm_gain, mem_norm_gain, w_in_even, w_out_even, q_norm_a, k_norm_a,
              sink_b, rel_bias, w_in_odd, w_out_odd, rpb_c, w_mem_kv, final_norm_gain):
    B, S, _ = x.shape
    t = jnp.arange(S)
    grid_row = t // GRID_W
    grid_col = t % GRID_W
    memn = rmsnorm(mem, mem_norm_gain)
    Mlen = mem.shape[1]
    for l in range(DEPTH):
        h = rmsnorm(x, norm_gain[l])
        mk, mv = jnp.split(memn @ w_mem_kv[l], 2, axis=-1)
        mk = mk.reshape(B, Mlen, M_HEADS, M_HEAD_DIM)
        mv = mv.reshape(B, Mlen, M_HEADS, M_HEAD_DIM)
        if l % 2 == 0:
            e = l // 2
            qa, ka, va, qb, kb, vb, qm, gate = split_cols(h @ w_in_even[e], SPLIT_EVEN)
            qa = rope_2d(rmsnorm(qa.reshape(B, S, A_HEADS, HEAD_DIM), q_norm_a[e]), grid_row, grid_col)
            ka = rope_2d(rmsnorm(ka.reshape(B, S, A_KV, HEAD_DIM), k_norm_a[e]), grid_row, grid_col)
            ya = global_gqa(qa, ka, va.reshape(B, S, A_KV, HEAD_DIM))
            yb = window_gqa(qb.reshape(B, S, B_HEADS, HEAD_DIM),
                            kb.reshape(B, S, B_KV, HEAD_DIM),
                            vb.reshape(B, S, B_KV, HEAD_DIM), sink_b[e], rel_bias)
            ym = memory_attn(qm.reshape(B, S, M_HEADS, M_HEAD_DIM), mk, mv)
            y = jnp.concatenate([ya, yb, ym], axis=-1) * jax.nn.silu(gate)
            x = x + y @ w_out_even[e]
        else:
            o = l // 2
            qc, kc, vc, qm, gate = split_cols(h @ w_in_odd[o], SPLIT_ODD)
            yc = neighborhood_attn(qc.reshape(B, S, C_HEADS, HEAD_DIM),
                                   kc.reshape(B, S, C_HEADS, HEAD_DIM),
                                   vc.reshape(B, S, C_HEADS, HEAD_DIM), rpb_c[o])
            ym = memory_attn(qm.reshape(B, S, M_HEADS, M_HEAD_DIM), mk, mv)
            y = jnp.concatenate([yc, ym], axis=-1) * jax.nn.silu(gate)
            x = x + y @ w_out_odd[o]
    return rmsnorm(x, final_norm_gain)
```

```python
from contextlib import ExitStack
import math
import numpy as np
import concourse.bass as bass
import concourse.mybir as mybir
from concourse.bass_utils import run_bass_kernel_spmd

F32 = mybir.dt.float32
BF16 = mybir.dt.bfloat16
ALU = mybir.AluOpType
AF = mybir.ActivationFunctionType
AX = mybir.AxisListType

ENGS = ("pe", "act", "dve", "pool", "sp")
DMA_SLOTS = 12

S = 4096
D = 1024
NT = 32
NB = 8
EPS = 1e-6


class Buf:
    __slots__ = ("w", "r", "name")

    def __init__(self, name=""):
        self.w = None
        self.r = []
        self.name = name


class Op:
    __slots__ = ("eng", "fn", "deps", "needs_inc", "inc_val", "dma", "slot", "slot_val", "id")

    def __init__(self, eng, fn, dma):
        self.eng = eng
        self.fn = fn
        self.dma = dma
        self.deps = set()
        self.needs_inc = False
        self.inc_val = None
        self.slot = None
        self.slot_val = None
        self.id = None


class Prog:
    def __init__(self, nc):
        self.nc = nc
        self.ops = []
        self.dma_count = {e: 0 for e in ENGS}
        self.dma_hist = {e: [] for e in ENGS}
        self.last_op = {e: None for e in ENGS}

    def add(self, eng, fn, reads=(), writes=(), dma=False):
        op = Op(eng, fn, dma)
        op.id = len(self.ops)
        deps = set()
        for b in reads:
            if b.w is not None:
                deps.add(b.w)
        for b in writes:
            if b.w is not None:
                deps.add(b.w)
            for r in b.r:
                deps.add(r)
        for b in reads:
            b.r.append(op)
        for b in writes:
            b.w = op
            b.r = []
        for d in deps:
            if d is op:
                continue
            if (not d.dma) and (not dma) and d.eng == "pe" and eng == "pe":
                continue
            op.deps.add(d)
            if not d.dma:
                d.needs_inc = True
        if dma:
            n = self.dma_count[eng]
            op.slot = n % DMA_SLOTS
            op.slot_val = 16 * (n // DMA_SLOTS + 1)
            hist = self.dma_hist[eng]
            if n >= DMA_SLOTS:
                op.deps.add(hist[n - DMA_SLOTS])
            hist.append(op)
            self.dma_count[eng] = n + 1
        else:
            self.last_op[eng] = op
        self.ops.append(op)
        return op

    def barrier(self):
        lasts = [self.last_op[e] for e in ENGS if self.last_op[e] is not None]
        dmas = []
        for e in ENGS:
            dmas += self.dma_hist[e][-DMA_SLOTS:]
        for e in ENGS:
            op = Op(e, None, False)
            op.id = len(self.ops)
            for d in lasts:
                if d.eng != e:
                    op.deps.add(d)
                    d.needs_inc = True
            for d in dmas:
                op.deps.add(d)
            self.ops.append(op)

    def emit(self):
        nc = self.nc
        with ExitStack() as st:
            sem = {e: st.enter_context(nc.semaphore("c_" + e)) for e in ENGS}
            dsem = {}
            for e in ENGS:
                if self.dma_count[e] > 0:
                    dsem[e] = [st.enter_context(nc.semaphore("d_%s_%d" % (e, i))) for i in range(DMA_SLOTS)]
            cnt = {e: 0 for e in ENGS}
            for op in self.ops:
                if op.dma or op.fn is None:
                    continue
                if op.needs_inc:
                    cnt[op.eng] += 1
                    op.inc_val = cnt[op.eng]
            per_eng = {e: [] for e in ENGS}
            seen = {e: {} for e in ENGS}
            for op in self.ops:
                waits = {}
                for d in op.deps:
                    if d.dma:
                        key = ("d", d.eng, d.slot)
                        val = d.slot_val
                    else:
                        key = ("c", d.eng)
                        val = d.inc_val
                    if waits.get(key, 0) < val:
                        waits[key] = val
                wl = []
                s = seen[op.eng]
                for key, val in waits.items():
                    if s.get(key, 0) >= val:
                        continue
                    s[key] = val
                    wl.append((key, val))
                per_eng[op.eng].append((op, wl))
            block = st.enter_context(nc.Block())

            def run(engname, e):
                for op, wl in per_eng[engname]:
                    for key, val in wl:
                        if key[0] == "d":
                            e.wait_ge(dsem[key[1]][key[2]], val)
                        else:
                            e.wait_ge(sem[key[1]], val)
                    if op.fn is None:
                        continue
                    ins = op.fn(e)
                    if op.dma:
                        ins.then_inc(dsem[engname][op.slot], 16)
                    elif op.needs_inc:
                        ins.then_inc(sem[engname], 1)

            @block.tensor
            def _(e):
                run("pe", e)

            @block.scalar
            def _(e):
                run("act", e)

            @block.vector
            def _(e):
                run("dve", e)

            @block.gpsimd
            def _(e):
                run("pool", e)

            @block.sync
            def _(e):
                run("sp", e)


def _t5_bucket(rel):
    nb = 16
    max_exact = 8
    ret = np.where(rel > 0, nb, 0)
    n = np.abs(rel)
    nf = np.maximum(n, 1).astype(np.float32)
    large = max_exact + (np.log(nf / np.float32(max_exact)) / np.float32(math.log(128 / max_exact))
                         * np.float32(nb - max_exact)).astype(np.int32)
    large = np.minimum(large, nb - 1)
    return ret + np.where(n < max_exact, n, large)


def host_consts():
    c = {}
    c["ident"] = np.eye(128, dtype=np.float32)
    R = np.zeros((128, 128), np.float32)
    for p in range(128):
        sub = p % 32
        if sub < 16:
            R[p, p + 16] = -1.0
        else:
            R[p, p - 16] = 1.0
    c["rotT"] = np.ascontiguousarray(R.T)
    bd = np.zeros((128, 128), np.float32)
    bd[0:64, 0:64] = 1.0
    bd[64:128, 64:128] = 1.0
    c["bd"] = bd
    t = np.arange(S)
    row = (t // 64).astype(np.float32)
    col = (t % 64).astype(np.float32)
    freqs = np.power(np.float32(10000.0), -np.arange(16, dtype=np.float32) / np.float32(16)).astype(np.float32)
    cosT = np.zeros((128, S), np.float32)
    sinT = np.zeros((128, S), np.float32)
    for p in range(128):
        dh = p % 64
        pos = row if dh < 32 else col
        ang = (pos * freqs[dh % 16]).astype(np.float32)
        cosT[p] = np.cos(ang)
        sinT[p] = np.sin(ang)
    c["cosT"] = cosT
    c["sinT"] = sinT
    rel = np.arange(-256, 384)
    bk = _t5_bucket(rel)
    oh1 = np.zeros((32, 640), np.float32)
    for u, r in enumerate(rel):
        if abs(r) <= 128:
            oh1[bk[u], u] = 1.0
    c["oh1"] = oh1
    ohc = np.zeros((31, 64, 128), np.float32)
    for qc in range(64):
        cs = min(max(qc - 8, 0), 48)
        for kc in range(cs, cs + 16):
            dc = kc - qc + 15
            ohc[dc, qc, kc] = 1.0
            ohc[dc, qc, 64 + kc] = 1.0
    c["ohc"] = ohc.reshape(31, 64 * 128)
    return c


class Builder:
    def __init__(self, debug=None, stop_after=None):
        self.debug = debug or ()
        self.stop_after = stop_after
        self.nc = bass.Bass("TRN2", target_bir_lowering=False)
        self.P = Prog(self.nc)
        self.dram = {}

    def din(self, name, shape, dt=F32):
        t = self.nc.dram_tensor(name, list(shape), dt, kind="ExternalInput")
        self.dram[name] = t
        return t

    def dscratch(self, name, shape, dt):
        kind = "ExternalOutput" if name in self.debug else "Internal"
        t = self.nc.dram_tensor(name, list(shape), dt, kind=kind)
        self.dram[name] = t
        return t

    def reset_arena(self):
        self.aoff = 0

    def alloc(self, ncols, dt=F32):
        nbytes = ncols * (4 if dt == F32 else 2)
        n32 = (nbytes + 3) // 4
        n32 = (n32 + 1) // 2 * 2
        a = self.ARENA[:, self.aoff:self.aoff + n32]
        self.aoff += n32
        assert self.aoff <= self.ARENA_N, "arena overflow %d > %d" % (self.aoff, self.ARENA_N)
        if dt == F32:
            return a[:, 0:ncols]
        return a.bitcast(BF16)[:, 0:ncols]

    def dma(self, out, in_, reads=(), writes=(), q="sp", **kw):
        return self.P.add(q, lambda e: e.dma_start(out=out, in_=in_, **kw), reads, writes, dma=True)

    def mm(self, out, lhsT, rhs, start, stop, reads=(), writes=()):
        return self.P.add("pe", lambda e: e.matmul(out, lhsT=lhsT, rhs=rhs, start=start, stop=stop), reads, writes)

    def tr(self, out, in_, ident, reads=(), writes=()):
        return self.P.add("pe", lambda e: e.transpose(out, in_, ident), reads, writes)

    def act(self, out, in_, func, reads=(), writes=(), **kw):
        return self.P.add("act", lambda e: e.activation(out=out, in_=in_, func=func, **kw), reads, writes)

    def tt(self, eng, out, in0, in1, op, reads=(), writes=()):
        return self.P.add(eng, lambda e: e.tensor_tensor(out=out, in0=in0, in1=in1, op=op), reads, writes)

    def ts(self, eng, out, in0, s1, op0, reads=(), writes=(), s2=None, op1=None):
        if op1 is None:
            return self.P.add(eng, lambda e: e.tensor_scalar(out=out, in0=in0, scalar1=s1, scalar2=None, op0=op0), reads, writes)
        return self.P.add(eng, lambda e: e.tensor_scalar(out=out, in0=in0, scalar1=s1, scalar2=s2, op0=op0, op1=op1), reads, writes)

    def cp(self, eng, out, in_, reads=(), writes=()):
        if eng == "act":
            return self.P.add("act", lambda e: e.copy(out=out, in_=in_), reads, writes)
        return self.P.add(eng, lambda e: e.tensor_copy(out=out, in_=in_), reads, writes)

    def memset(self, eng, ap, val, writes=()):
        return self.P.add(eng, lambda e: e.memset(ap, val), (), writes)

    def recip(self, out, in_, reads=(), writes=()):
        return self.P.add("dve", lambda e: e.reciprocal(out=out, in_=in_), reads, writes)

    def build(self):
        nc = self.nc
        self.x = self.din("x", [S, D])
        self.mem = self.din("mem", [256, D])
        self.w_in = [self.din("w_in_even", [D, 3584]), self.din("w_in_odd", [D, 5120])]
        self.w_out = [self.din("w_out_even", [1536, D]), self.din("w_out_odd", [1536, D])]
        self.w_mem = self.din("w_mem_kv", [2, D, 1024])
        self.gains = self.din("gains_pp", [128, 24])
        self.qkg = self.din("qk_gain", [128, 2])
        self.fgain = self.din("final_gain", [1, D])
        self.sink = self.din("sink_b", [1, 8])
        self.relb = self.din("rel_bias", [32, 8])
        self.rpbT = self.din("rpbT", [31, 240])
        self.c_ident = self.din("ident", [128, 128])
        self.c_rotT = self.din("rotT", [128, 128])
        self.c_bd = self.din("bd", [128, 128])
        self.c_cos = self.din("cosT", [128, S])
        self.c_sin = self.din("sinT", [128, S])
        self.c_oh1 = self.din("oh1", [32, 640])
        self.c_ohc = self.din("ohc", [31, 64 * 128])
        self.out = nc.dram_tensor("out", [S, D], F32, kind="ExternalOutput")
        self.QA_T = self.dscratch("QA_T", [512, S], BF16)
        self.KA_T = self.dscratch("KA_T", [256, S], BF16)
        self.QB_T = self.dscratch("QB_T", [512, S], BF16)
        self.KB_T = self.dscratch("KB_T", [256, S], BF16)
        self.QM_T = self.dscratch("QM_T", [512, S], BF16)
        self.G_T = self.dscratch("G_T", [1536, S], BF16)
        self.VAB = self.dscratch("VAB", [S, 768], BF16)
        self.X1 = self.dscratch("X1", [S, D], F32)
        self.QC_T = self.dscratch("QC_T", [1024, S], BF16)
        self.KC_T = self.dscratch("KC_T", [1024, S], BF16)
        self.VC = self.dscratch("VC", [S, 1536], BF16)
        self.MK_T = self.dscratch("MK_T", [2, 512, 256], BF16)
        self.MV = self.dscratch("MV", [2, 256, 512], BF16)

        with ExitStack() as st:
            self.ARENA_N = 52000
            self.ARENA = st.enter_context(nc.sbuf_tensor("arena", [128, self.ARENA_N], F32))
            self.PS = st.enter_context(nc.psum_tensor("ps", [128, 4096], F32))
            self.bank = [self.PS[:, i * 512:(i + 1) * 512] for i in range(8)]
            self.bb = [Buf("bank%d" % i) for i in range(8)]
            self.phase_mem()
            self.P.barrier()
            for layer in range(2):
                self.phase_proj(layer)
                self.P.barrier()
                if self.stop_after == ("proj", layer):
                    break
                self.phase_attn(layer)
                self.P.barrier()
                if self.stop_after == ("attn", layer):
                    break
            self.P.emit()
        return nc

    def load_consts(self):
        c = {}
        c["ident"] = self.alloc(128)
        c["b_ident"] = Buf()
        self.dma(c["ident"], self.c_ident.ap(), writes=[c["b_ident"]])
        c["gains"] = self.alloc(24)
        c["b_gains"] = Buf()
        self.dma(c["gains"], self.gains.ap(), writes=[c["b_gains"]])
        return c

    def norm_transpose(self, c, xt, bx, ht3, bht, j, gcol, tp_banks, btp, scr):
        junk, bjunk, stat, bstat = scr
        self.act(junk, xt, AF.Square, reads=[bx], writes=[bjunk, bstat], accum_out=stat[:, 0:1])
        self.ts("dve", stat[:, 1:2], stat[:, 0:1], 1.0 / D, ALU.mult, reads=[bstat], writes=[bstat], s2=EPS, op1=ALU.add)
        self.act(stat[:, 2:3], stat[:, 1:2], AF.Sqrt, reads=[bstat], writes=[bstat])
        self.recip(stat[:, 3:4], stat[:, 2:3], reads=[bstat], writes=[bstat])
        self.ts("dve", xt, xt, stat[:, 3:4], ALU.mult, reads=[bx, bstat], writes=[bx])
        tp = tp_banks
        for kc in range(8):
            self.tr(tp[:, kc * 128:(kc + 1) * 128], xt[:, kc * 128:(kc + 1) * 128], c["ident"],
                    reads=[bx, c["b_ident"]], writes=list(btp))
        g = c["gains"][:, gcol:gcol + 8]
        gb = g.unsqueeze(2).broadcast_to([128, 8, 128])
        tp3 = tp.rearrange("p (a b) -> p a b", a=8)
        self.tt("dve", ht3[:, :, j * 128:(j + 1) * 128], tp3, gb, ALU.mult,
                reads=list(btp) + [c["b_gains"]], writes=[bht])

    def phase_mem(self):
        self.reset_arena()
        c = self.load_consts()
        W = self.alloc(2 * 8 * 1024, BF16).rearrange("p (l k n) -> p l k n", l=2, k=8)
        bW = Buf()
        for l in range(2):
            self.dma(W[:, l], self.w_mem.ap()[l].rearrange("(k p) n -> p k n", p=128), writes=[bW], q="pool")
        ht3 = self.alloc(8 * 256, BF16).rearrange("p (k t) -> p k t", k=8)
        bht = Buf()
        junk = self.alloc(1024, BF16)
        scr = (junk, Buf(), self.alloc(4), Buf())
        tpb = self.PS[:, 0:1024]
        btp = [self.bb[0], self.bb[1]]
        for j in range(2):
            xt = self.alloc(1024)
            bx = Buf()
            self.dma(xt, self.mem.ap()[j * 128:(j + 1) * 128, :], writes=[bx])
            self.norm_transpose(c, xt, bx, ht3, bht, j, 16, tpb, btp, scr)
        stg = self.alloc(4 * 256, BF16).rearrange("p (c t) -> p c t", c=4)
        stv = self.alloc(2 * 512, BF16).rearrange("p (c t) -> p c t", c=2)
        bst, bsv = Buf(), Buf()
        for l in range(2):
            for hm in range(4):
                pb = self.bank[2 + hm % 2]
                bpb = self.bb[2 + hm % 2]
                for kc in range(8):
                    self.mm(pb[:, 0:256], W[:, l, kc, hm * 128:(hm + 1) * 128], ht3[:, kc, :], kc == 0, kc == 7,
                            reads=[bW, bht], writes=[bpb])
                self.cp("dve", stg[:, hm, :], pb[:, 0:256], reads=[bpb], writes=[bst])
            self.dma(self.MK_T.ap()[l].rearrange("(c p) t -> p c t", p=128), stg, reads=[bst])
            for mt in range(2):
                pb = self.bank[4 + mt]
                bpb = self.bb[4 + mt]
                for kc in range(8):
                    self.mm(pb, ht3[:, kc, mt * 128:(mt + 1) * 128], W[:, l, kc, 512:1024], kc == 0, kc == 7,
                            reads=[bW, bht], writes=[bpb])
                self.cp("act", stv[:, mt, :], pb, reads=[bpb], writes=[bsv])
            self.dma(self.MV.ap()[l].rearrange("(c p) n -> p c n", p=128), stv, reads=[bsv])

    def phase_proj(self, layer):
        self.reset_arena()
        c = self.load_consts()
        if layer == 0:
            wl = [(0, 512, 0)]
            wl += [(512, 64, 512), (512, 64, 576), (576, 64, 640), (576, 64, 704)]
            wl += [(768, 512, 768)]
            wl += [(1280, 64, 1280), (1280, 64, 1344), (1344, 64, 1408), (1344, 64, 1472)]
            wl += [(1536, 512, 1536), (2048, 1536, 2048), (640, 128, 3584), (1408, 128, 3712)]
            NC = 3840
            groups = [("qa", 0, 4, self.QA_T, "rope_q"), ("ka", 512, 2, self.KA_T, "rope_k"),
                      ("qb", 768, 4, self.QB_T, "copy"), ("kb", 1280, 2, self.KB_T, "copy"),
                      ("qm", 1536, 4, self.QM_T, "copy"), ("gate", 2048, 12, self.G_T, "silu")]
            vcol, vn = 3584, 256
        else:
            wl = [(0, 2048, 0), (3072, 2048, 2048), (2048, 1024, 4096)]
            NC = 5120
            groups = [("qc", 0, 8, self.QC_T, "copy"), ("kc", 1024, 8, self.KC_T, "copy"),
                      ("qm", 2048, 4, self.QM_T, "copy"), ("gate", 2560, 12, self.G_T, "silu")]
            vcol, vn = 4096, 1024
        W = self.alloc(8 * NC, BF16).rearrange("p (k n) -> p k n", k=8)
        bW = Buf()
        wsrc = self.w_in[layer].ap().rearrange("(k p) n -> p k n", p=128)
        for (s0, n, d0) in wl:
            for a in range(0, n, 1024):
                m = min(1024, n - a)
                self.dma(W[:, :, d0 + a:d0 + a + m], wsrc[:, :, s0 + a:s0 + a + m], writes=[bW], q="pool")
        if layer == 0:
            rotT = self.alloc(128)
            bd = self.alloc(128)
            qkg = self.alloc(2)
            bcst = Buf()
            self.dma(rotT, self.c_rotT.ap(), writes=[bcst])
            self.dma(bd, self.c_bd.ap(), writes=[bcst])
            self.dma(qkg, self.qkg.ap(), writes=[bcst])
            cs = [(self.alloc(512), self.alloc(512), Buf()) for _ in range(2)]
            tq = self.alloc(512)
            tsq = self.alloc(512)
            t1 = self.alloc(512)
            trs = self.alloc(512)
            btq, btsq, bt1, btrs = Buf(), Buf(), Buf(), Buf()
        XT = [self.alloc(1024) for _ in range(4)]
        bXT = [Buf() for _ in range(4)]
        HT = [self.alloc(8 * 512, BF16).rearrange("p (k t) -> p k t", k=8) for _ in range(2)]
        bHT = [Buf() for _ in range(2)]
        junk = self.alloc(1024, BF16)
        scr = (junk, Buf(), self.alloc(4), Buf())
        tpb = self.PS[:, 0:1024]
        btp = [self.bb[0], self.bb[1]]
        stg = {}
        for (name, col, nch, dst, kind) in groups:
            stg[name] = [(self.alloc(nch * 512, BF16).rearrange("p (c t) -> p c t", c=nch), Buf()) for _ in range(2)]
        if layer == 0:
            VST = [(self.alloc(768, BF16), Buf()) for _ in range(2)]
        else:
            VST = [(self.alloc(1536, BF16), Buf()) for _ in range(2)]
        for (v, bv) in VST:
            self.memset("pool", v, 1.0, writes=[bv])
        pbanks = [self.bank[2], self.bank[3], self.bank[4]]
        bpb = [self.bb[2], self.bb[3], self.bb[4]]
        aux = [self.bank[5], self.bank[6]]
        baux = [self.bb[5], self.bb[6]]
        vbank = self.bank[7]
        bvb = self.bb[7]
        xsrc = self.x if layer == 0 else self.X1
        gcol = 0 if layer == 0 else 8

        def load_x(b):
            for j in range(4):
                t = b * 4 + j
                self.dma(XT[j], xsrc.ap()[t * 128:(t + 1) * 128, :], writes=[bXT[j]])

        def norm_tr(b):
            for j in range(4):
                self.norm_transpose(c, XT[j], bXT[j], HT[b % 2], bHT[b % 2], j, gcol, tpb, btp, scr)

        state = {"pi": 0, "vi": 0}

        def do_chunk(b, name, col, ci, kind, sbuf, bs):
            i = state["pi"] % 3
            state["pi"] += 1
            pb, bp = pbanks[i], bpb[i]
            ht, bh = HT[b % 2], bHT[b % 2]
            for kc in range(8):
                self.mm(pb, W[:, kc, col + ci * 128:col + (ci + 1) * 128], ht[:, kc, :], kc == 0, kc == 7,
                        reads=[bW, bh], writes=[bp])
            dst = sbuf[:, ci, :]
            if kind == "copy":
                self.cp("act" if (state["pi"] % 2 == 0) else "dve", dst, pb, reads=[bp], writes=[bs])
            elif kind == "silu":
                self.act(dst, pb, AF.Silu, reads=[bp], writes=[bs])
            else:
                gi = 0 if kind == "rope_q" else 1
                cosb, sinb, bcs = cs[b % 2]
                self.act(tq, pb, AF.Copy, reads=[bp, bcst], writes=[btq], scale=qkg[:, gi:gi + 1])
                self.act(tsq, pb, AF.Square, reads=[bp], writes=[btsq])
                self.mm(aux[0], bd, tsq, True, True, reads=[bcst, btsq], writes=[baux[0]])
                self.mm(aux[1], rotT, tq, True, True, reads=[bcst, btq], writes=[baux[1]])
                self.ts("dve", trs, aux[0], 1.0 / 64, ALU.mult, reads=[baux[0]], writes=[btrs], s2=EPS, op1=ALU.add)
                self.act(trs, trs, AF.Sqrt, reads=[btrs], writes=[btrs])
                self.recip(trs, trs, reads=[btrs], writes=[btrs])
                self.tt("pool", t1, tq, cosb, ALU.mult, reads=[btq, bcs], writes=[bt1])
                self.tt("dve", tsq, aux[1], sinb, ALU.mult, reads=[baux[1], bcs, btsq], writes=[btsq])
                self.tt("pool", t1, t1, tsq, ALU.add, reads=[bt1, btsq], writes=[bt1])
                self.tt("dve", dst, t1, trs, ALU.mult, reads=[bt1, btrs], writes=[bs])

        def do_v(b):
            ht, bh = HT[b % 2], bHT[b % 2]
            for j in range(4):
                t = b * 4 + j
                v, bv = VST[state["vi"] % 2]
                state["vi"] += 1
                if layer == 0:
                    for kc in range(8):
                        self.mm(vbank[:, 0:256], ht[:, kc, j * 128:(j + 1) * 128], W[:, kc, vcol:vcol + 256],
                                kc == 0, kc == 7, reads=[bW, bh], writes=[bvb])
                    v3 = v.rearrange("p (g s) -> p g s", g=4)
                    src = vbank[:, 0:256].rearrange("p (g d) -> p g d", g=4)
                    self.cp("dve", v3[:, :, 0:64], src, reads=[bvb], writes=[bv])
                    self.cp("act", v3[:, :, 128:192], src, reads=[bvb], writes=[bv])
                    self.dma(self.VAB.ap()[t * 128:(t + 1) * 128, :], v, reads=[bv])
                else:
                    v3 = v.rearrange("p (g s) -> p g s", g=8)
                    for half in range(2):
                        for kc in range(8):
                            self.mm(vbank, ht[:, kc, j * 128:(j + 1) * 128],
                                    W[:, kc, vcol + half * 512:vcol + (half + 1) * 512],
                                    kc == 0, kc == 7, reads=[bW, bh], writes=[bvb])
                        src = vbank.rearrange("p (g e d) -> p g e d", g=4, e=2)
                        self.cp("dve", v3[:, half * 4:(half + 1) * 4, 0:64], src[:, :, 0, :], reads=[bvb], writes=[bv])
                        self.cp("act", v3[:, half * 4:(half + 1) * 4, 128:192], src[:, :, 1, :], reads=[bvb], writes=[bv])
                    self.dma(self.VC.ap()[t * 128:(t + 1) * 128, :], v, reads=[bv])

        chunks = []
        for (name, col, nch, dst, kind) in groups:
            for ci in range(nch):
                chunks.append((name, col, ci, nch, dst, kind))
        load_x(0)
        norm_tr(0)
        for b in range(NB):
            if layer == 0:
                cosb, sinb, bcs = cs[b % 2]
                self.dma(cosb, self.c_cos.ap()[:, b * 512:(b + 1) * 512], writes=[bcs])
                self.dma(sinb, self.c_sin.ap()[:, b * 512:(b + 1) * 512], writes=[bcs])
            if b + 1 < NB:
                load_x(b + 1)
            half_n = len(chunks) // 2
            for idx, (name, col, ci, nch, dst, kind) in enumerate(chunks):
                if idx == half_n and b + 1 < NB:
                    norm_tr(b + 1)
                sbuf, bs = stg[name][b % 2]
                do_chunk(b, name, col, ci, kind, sbuf, bs)
                if ci == nch - 1:
                    self.dma(dst.ap().rearrange("(c p) t -> p c t", p=128)[:, :, b * 512:(b + 1) * 512], sbuf, reads=[bs])
            do_v(b)

    def phase_attn(self, layer):
        self.reset_arena()
        P = self.P
        WOUT = self.alloc(12 * 1024, BF16).rearrange("p (c n) -> p c n", c=12)
        bWOUT = Buf()
        self.dma(WOUT, self.w_out[layer].ap().rearrange("(c p) n -> p c n", p=128), writes=[bWOUT], q="pool")
        ones_bf = self.alloc(128, BF16)
        bones = Buf()
        self.memset("pool", ones_bf, 1.0, writes=[bones])
        MKT = self.alloc(4 * 256, BF16).rearrange("p (c t) -> p c t", c=4)
        MVs = self.alloc(2 * 512, BF16).rearrange("p (c n) -> p c n", c=2)
        bMK, bMV = Buf(), Buf()
        self.dma(MKT, self.MK_T.ap()[layer].rearrange("(c p) t -> p c t", p=128), writes=[bMK])
        self.dma(MVs, self.MV.ap()[layer].rearrange("(c p) n -> p c n", p=128), writes=[bMV])
        QM = self.alloc(4 * 512, BF16).rearrange("p (c t) -> p c t", c=4)
        G = self.alloc(12 * 512, BF16).rearrange("p (c t) -> p c t", c=12)
        YT = self.alloc(12 * 512, BF16).rearrange("p (c t) -> p c t", c=12)
        bQM, bG = Buf(), Buf()
        bYT = [Buf() for _ in range(12)]
        nxb = 2 if layer == 0 else 1
        XR = [(self.alloc(1024), Buf()) for _ in range(nxb)]
        OUTT = [(self.alloc(1024), Buf()) for _ in range(nxb)]
        PT = [(self.alloc(640, BF16), Buf()) for _ in range(6)]
        RD = [(self.alloc(512), Buf()) for _ in range(2)]
        TN = [(self.alloc(512), Buf()) for _ in range(2)]
        st = {"s": 0, "pt": 0, "o": 0, "rd": 0, "x": 0}

        def sbank():
            i = st["s"] % 4
            st["s"] += 1
            return self.bank[i], self.bb[i]

        def obank(n=1):
            if n == 2 and st["o"] % 2 == 1:
                st["o"] += 1
            r = []
            for _ in range(n):
                i = 4 + st["o"] % 4
                st["o"] += 1
                r.append((self.bank[i], self.bb[i]))
            return r

        def ptbuf():
            i = st["pt"] % 6
            st["pt"] += 1
            return PT[i]

        def normalize(ob, bob, o_rows, d_rows, ci, add_scalar=None):
            i = st["rd"] % 2
            st["rd"] += 1
            rd, brd = RD[i]
            tn, btn = TN[i]
            o0, o1 = o_rows
            d0, d1 = d_rows
            if add_scalar is not None:
                self.ts("dve", rd[d0:d1, :], ob[d0:d1, :], add_scalar[0], ALU.add, reads=[bob, add_scalar[1]], writes=[brd])
                self.recip(rd[d0:d1, :], rd[d0:d1, :], reads=[brd], writes=[brd])
            elif layer == 1:
                self.act(rd[d0:d1, :], ob[d0:d1, :], AF.Ln, reads=[bob], writes=[brd])
                self.act(rd[d0:d1, :], rd[d0:d1, :], AF.Exp, reads=[brd], writes=[brd], scale=-1.0)
            else:
                self.recip(rd[d0:d1, :], ob[d0:d1, :], reads=[bob], writes=[brd])
            self.tt("dve", tn[o0:o1, :], ob[o0:o1, :], rd[d0:d1, :], ALU.mult, reads=[bob, brd], writes=[btn])
            self.tt("pool", YT[o0:o1, ci, :], tn[o0:o1, :], G[o0:o1, ci, :], ALU.mult, reads=[btn, bG], writes=[bYT[ci]])

        def mem_attn(b):
            sc = 128.0 ** -0.5
            for hm in range(4):
                (ob, bob), (db, bdb) = obank(2)
                pts = []
                for mt in range(2):
                    sb, bsb = sbank()
                    self.mm(sb, MKT[:, hm, mt * 128:(mt + 1) * 128], QM[:, hm, :], True, True, reads=[bMK, bQM], writes=[bsb])
                    pt, bpt = ptbuf()
                    self.act(pt[:, 0:512], sb, AF.Exp, reads=[bsb], writes=[bpt], scale=sc)
                    pts.append((pt, bpt))
                for mt in range(2):
                    pt, bpt = pts[mt]
                    self.mm(ob, MVs[:, mt, hm * 128:(hm + 1) * 128], pt[:, 0:512], mt == 0, mt == 1, reads=[bMV, bpt], writes=[bob])
                for mt in range(2):
                    pt, bpt = pts[mt]
                    self.mm(db, ones_bf, pt[:, 0:512], mt == 0, mt == 1, reads=[bones, bpt], writes=[bdb])
                i = st["rd"] % 2
                st["rd"] += 1
                rd, brd = RD[i]
                tn, btn = TN[i]
                if layer == 1:
                    self.act(rd, db, AF.Ln, reads=[bdb], writes=[brd])
                    self.act(rd, rd, AF.Exp, reads=[brd], writes=[brd], scale=-1.0)
                else:
                    self.recip(rd, db, reads=[bdb], writes=[brd])
                self.tt("dve", tn, ob, rd, ALU.mult, reads=[bob, brd], writes=[btn])
                self.tt("pool", YT[:, 8 + hm, :], tn, G[:, 8 + hm, :], ALU.mult, reads=[btn, bG], writes=[bYT[8 + hm]])

        def out_proj(b):
            for jt in range(4):
                t = b * 4 + jt
                xr, bxr = XR[st["x"] % nxb]
                ot, bot = OUTT[st["x"] % nxb]
                st["x"] += 1
                src = self.x if layer == 0 else self.X1
                self.dma(xr, src.ap()[t * 128:(t + 1) * 128, :], writes=[bxr])
                obs = obank(2)
                for nh in range(2):
                    ob, bob = obs[nh]
                    for cc in range(12):
                        self.mm(ob, YT[:, cc, jt * 128:(jt + 1) * 128], WOUT[:, cc, nh * 512:(nh + 1) * 512],
                                cc == 0, cc == 11, reads=[bYT[cc], bWOUT], writes=[bob])
                    self.tt("dve", ot[:, nh * 512:(nh + 1) * 512], ob, xr[:, nh * 512:(nh + 1) * 512], ALU.add,
                            reads=[bob, bxr], writes=[bot])
                if layer == 0:
                    self.dma(self.X1.ap()[t * 128:(t + 1) * 128, :], ot, reads=[bot])
                else:
                    self.act(fjunk, ot, AF.Square, reads=[bot], writes=[bfj, bfs], accum_out=fstat[:, 0:1])
                    self.ts("dve", fstat[:, 1:2], fstat[:, 0:1], 1.0 / D, ALU.mult, reads=[bfs], writes=[bfs], s2=EPS, op1=ALU.add)
                    self.act(fstat[:, 2:3], fstat[:, 1:2], AF.Sqrt, reads=[bfs], writes=[bfs])
                    self.recip(fstat[:, 3:4], fstat[:, 2:3], reads=[bfs], writes=[bfs])
                    P.add("dve", lambda e, ot=ot: e.scalar_tensor_tensor(out=ot, in0=ot, scalar=fstat[:, 3:4], in1=FG,
                                                                       op0=ALU.mult, op1=ALU.mult),
                          reads=[bot, bfs, bFG], writes=[bot])
                    self.dma(self.out.ap()[t * 128:(t + 1) * 128, :], ot, reads=[bot])

        if layer == 0:
            self.attn_layer0(locals())
        else:
            FG = self.alloc(1024)
            bFG = Buf()
            self.dma(FG, self.fgain.ap().broadcast_to([128, D]), writes=[bFG])
            fjunk = self.alloc(1024, BF16)
            fstat = self.alloc(4)
            bfj, bfs = Buf(), Buf()
            self.attn_layer1(locals())

    def attn_layer0(self, L):
        P = self.P
        QM, G, YT, bQM, bG, bYT = L["QM"], L["G"], L["YT"], L["bQM"], L["bG"], L["bYT"]
        sbank, obank, ptbuf, normalize, mem_attn, out_proj = (L["sbank"], L["obank"], L["ptbuf"], L["normalize"],
                                                              L["mem_attn"], L["out_proj"])
        KA = self.alloc(2 * S, BF16).rearrange("p (j t) -> p j t", j=2)
        KB = self.alloc(2 * S, BF16).rearrange("p (j t) -> p j t", j=2)
        VA = self.alloc(NT * 768, BF16).rearrange("p (t n) -> p t n", t=NT)
        bKA, bKB, bVA = Buf(), Buf(), Buf()
        self.dma(KA, self.KA_T.ap().rearrange("(j p) t -> p j t", p=128), writes=[bKA])
        for a in range(0, NT, 8):
            self.dma(VA[:, a:a + 8, :], self.VAB.ap().rearrange("(t p) n -> p t n", p=128)[:, a:a + 8, :], writes=[bVA])
        self.dma(KB, self.KB_T.ap().rearrange("(j p) t -> p j t", p=128), writes=[bKB])
        QA = self.alloc(4 * 512, BF16).rearrange("p (c t) -> p c t", c=4)
        QB = self.alloc(4 * 512, BF16).rearrange("p (c t) -> p c t", c=4)
        bQA, bQB = Buf(), Buf()
        PTF = [(self.alloc(384), Buf()) for _ in range(2)]
        EB = self.alloc(8 * 384, BF16).rearrange("p (h n) -> p h n", h=8)
        bEB = Buf()
        oh1 = self.alloc(640)
        erb = self.alloc(8)
        esink = self.alloc(8)
        boh, berb, bes = Buf(), Buf(), Buf()
        self.dma(oh1[0:32, :], self.c_oh1.ap(), writes=[boh])
        self.dma(erb[0:32, :], self.relb.ap(), writes=[berb])
        self.dma(esink, self.sink.ap().broadcast_to([128, 8]), writes=[bes])
        self.act(erb[0:32, :], erb[0:32, :], AF.Exp, reads=[berb], writes=[berb])
        self.act(esink, esink, AF.Exp, reads=[bes], writes=[bes])
        ps6 = self.PS[:, 0:3072]
        for g in range(3):
            for qq in range(128):
                u0 = (g - 1) * 128 - qq + 256
                idx = g * 128 + qq
                bk = idx // 64
                self.mm(ps6[:, idx * 8:(idx + 1) * 8], oh1[0:32, u0:u0 + 128], erb[0:32, 0:8], True, True,
                        reads=[boh, berb], writes=[self.bb[bk]])
        ps6v = ps6.rearrange("p (n h) -> p n h", h=8)
        for h in range(8):
            self.cp("dve" if h % 2 == 0 else "act", EB[:, h, :], ps6v[:, :, h], reads=self.bb[0:6], writes=[bEB])

        def normalize0(*a, **k):
            return normalize(*a, **k)

        def load_block(b, what):
            sl = slice(b * 512, (b + 1) * 512)
            if what == "QA":
                self.dma(QA, self.QA_T.ap().rearrange("(c p) t -> p c t", p=128)[:, :, sl], writes=[bQA])
            elif what == "QB":
                self.dma(QB, self.QB_T.ap().rearrange("(c p) t -> p c t", p=128)[:, :, sl], writes=[bQB])
            elif what == "QM":
                self.dma(QM, self.QM_T.ap().rearrange("(c p) t -> p c t", p=128)[:, :, sl], writes=[bQM])
            else:
                self.dma(G, self.G_T.ap().rearrange("(c p) t -> p c t", p=128)[:, :, sl], writes=[bG])

        for w in ("QA", "G", "QB", "QM"):
            load_block(0, w)
        LA = 2
        for b in range(NB):
            for c in range(4):
                j = c // 2
                obs = obank(2)
                units = [(kt, hh) for kt in range(NT) for hh in range(2)]
                sbs = {}

                def qk(u):
                    kt, hh = u
                    sb, bsb = sbank()
                    r0, r1 = hh * 64, hh * 64 + 64
                    self.mm(sb, KA[r0:r1, j, kt * 128:(kt + 1) * 128], QA[r0:r1, c, :], True, True,
                            reads=[bKA, bQA], writes=[bsb])
                    sbs[u] = (sb, bsb)

                for u in units[:LA]:
                    qk(u)
                for idx, u in enumerate(units):
                    if idx + LA < len(units):
                        qk(units[idx + LA])
                    kt, hh = u
                    sb, bsb = sbs.pop(u)
                    pt, bpt = ptbuf()
                    self.act(pt[:, 0:512], sb, AF.Exp, reads=[bsb], writes=[bpt], scale=0.125)
                    ob, bob = obs[hh]
                    v0 = j * 192 + hh * 64
                    self.mm(ob, VA[:, kt, v0:v0 + 128], pt[:, 0:512], kt == 0, kt == NT - 1, reads=[bVA, bpt], writes=[bob])
                for hh in range(2):
                    ob, bob = obs[hh]
                    normalize(ob, bob, (hh * 64, hh * 64 + 64), ((1 - hh) * 64, (1 - hh) * 64 + 64), c)
            if b + 1 < NB:
                load_block(b + 1, "QA")
            for h in range(8):
                c, hh, j = h // 2, h % 2, h // 4
                r0, r1 = hh * 64, hh * 64 + 64
                (ob, bob), = obank(1)
                for ql in range(4):
                    i = b * 4 + ql
                    gs = [g for g in range(3) if 0 <= i + g - 1 < NT]
                    c0, c1 = gs[0] * 128, (gs[-1] + 1) * 128
                    sb, bsb = sbank()
                    for g in gs:
                        kt = i + g - 1
                        self.mm(sb[:, g * 128:(g + 1) * 128], KB[r0:r1, j, kt * 128:(kt + 1) * 128],
                                QB[r0:r1, c, ql * 128:(ql + 1) * 128], True, True, reads=[bKB, bQB], writes=[bsb])
                    pf, bpf = PTF[(h * 4 + ql) % 2]
                    self.act(pf[:, c0:c1], sb[:, c0:c1], AF.Exp, reads=[bsb], writes=[bpf], scale=0.125)
                    pt, bpt = ptbuf()
                    self.tt("dve", pt[:, c0:c1], pf[:, c0:c1], EB[:, h, c0:c1], ALU.mult, reads=[bpf, bEB], writes=[bpt])
                    v0 = (2 + j) * 192 + hh * 64
                    for g in gs:
                        kt = i + g - 1
                        self.mm(ob[:, ql * 128:(ql + 1) * 128], VA[:, kt, v0:v0 + 128], pt[:, g * 128:(g + 1) * 128],
                                g == gs[0], g == gs[-1], reads=[bVA, bpt], writes=[bob])
                d0 = (1 - hh) * 64
                normalize(ob, bob, (r0, r1), (d0, d0 + 64), 4 + c, add_scalar=(esink[d0:d0 + 64, h:h + 1], bes))
            if b + 1 < NB:
                load_block(b + 1, "QB")
            mem_attn(b)
            if b + 1 < NB:
                load_block(b + 1, "QM")
            out_proj(b)
            if b + 1 < NB:
                load_block(b + 1, "G")

    def attn_layer1(self, L):
        P = self.P
        QM, G, YT, bQM, bG, bYT = L["QM"], L["G"], L["YT"], L["bQM"], L["bG"], L["bYT"]
        sbank, obank, ptbuf, normalize, mem_attn, out_proj = (L["sbank"], L["obank"], L["ptbuf"], L["normalize"],
                                                              L["mem_attn"], L["out_proj"])
        st = L["st"]
        E = self.alloc(16 * 9 * 128, BF16).rearrange("p (h t q) -> p h t q", h=16, t=9)
        bE = Buf()
        save = self.aoff
        erT = self.alloc(240)
        berT = Buf()
        self.dma(erT[0:31, :], self.rpbT.ap(), writes=[berT])
        self.act(erT[0:31, :], erT[0:31, :], AF.Exp, reads=[berT], writes=[berT])
        self.memset("pool", E.rearrange("p h t q -> p (h t q)"), 0.0, writes=[bE])
        ohp = [(self.alloc(1024), Buf()) for _ in range(2)]
        tiles = [(-3, False), (-2, False), (-2, True), (-1, False), (0, False), (1, False), (2, True), (2, False), (3, False)]
        ps4 = self.PS[:, 0:2048].rearrange("p (q n) -> p q n", n=256)
        cnt = 0
        for r in range(8):
            oh, boh = ohp[r % 2]
            self.dma(oh[0:31, :], self.c_ohc.ap()[:, r * 1024:(r + 1) * 1024], writes=[boh])
            for i in range(8):
                self.mm(self.PS[:, i * 256:i * 256 + 240], oh[0:31, i * 128:(i + 1) * 128], erT[0:31, 0:240], True, True,
                        reads=[boh, berT], writes=[self.bb[i // 2]])
            for ti, (c, masked) in enumerate(tiles):
                for kl in range(2):
                    for ql in range(2):
                        dkr = 2 * c + kl - ql
                        if abs(dkr) > 7:
                            continue
                        if masked and not (-4 <= dkr <= 3):
                            continue
                        dr = dkr + 7
                        src = ps4[kl * 64:(kl + 1) * 64, :, 0:240].rearrange("p q (h r) -> p h q r", r=15)[:, :, :, dr]
                        dst = E[kl * 64:(kl + 1) * 64, :, ti, ql * 64 + r * 8:ql * 64 + r * 8 + 8]
                        self.cp("dve" if cnt % 2 == 0 else "act", dst, src, reads=self.bb[0:4], writes=[bE])
                        cnt += 1
        self.P.barrier()
        self.aoff = save
        R = 12
        KR = self.alloc(8 * R * 128, BF16).rearrange("p (c t) -> p c t", c=8)
        VR = self.alloc(R * 1536, BF16).rearrange("p (s n) -> p s n", s=R)
        bKR = [Buf() for _ in range(R)]
        bVR = [Buf() for _ in range(R)]
        QC = self.alloc(8 * 512, BF16).rearrange("p (c t) -> p c t", c=8)
        bQC = Buf()
        PTF = [(self.alloc(640), Buf()) for _ in range(2)]
        kview = self.KC_T.ap().rearrange("(c p) t -> p c t", p=128)

        def load_tile(kt):
            s_ = kt % R
            self.dma(KR[:, :, s_ * 128:(s_ + 1) * 128], kview[:, :, kt * 128:(kt + 1) * 128], writes=[bKR[s_]])
            self.dma(VR[:, s_, :], self.VC.ap()[kt * 128:(kt + 1) * 128, :], writes=[bVR[s_]])

        def load_block(b, what):
            sl = slice(b * 512, (b + 1) * 512)
            if what == "QC":
                self.dma(QC, self.QC_T.ap().rearrange("(c p) t -> p c t", p=128)[:, :, sl], writes=[bQC])
            elif what == "QM":
                self.dma(QM, self.QM_T.ap().rearrange("(c p) t -> p c t", p=128)[:, :, sl], writes=[bQM])
            else:
                self.dma(G, self.G_T.ap().rearrange("(c p) t -> p c t", p=128)[:, :, sl], writes=[bG])

        def window(qt):
            if qt == 0:
                return [(0, 4), (1, 5), (2, 7), (3, 8)]
            if qt == 1:
                return [(0, 3), (1, 4), (2, 5), (3, 7)]
            if qt == 30:
                return [(28, 1), (29, 3), (30, 4), (31, 5)]
            if qt == 31:
                return [(28, 0), (29, 1), (30, 3), (31, 4)]
            return [(qt - 2 + i, 2 + i) for i in range(5)]

        for w in ("QC", "G", "QM"):
            load_block(0, w)
        for kt in range(0, 6):
            load_tile(kt)
        sp = 0
        for b in range(NB):
            if b + 1 < NB:
                for kt in range(4 * b + 6, min(4 * b + 10, NT)):
                    load_tile(kt)
            for h in range(16):
                c, hh = h // 2, h % 2
                r0, r1 = hh * 64, hh * 64 + 64
                (ob, bob), = obank(1)
                for ql in range(4):
                    qt = b * 4 + ql
                    win = window(qt)
                    n = len(win)
                    pair = sp % 2
                    sp += 1
                    sb2 = self.PS[:, pair * 1024:pair * 1024 + 640]
                    bsb2 = [self.bb[pair * 2], self.bb[pair * 2 + 1]]
                    for i, (kt, ti) in enumerate(win):
                        s_ = kt % R
                        self.mm(sb2[:, i * 128:(i + 1) * 128], KR[r0:r1, c, s_ * 128:(s_ + 1) * 128],
                                QC[r0:r1, c, ql * 128:(ql + 1) * 128], True, True,
                                reads=[bKR[s_], bQC], writes=[bsb2[i // 4]])
                    pf, bpf = PTF[sp % 2]
                    self.act(pf[:, 0:n * 128], sb2[:, 0:n * 128], AF.Exp, reads=bsb2, writes=[bpf], scale=0.125)
                    pt, bpt = ptbuf()
                    i0 = 0
                    while i0 < n:
                        i1 = i0
                        while i1 + 1 < n and win[i1 + 1][1] == win[i1][1] + 1:
                            i1 += 1
                        t0, t1 = win[i0][1], win[i1][1]
                        ev = E[:, h, t0:t1 + 1, :].rearrange("p t q -> p (t q)")
                        self.tt("dve", pt[:, i0 * 128:(i1 + 1) * 128], pf[:, i0 * 128:(i1 + 1) * 128], ev, ALU.mult,
                                reads=[bpf, bE], writes=[bpt])
                        i0 = i1 + 1
                    v0 = c * 192 + hh * 64
                    for i, (kt, ti) in enumerate(win):
                        s_ = kt % R
                        self.mm(ob[:, ql * 128:(ql + 1) * 128], VR[:, s_, v0:v0 + 128], pt[:, i * 128:(i + 1) * 128],
                                i == 0, i == n - 1, reads=[bVR[s_], bpt], writes=[bob])
                d0 = (1 - hh) * 64
                normalize(ob, bob, (r0, r1), (d0, d0 + 64), c)
            if b + 1 < NB:
                load_block(b + 1, "QC")
            mem_attn(b)
            if b + 1 < NB:
                load_block(b + 1, "QM")
            out_proj(b)
            if b + 1 < NB:
                load_block(b + 1, "G")


def prep_inputs(inputs):
    f = lambda a: np.ascontiguousarray(np.asarray(a, dtype=np.float32))
    x = f(inputs["x"])
    mem = f(inputs["mem"])
    ng = f(inputs["norm_gain"])
    mg = f(inputs["mem_norm_gain"])
    gains = np.concatenate([ng[0].reshape(8, 128).T, ng[1].reshape(8, 128).T, mg.reshape(8, 128).T], axis=1)
    qkg = np.stack([np.tile(f(inputs["q_norm_a"])[0], 2), np.tile(f(inputs["k_norm_a"])[0], 2)], axis=1)
    shared = {
        "w_in_even": f(inputs["w_in_even"])[0], "w_in_odd": f(inputs["w_in_odd"])[0],
        "w_out_even": f(inputs["w_out_even"])[0], "w_out_odd": f(inputs["w_out_odd"])[0],
        "w_mem_kv": f(inputs["w_mem_kv"]),
        "gains_pp": f(gains), "qk_gain": f(qkg),
        "final_gain": f(inputs["final_norm_gain"]).reshape(1, D),
        "sink_b": f(inputs["sink_b"]).reshape(1, 8),
        "rel_bias": f(inputs["rel_bias"]),
        "rpbT": f(np.transpose(f(inputs["rpb_c"])[0], (2, 0, 1)).reshape(31, 240)),
    }
    shared.update(host_consts())
    in_maps = []
    for b in range(8):
        m = dict(shared)
        m["x"] = x[b]
        m["mem"] = mem[b]
        in_maps.append(m)
    return in_maps


def kernel(**inputs):
    bld = Builder()
    nc = bld.build()
    in_maps = prep_inputs(inputs)
    res = run_bass_kernel_spmd(nc, in_maps, core_ids=list(range(8)))
    return np.stack([np.asarray(r["out"]) for r in res.results], axis=0).astype(np.float32)
```

```python
from contextlib import ExitStack
import math
import numpy as np
import concourse.bass as bass
import concourse.mybir as mybir
from concourse.bass_utils import run_bass_kernel_spmd

F32 = mybir.dt.float32
BF16 = mybir.dt.bfloat16
ALU = mybir.AluOpType
AF = mybir.ActivationFunctionType
AX = mybir.AxisListType

ENGS = ("pe", "act", "dve", "pool", "sp")
DMA_SLOTS = 12

S = 4096
D = 1024
NT = 32
NB = 8
EPS = 1e-6


class Buf:
    __slots__ = ("w", "r", "name")

    def __init__(self, name=""):
        self.w = None
        self.r = []
        self.name = name


class Op:
    __slots__ = ("eng", "fn", "deps", "needs_inc", "inc_val", "dma", "slot", "slot_val", "id")

    def __init__(self, eng, fn, dma):
        self.eng = eng
        self.fn = fn
        self.dma = dma
        self.deps = set()
        self.needs_inc = False
        self.inc_val = None
        self.slot = None
        self.slot_val = None
        self.id = None


class Prog:
    def __init__(self, nc):
        self.nc = nc
        self.ops = []
        self.dma_count = {e: 0 for e in ENGS}
        self.dma_hist = {e: [] for e in ENGS}
        self.last_op = {e: None for e in ENGS}

    def add(self, eng, fn, reads=(), writes=(), dma=False):
        op = Op(eng, fn, dma)
        op.id = len(self.ops)
        deps = set()
        for b in reads:
            if b.w is not None:
                deps.add(b.w)
        for b in writes:
            if b.w is not None:
                deps.add(b.w)
            for r in b.r:
                deps.add(r)
        for b in reads:
            b.r.append(op)
        for b in writes:
            b.w = op
            b.r = []
        for d in deps:
            if d is op:
                continue
            if (not d.dma) and (not dma) and d.eng == "pe" and eng == "pe":
                continue
            op.deps.add(d)
            if not d.dma:
                d.needs_inc = True
        if dma:
            n = self.dma_count[eng]
            op.slot = n % DMA_SLOTS
            op.slot_val = 16 * (n // DMA_SLOTS + 1)
            hist = self.dma_hist[eng]
            if n >= DMA_SLOTS:
                op.deps.add(hist[n - DMA_SLOTS])
            hist.append(op)
            self.dma_count[eng] = n + 1
        else:
            self.last_op[eng] = op
        self.ops.append(op)
        return op

    def barrier(self):
        lasts = [self.last_op[e] for e in ENGS if self.last_op[e] is not None]
        dmas = []
        for e in ENGS:
            dmas += self.dma_hist[e][-DMA_SLOTS:]
        for e in ENGS:
            op = Op(e, None, False)
            op.id = len(self.ops)
            for d in lasts:
                if d.eng != e:
                    op.deps.add(d)
                    d.needs_inc = True
            for d in dmas:
                op.deps.add(d)
            self.ops.append(op)

    def emit(self):
        nc = self.nc
        with ExitStack() as st:
            sem = {e: st.enter_context(nc.semaphore("c_" + e)) for e in ENGS}
            dsem = {}
            for e in ENGS:
                if self.dma_count[e] > 0:
                    dsem[e] = [st.enter_context(nc.semaphore("d_%s_%d" % (e, i))) for i in range(DMA_SLOTS)]
            cnt = {e: 0 for e in ENGS}
            for op in self.ops:
                if op.dma or op.fn is None:
                    continue
                if op.needs_inc:
                    cnt[op.eng] += 1
                    op.inc_val = cnt[op.eng]
            per_eng = {e: [] for e in ENGS}
            seen = {e: {} for e in ENGS}
            for op in self.ops:
                waits = {}
                for d in op.deps:
                    if d.dma:
                        key = ("d", d.eng, d.slot)
                        val = d.slot_val
                    else:
                        key = ("c", d.eng)
                        val = d.inc_val
                    if waits.get(key, 0) < val:
                        waits[key] = val
                wl = []
                s = seen[op.eng]
                for key, val in waits.items():
                    if s.get(key, 0) >= val:
                        continue
                    s[key] = val
                    wl.append((key, val))
                per_eng[op.eng].append((op, wl))
            block = st.enter_context(nc.Block())

            def run(engname, e):
                for op, wl in per_eng[engname]:
                    for key, val in wl:
                        if key[0] == "d":
                            e.wait_ge(dsem[key[1]][key[2]], val)
                        else:
                            e.wait_ge(sem[key[1]], val)
                    if op.fn is None:
                        continue
                    ins = op.fn(e)
                    if op.dma:
                        ins.then_inc(dsem[engname][op.slot], 16)
                    elif op.needs_inc:
                        ins.then_inc(sem[engname], 1)

            @block.tensor
            def _(e):
                run("pe", e)

            @block.scalar
            def _(e):
                run("act", e)

            @block.vector
            def _(e):
                run("dve", e)

            @block.gpsimd
            def _(e):
                run("pool", e)

            @block.sync
            def _(e):
                run("sp", e)


def _t5_bucket(rel):
    nb = 16
    max_exact = 8
    ret = np.where(rel > 0, nb, 0)
    n = np.abs(rel)
    nf = np.maximum(n, 1).astype(np.float32)
    large = max_exact + (np.log(nf / np.float32(max_exact)) / np.float32(math.log(128 / max_exact))
                         * np.float32(nb - max_exact)).astype(np.int32)
    large = np.minimum(large, nb - 1)
    return ret + np.where(n < max_exact, n, large)


def host_consts():
    c = {}
    c["ident"] = np.eye(128, dtype=np.float32)
    R = np.zeros((128, 128), np.float32)
    for p in range(128):
        sub = p % 32
        if sub < 16:
            R[p, p + 16] = -1.0
        else:
            R[p, p - 16] = 1.0
    c["rotT"] = np.ascontiguousarray(R.T)
    bd = np.zeros((128, 128), np.float32)
    bd[0:64, 0:64] = 1.0
    bd[64:128, 64:128] = 1.0
    c["bd"] = bd
    t = np.arange(S)
    row = (t // 64).astype(np.float32)
    col = (t % 64).astype(np.float32)
    freqs = np.power(np.float32(10000.0), -np.arange(16, dtype=np.float32) / np.float32(16)).astype(np.float32)
    cosT = np.zeros((128, S), np.float32)
    sinT = np.zeros((128, S), np.float32)
    for p in range(128):
        dh = p % 64
        pos = row if dh < 32 else col
        ang = (pos * freqs[dh % 16]).astype(np.float32)
        cosT[p] = np.cos(ang)
        sinT[p] = np.sin(ang)
    c["cosT"] = cosT
    c["sinT"] = sinT
    rel = np.arange(-256, 384)
    bk = _t5_bucket(rel)
    oh1 = np.zeros((32, 640), np.float32)
    for u, r in enumerate(rel):
        if abs(r) <= 128:
            oh1[bk[u], u] = 1.0
    c["oh1"] = oh1
    ohc = np.zeros((31, 64, 128), np.float32)
    for qc in range(64):
        cs = min(max(qc - 8, 0), 48)
        for kc in range(cs, cs + 16):
            dc = kc - qc + 15
            ohc[dc, qc, kc] = 1.0
            ohc[dc, qc, 64 + kc] = 1.0
    c["ohc"] = ohc.reshape(31, 64 * 128)
    return c


class Builder:
    def __init__(self, debug=None, stop_after=None):
        self.debug = debug or ()
        self.stop_after = stop_after
        self.nc = bass.Bass("TRN2", target_bir_lowering=False)
        self.P = Prog(self.nc)
        self.dram = {}

    def din(self, name, shape, dt=F32):
        t = self.nc.dram_tensor(name, list(shape), dt, kind="ExternalInput")
        self.dram[name] = t
        return t

    def dscratch(self, name, shape, dt):
        kind = "ExternalOutput" if name in self.debug else "Internal"
        t = self.nc.dram_tensor(name, list(shape), dt, kind=kind)
        self.dram[name] = t
        return t

    def reset_arena(self):
        self.aoff = 0

    def alloc(self, ncols, dt=F32):
        nbytes = ncols * (4 if dt == F32 else 2)
        n32 = (nbytes + 3) // 4
        n32 = (n32 + 1) // 2 * 2
        a = self.ARENA[:, self.aoff:self.aoff + n32]
        self.aoff += n32
        assert self.aoff <= self.ARENA_N, "arena overflow %d > %d" % (self.aoff, self.ARENA_N)
        if dt == F32:
            return a[:, 0:ncols]
        return a.bitcast(BF16)[:, 0:ncols]

    def dma(self, out, in_, reads=(), writes=(), q="sp", **kw):
        return self.P.add(q, lambda e: e.dma_start(out=out, in_=in_, **kw), reads, writes, dma=True)

    def mm(self, out, lhsT, rhs, start, stop, reads=(), writes=()):
        return self.P.add("pe", lambda e: e.matmul(out, lhsT=lhsT, rhs=rhs, start=start, stop=stop), reads, writes)

    def tr(self, out, in_, ident, reads=(), writes=()):
        return self.P.add("pe", lambda e: e.transpose(out, in_, ident), reads, writes)

    def act(self, out, in_, func, reads=(), writes=(), **kw):
        return self.P.add("act", lambda e: e.activation(out=out, in_=in_, func=func, **kw), reads, writes)

    def tt(self, eng, out, in0, in1, op, reads=(), writes=()):
        return self.P.add(eng, lambda e: e.tensor_tensor(out=out, in0=in0, in1=in1, op=op), reads, writes)

    def ts(self, eng, out, in0, s1, op0, reads=(), writes=(), s2=None, op1=None):
        if op1 is None:
            return self.P.add(eng, lambda e: e.tensor_scalar(out=out, in0=in0, scalar1=s1, scalar2=None, op0=op0), reads, writes)
        return self.P.add(eng, lambda e: e.tensor_scalar(out=out, in0=in0, scalar1=s1, scalar2=s2, op0=op0, op1=op1), reads, writes)

    def cp(self, eng, out, in_, reads=(), writes=()):
        if eng == "act":
            return self.P.add("act", lambda e: e.copy(out=out, in_=in_), reads, writes)
        return self.P.add(eng, lambda e: e.tensor_copy(out=out, in_=in_), reads, writes)

    def memset(self, eng, ap, val, writes=()):
        return self.P.add(eng, lambda e: e.memset(ap, val), (), writes)

    def recip(self, out, in_, reads=(), writes=()):
        return self.P.add("dve", lambda e: e.reciprocal(out=out, in_=in_), reads, writes)

    def build(self):
        nc = self.nc
        self.x = self.din("x", [S, D])
        self.mem = self.din("mem", [256, D])
        self.w_in = [self.din("w_in_even", [D, 3584]), self.din("w_in_odd", [D, 5120])]
        self.w_out = [self.din("w_out_even", [1536, D]), self.din("w_out_odd", [1536, D])]
        self.w_mem = self.din("w_mem_kv", [2, D, 1024])
        self.gains = self.din("gains_pp", [128, 24])
        self.qkg = self.din("qk_gain", [128, 2])
        self.fgain = self.din("final_gain", [1, D])
        self.sink = self.din("sink_b", [1, 8])
        self.relb = self.din("rel_bias", [32, 8])
        self.rpbT = self.din("rpbT", [31, 240])
        self.c_ident = self.din("ident", [128, 128])
        self.c_rotT = self.din("rotT", [128, 128])
        self.c_bd = self.din("bd", [128, 128])
        self.c_cos = self.din("cosT", [128, S])
        self.c_sin = self.din("sinT", [128, S])
        self.c_oh1 = self.din("oh1", [32, 640])
        self.c_ohc = self.din("ohc", [31, 64 * 128])
        self.out = nc.dram_tensor("out", [S, D], F32, kind="ExternalOutput")
        self.QA_T = self.dscratch("QA_T", [512, S], BF16)
        self.KA_T = self.dscratch("KA_T", [256, S], BF16)
        self.QB_T = self.dscratch("QB_T", [512, S], BF16)
        self.KB_T = self.dscratch("KB_T", [256, S], BF16)
        self.QM_T = self.dscratch("QM_T", [512, S], BF16)
        self.G_T = self.dscratch("G_T", [1536, S], BF16)
        self.VAB = self.dscratch("VAB", [S, 768], BF16)
        self.X1 = self.dscratch("X1", [S, D], F32)
        self.QC_T = self.dscratch("QC_T", [1024, S], BF16)
        self.KC_T = self.dscratch("KC_T", [1024, S], BF16)
        self.VC = self.dscratch("VC", [S, 1536], BF16)
        self.MK_T = self.dscratch("MK_T", [2, 512, 256], BF16)
        self.MV = self.dscratch("MV", [2, 256, 512], BF16)

        with ExitStack() as st:
            self.ARENA_N = 52000
            self.ARENA = st.enter_context(nc.sbuf_tensor("arena", [128, self.ARENA_N], F32))
            self.PS = st.enter_context(nc.psum_tensor("ps", [128, 4096], F32))
            self.bank = [self.PS[:, i * 512:(i + 1) * 512] for i in range(8)]
            self.bb = [Buf("bank%d" % i) for i in range(8)]
            self.phase_mem()
            self.P.barrier()
            for layer in range(2):
                self.phase_proj(layer)
                self.P.barrier()
                if self.stop_after == ("proj", layer):
                    break
                self.phase_attn(layer)
                self.P.barrier()
                if self.stop_after == ("attn", layer):
                    break
            self.P.emit()
        return nc

    def load_consts(self):
        c = {}
        c["ident"] = self.alloc(128)
        c["b_ident"] = Buf()
        self.dma(c["ident"], self.c_ident.ap(), writes=[c["b_ident"]])
        c["gains"] = self.alloc(24)
        c["b_gains"] = Buf()
        self.dma(c["gains"], self.gains.ap(), writes=[c["b_gains"]])
        return c

    def norm_transpose(self, c, xt, bx, ht3, bht, j, gcol, tp_banks, btp, scr):
        junk, bjunk, stat, bstat = scr
        self.act(junk, xt, AF.Square, reads=[bx], writes=[bjunk, bstat], accum_out=stat[:, 0:1])
        self.ts("dve", stat[:, 1:2], stat[:, 0:1], 1.0 / D, ALU.mult, reads=[bstat], writes=[bstat], s2=EPS, op1=ALU.add)
        self.act(stat[:, 2:3], stat[:, 1:2], AF.Sqrt, reads=[bstat], writes=[bstat])
        self.recip(stat[:, 3:4], stat[:, 2:3], reads=[bstat], writes=[bstat])
        self.ts("dve", xt, xt, stat[:, 3:4], ALU.mult, reads=[bx, bstat], writes=[bx])
        tp = tp_banks
        for kc in range(8):
            self.tr(tp[:, kc * 128:(kc + 1) * 128], xt[:, kc * 128:(kc + 1) * 128], c["ident"],
                    reads=[bx, c["b_ident"]], writes=list(btp))
        g = c["gains"][:, gcol:gcol + 8]
        gb = g.unsqueeze(2).broadcast_to([128, 8, 128])
        tp3 = tp.rearrange("p (a b) -> p a b", a=8)
        self.tt("dve", ht3[:, :, j * 128:(j + 1) * 128], tp3, gb, ALU.mult,
                reads=list(btp) + [c["b_gains"]], writes=[bht])

    def phase_mem(self):
        self.reset_arena()
        c = self.load_consts()
        W = self.alloc(2 * 8 * 1024, BF16).rearrange("p (l k n) -> p l k n", l=2, k=8)
        bW = Buf()
        for l in range(2):
            self.dma(W[:, l], self.w_mem.ap()[l].rearrange("(k p) n -> p k n", p=128), writes=[bW], q="pool")
        ht3 = self.alloc(8 * 256, BF16).rearrange("p (k t) -> p k t", k=8)
        bht = Buf()
        junk = self.alloc(1024, BF16)
        scr = (junk, Buf(), self.alloc(4), Buf())
        tpb = self.PS[:, 0:1024]
        btp = [self.bb[0], self.bb[1]]
        for j in range(2):
            xt = self.alloc(1024)
            bx = Buf()
            self.dma(xt, self.mem.ap()[j * 128:(j + 1) * 128, :], writes=[bx])
            self.norm_transpose(c, xt, bx, ht3, bht, j, 16, tpb, btp, scr)
        stg = self.alloc(4 * 256, BF16).rearrange("p (c t) -> p c t", c=4)
        stv = self.alloc(2 * 512, BF16).rearrange("p (c t) -> p c t", c=2)
        bst, bsv = Buf(), Buf()
        for l in range(2):
            for hm in range(4):
                pb = self.bank[2 + hm % 2]
                bpb = self.bb[2 + hm % 2]
                for kc in range(8):
                    self.mm(pb[:, 0:256], W[:, l, kc, hm * 128:(hm + 1) * 128], ht3[:, kc, :], kc == 0, kc == 7,
                            reads=[bW, bht], writes=[bpb])
                self.cp("dve", stg[:, hm, :], pb[:, 0:256], reads=[bpb], writes=[bst])
            self.dma(self.MK_T.ap()[l].rearrange("(c p) t -> p c t", p=128), stg, reads=[bst])
            for mt in range(2):
                pb = self.bank[4 + mt]
                bpb = self.bb[4 + mt]
                for kc in range(8):
                    self.mm(pb, ht3[:, kc, mt * 128:(mt + 1) * 128], W[:, l, kc, 512:1024], kc == 0, kc == 7,
                            reads=[bW, bht], writes=[bpb])
                self.cp("act", stv[:, mt, :], pb, reads=[bpb], writes=[bsv])
            self.dma(self.MV.ap()[l].rearrange("(c p) n -> p c n", p=128), stv, reads=[bsv])

    def phase_proj(self, layer):
        self.reset_arena()
        c = self.load_consts()
        if layer == 0:
            wl = [(0, 512, 0)]
            wl += [(512, 64, 512), (512, 64, 576), (576, 64, 640), (576, 64, 704)]
            wl += [(768, 512, 768)]
            wl += [(1280, 64, 1280), (1280, 64, 1344), (1344, 64, 1408), (1344, 64, 1472)]
            wl += [(1536, 512, 1536), (2048, 1536, 2048), (640, 128, 3584), (1408, 128, 3712)]
            NC = 3840
            groups = [("qa", 0, 4, self.QA_T, "rope_q"), ("ka", 512, 2, self.KA_T, "rope_k"),
                      ("qb", 768, 4, self.QB_T, "copy"), ("kb", 1280, 2, self.KB_T, "copy"),
                      ("qm", 1536, 4, self.QM_T, "copy"), ("gate", 2048, 12, self.G_T, "silu")]
            vcol, vn = 3584, 256
        else:
            wl = [(0, 2048, 0), (3072, 2048, 2048), (2048, 1024, 4096)]
            NC = 5120
            groups = [("qc", 0, 8, self.QC_T, "copy"), ("kc", 1024, 8, self.KC_T, "copy"),
                      ("qm", 2048, 4, self.QM_T, "copy"), ("gate", 2560, 12, self.G_T, "silu")]
            vcol, vn = 4096, 1024
        W = self.alloc(8 * NC, BF16).rearrange("p (k n) -> p k n", k=8)
        bW = Buf()
        wsrc = self.w_in[layer].ap().rearrange("(k p) n -> p k n", p=128)
        for (s0, n, d0) in wl:
            for a in range(0, n, 1024):
                m = min(1024, n - a)
                self.dma(W[:, :, d0 + a:d0 + a + m], wsrc[:, :, s0 + a:s0 + a + m], writes=[bW], q="pool")
        if layer == 0:
            rotT = self.alloc(128)
            bd = self.alloc(128)
            qkg = self.alloc(2)
            bcst = Buf()
            self.dma(rotT, self.c_rotT.ap(), writes=[bcst])
            self.dma(bd, self.c_bd.ap(), writes=[bcst])
            self.dma(qkg, self.qkg.ap(), writes=[bcst])
            cs = [(self.alloc(512), self.alloc(512), Buf()) for _ in range(2)]
            tq = self.alloc(512)
            tsq = self.alloc(512)
            t1 = self.alloc(512)
            trs = self.alloc(512)
            btq, btsq, bt1, btrs = Buf(), Buf(), Buf(), Buf()
        XT = [self.alloc(1024) for _ in range(4)]
        bXT = [Buf() for _ in range(4)]
        HT = [self.alloc(8 * 512, BF16).rearrange("p (k t) -> p k t", k=8) for _ in range(2)]
        bHT = [Buf() for _ in range(2)]
        junk = self.alloc(1024, BF16)
        scr = (junk, Buf(), self.alloc(4), Buf())
        tpb = self.PS[:, 0:1024]
        btp = [self.bb[0], self.bb[1]]
        stg = {}
        for (name, col, nch, dst, kind) in groups:
            stg[name] = [(self.alloc(nch * 512, BF16).rearrange("p (c t) -> p c t", c=nch), Buf()) for _ in range(2)]
        if layer == 0:
            VST = [(self.alloc(768, BF16), Buf()) for _ in range(2)]
        else:
            VST = [(self.alloc(1536, BF16), Buf()) for _ in range(2)]
        for (v, bv) in VST:
            self.memset("pool", v, 1.0, writes=[bv])
        pbanks = [self.bank[2], self.bank[3], self.bank[4]]
        bpb = [self.bb[2], self.bb[3], self.bb[4]]
        aux = [self.bank[5], self.bank[6]]
        baux = [self.bb[5], self.bb[6]]
        vbank = self.bank[7]
        bvb = self.bb[7]
        xsrc = self.x if layer == 0 else self.X1
        gcol = 0 if layer == 0 else 8

        def load_x(b):
            for j in range(4):
                t = b * 4 + j
                self.dma(XT[j], xsrc.ap()[t * 128:(t + 1) * 128, :], writes=[bXT[j]])

        def norm_tr(b):
            for j in range(4):
                self.norm_transpose(c, XT[j], bXT[j], HT[b % 2], bHT[b % 2], j, gcol, tpb, btp, scr)

        state = {"pi": 0, "vi": 0}

        def do_chunk(b, name, col, ci, kind, sbuf, bs):
            i = state["pi"] % 3
            state["pi"] += 1
            pb, bp = pbanks[i], bpb[i]
            ht, bh = HT[b % 2], bHT[b % 2]
            for kc in range(8):
                self.mm(pb, W[:, kc, col + ci * 128:col + (ci + 1) * 128], ht[:, kc, :], kc == 0, kc == 7,
                        reads=[bW, bh], writes=[bp])
            dst = sbuf[:, ci, :]
            if kind == "copy":
                self.cp("act" if (state["pi"] % 2 == 0) else "dve", dst, pb, reads=[bp], writes=[bs])
            elif kind == "silu":
                self.act(dst, pb, AF.Silu, reads=[bp], writes=[bs])
            else:
                gi = 0 if kind == "rope_q" else 1
                cosb, sinb, bcs = cs[b % 2]
                self.act(tq, pb, AF.Copy, reads=[bp, bcst], writes=[btq], scale=qkg[:, gi:gi + 1])
                self.act(tsq, pb, AF.Square, reads=[bp], writes=[btsq])
                self.mm(aux[0], bd, tsq, True, True, reads=[bcst, btsq], writes=[baux[0]])
                self.mm(aux[1], rotT, tq, True, True, reads=[bcst, btq], writes=[baux[1]])
                self.ts("dve", trs, aux[0], 1.0 / 64, ALU.mult, reads=[baux[0]], writes=[btrs], s2=EPS, op1=ALU.add)
                self.act(trs, trs, AF.Sqrt, reads=[btrs], writes=[btrs])
                self.recip(trs, trs, reads=[btrs], writes=[btrs])
                self.tt("pool", t1, tq, cosb, ALU.mult, reads=[btq, bcs], writes=[bt1])
                self.tt("dve", tsq, aux[1], sinb, ALU.mult, reads=[baux[1], bcs, btsq], writes=[btsq])
                self.tt("pool", t1, t1, tsq, ALU.add, reads=[bt1, btsq], writes=[bt1])
                self.tt("dve", dst, t1, trs, ALU.mult, reads=[bt1, btrs], writes=[bs])

        def do_v(b):
            ht, bh = HT[b % 2], bHT[b % 2]
            for j in range(4):
                t = b * 4 + j
                v, bv = VST[state["vi"] % 2]
                state["vi"] += 1
                if layer == 0:
                    for kc in range(8):
                        self.mm(vbank[:, 0:256], ht[:, kc, j * 128:(j + 1) * 128], W[:, kc, vcol:vcol + 256],
                                kc == 0, kc == 7, reads=[bW, bh], writes=[bvb])
                    v3 = v.rearrange("p (g s) -> p g s", g=4)
                    src = vbank[:, 0:256].rearrange("p (g d) -> p g d", g=4)
                    self.cp("dve", v3[:, :, 0:64], src, reads=[bvb], writes=[bv])
                    self.cp("act", v3[:, :, 128:192], src, reads=[bvb], writes=[bv])
                    self.dma(self.VAB.ap()[t * 128:(t + 1) * 128, :], v, reads=[bv])
                else:
                    v3 = v.rearrange("p (g s) -> p g s", g=8)
                    for half in range(2):
                        for kc in range(8):
                            self.mm(vbank, ht[:, kc, j * 128:(j + 1) * 128],
                                    W[:, kc, vcol + half * 512:vcol + (half + 1) * 512],
                                    kc == 0, kc == 7, reads=[bW, bh], writes=[bvb])
                        src = vbank.rearrange("p (g e d) -> p g e d", g=4, e=2)
                        self.cp("dve", v3[:, half * 4:(half + 1) * 4, 0:64], src[:, :, 0, :], reads=[bvb], writes=[bv])
                        self.cp("act", v3[:, half * 4:(half + 1) * 4, 128:192], src[:, :, 1, :], reads=[bvb], writes=[bv])
                    self.dma(self.VC.ap()[t * 128:(t + 1) * 128, :], v, reads=[bv])

        chunks = []
        for (name, col, nch, dst, kind) in groups:
            for ci in range(nch):
                chunks.append((name, col, ci, nch, dst, kind))
        load_x(0)
        norm_tr(0)
        for b in range(NB):
            if layer == 0:
                cosb, sinb, bcs = cs[b % 2]
                self.dma(cosb, self.c_cos.ap()[:, b * 512:(b + 1) * 512], writes=[bcs])
                self.dma(sinb, self.c_sin.ap()[:, b * 512:(b + 1) * 512], writes=[bcs])
            if b + 1 < NB:
                load_x(b + 1)
            half_n = len(chunks) // 2
            for idx, (name, col, ci, nch, dst, kind) in enumerate(chunks):
                if idx == half_n and b + 1 < NB:
                    norm_tr(b + 1)
                sbuf, bs = stg[name][b % 2]
                do_chunk(b, name, col, ci, kind, sbuf, bs)
                if ci == nch - 1:
                    self.dma(dst.ap().rearrange("(c p) t -> p c t", p=128)[:, :, b * 512:(b + 1) * 512], sbuf, reads=[bs])
            do_v(b)

    def phase_attn(self, layer):
        self.reset_arena()
        P = self.P
        WOUT = self.alloc(12 * 1024, BF16).rearrange("p (c n) -> p c n", c=12)
        bWOUT = Buf()
        self.dma(WOUT, self.w_out[layer].ap().rearrange("(c p) n -> p c n", p=128), writes=[bWOUT], q="pool")
        ones_bf = self.alloc(128, BF16)
        bones = Buf()
        self.memset("pool", ones_bf, 1.0, writes=[bones])
        MKT = self.alloc(4 * 256, BF16).rearrange("p (c t) -> p c t", c=4)
        MVs = self.alloc(2 * 512, BF16).rearrange("p (c n) -> p c n", c=2)
        bMK, bMV = Buf(), Buf()
        self.dma(MKT, self.MK_T.ap()[layer].rearrange("(c p) t -> p c t", p=128), writes=[bMK])
        self.dma(MVs, self.MV.ap()[layer].rearrange("(c p) n -> p c n", p=128), writes=[bMV])
        QM = self.alloc(4 * 512, BF16).rearrange("p (c t) -> p c t", c=4)
        G = self.alloc(12 * 512, BF16).rearrange("p (c t) -> p c t", c=12)
        YT = self.alloc(12 * 512, BF16).rearrange("p (c t) -> p c t", c=12)
        bQM, bG = Buf(), Buf()
        bYT = [Buf() for _ in range(12)]
        nxb = 2 if layer == 0 else 1
        XR = [(self.alloc(1024), Buf()) for _ in range(nxb)]
        OUTT = [(self.alloc(1024), Buf()) for _ in range(nxb)]
        PT = [(self.alloc(640, BF16), Buf()) for _ in range(6)]
        RD = [(self.alloc(512), Buf()) for _ in range(2)]
        TN = [(self.alloc(512), Buf()) for _ in range(2)]
        st = {"s": 0, "pt": 0, "o": 0, "rd": 0, "x": 0}

        SB = [0, 1, 2, 3] if layer == 0 else [5]
        OB = [4, 5, 6, 7] if layer == 0 else [6, 7]

        def sbank():
            i = SB[st["s"] % len(SB)]
            st["s"] += 1
            return self.bank[i], self.bb[i]

        def obank(n=1):
            if n == 2 and st["o"] % 2 == 1:
                st["o"] += 1
            r = []
            for _ in range(n):
                i = OB[st["o"] % len(OB)]
                st["o"] += 1
                r.append((self.bank[i], self.bb[i]))
            return r

        def ptbuf():
            i = st["pt"] % 6
            st["pt"] += 1
            return PT[i]

        def normalize(ob, bob, o_rows, d_rows, ci, add_scalar=None):
            i = st["rd"] % 2
            st["rd"] += 1
            rd, brd = RD[i]
            tn, btn = TN[i]
            o0, o1 = o_rows
            d0, d1 = d_rows
            if add_scalar is not None:
                self.ts("dve", rd[d0:d1, :], ob[d0:d1, :], add_scalar[0], ALU.add, reads=[bob, add_scalar[1]], writes=[brd])
                self.recip(rd[d0:d1, :], rd[d0:d1, :], reads=[brd], writes=[brd])
            elif layer == 1:
                self.act(rd[d0:d1, :], ob[d0:d1, :], AF.Ln, reads=[bob], writes=[brd])
                self.act(rd[d0:d1, :], rd[d0:d1, :], AF.Exp, reads=[brd], writes=[brd], scale=-1.0)
            else:
                self.recip(rd[d0:d1, :], ob[d0:d1, :], reads=[bob], writes=[brd])
            self.tt("dve", tn[o0:o1, :], ob[o0:o1, :], rd[d0:d1, :], ALU.mult, reads=[bob, brd], writes=[btn])
            self.tt("pool", YT[o0:o1, ci, :], tn[o0:o1, :], G[o0:o1, ci, :], ALU.mult, reads=[btn, bG], writes=[bYT[ci]])

        def mem_attn(b):
            sc = 128.0 ** -0.5
            for hm in range(4):
                (ob, bob), (db, bdb) = obank(2)
                pts = []
                for mt in range(2):
                    sb, bsb = sbank()
                    self.mm(sb, MKT[:, hm, mt * 128:(mt + 1) * 128], QM[:, hm, :], True, True, reads=[bMK, bQM], writes=[bsb])
                    pt, bpt = ptbuf()
                    self.act(pt[:, 0:512], sb, AF.Exp, reads=[bsb], writes=[bpt], scale=sc)
                    pts.append((pt, bpt))
                for mt in range(2):
                    pt, bpt = pts[mt]
                    self.mm(ob, MVs[:, mt, hm * 128:(hm + 1) * 128], pt[:, 0:512], mt == 0, mt == 1, reads=[bMV, bpt], writes=[bob])
                for mt in range(2):
                    pt, bpt = pts[mt]
                    self.mm(db, ones_bf, pt[:, 0:512], mt == 0, mt == 1, reads=[bones, bpt], writes=[bdb])
                i = st["rd"] % 2
                st["rd"] += 1
                rd, brd = RD[i]
                tn, btn = TN[i]
                if layer == 1:
                    self.act(rd, db, AF.Ln, reads=[bdb], writes=[brd])
                    self.act(rd, rd, AF.Exp, reads=[brd], writes=[brd], scale=-1.0)
                else:
                    self.recip(rd, db, reads=[bdb], writes=[brd])
                self.tt("dve", tn, ob, rd, ALU.mult, reads=[bob, brd], writes=[btn])
                self.tt("pool", YT[:, 8 + hm, :], tn, G[:, 8 + hm, :], ALU.mult, reads=[btn, bG], writes=[bYT[8 + hm]])

        def out_proj(b):
            for jt in range(4):
                t = b * 4 + jt
                xr, bxr = XR[st["x"] % nxb]
                ot, bot = OUTT[st["x"] % nxb]
                st["x"] += 1
                src = self.x if layer == 0 else self.X1
                self.dma(xr, src.ap()[t * 128:(t + 1) * 128, :], writes=[bxr])
                obs = obank(2)
                for nh in range(2):
                    ob, bob = obs[nh]
                    for cc in range(12):
                        self.mm(ob, YT[:, cc, jt * 128:(jt + 1) * 128], WOUT[:, cc, nh * 512:(nh + 1) * 512],
                                cc == 0, cc == 11, reads=[bYT[cc], bWOUT], writes=[bob])
                    self.tt("dve", ot[:, nh * 512:(nh + 1) * 512], ob, xr[:, nh * 512:(nh + 1) * 512], ALU.add,
                            reads=[bob, bxr], writes=[bot])
                if layer == 0:
                    self.dma(self.X1.ap()[t * 128:(t + 1) * 128, :], ot, reads=[bot])
                else:
                    self.act(fjunk, ot, AF.Square, reads=[bot], writes=[bfj, bfs], accum_out=fstat[:, 0:1])
                    self.ts("dve", fstat[:, 1:2], fstat[:, 0:1], 1.0 / D, ALU.mult, reads=[bfs], writes=[bfs], s2=EPS, op1=ALU.add)
                    self.act(fstat[:, 2:3], fstat[:, 1:2], AF.Sqrt, reads=[bfs], writes=[bfs])
                    self.recip(fstat[:, 3:4], fstat[:, 2:3], reads=[bfs], writes=[bfs])
                    P.add("dve", lambda e, ot=ot: e.scalar_tensor_tensor(out=ot, in0=ot, scalar=fstat[:, 3:4], in1=FG,
                                                                       op0=ALU.mult, op1=ALU.mult),
                          reads=[bot, bfs, bFG], writes=[bot])
                    self.dma(self.out.ap()[t * 128:(t + 1) * 128, :], ot, reads=[bot])

        if layer == 0:
            self.attn_layer0(locals())
        else:
            FG = self.alloc(1024)
            bFG = Buf()
            self.dma(FG, self.fgain.ap().broadcast_to([128, D]), writes=[bFG])
            fjunk = self.alloc(1024, BF16)
            fstat = self.alloc(4)
            bfj, bfs = Buf(), Buf()
            self.attn_layer1(locals())

    def attn_layer0(self, L):
        P = self.P
        QM, G, YT, bQM, bG, bYT = L["QM"], L["G"], L["YT"], L["bQM"], L["bG"], L["bYT"]
        sbank, obank, ptbuf, normalize, mem_attn, out_proj = (L["sbank"], L["obank"], L["ptbuf"], L["normalize"],
                                                              L["mem_attn"], L["out_proj"])
        KA = self.alloc(2 * S, BF16).rearrange("p (j t) -> p j t", j=2)
        KB = self.alloc(2 * S, BF16).rearrange("p (j t) -> p j t", j=2)
        VA = self.alloc(NT * 768, BF16).rearrange("p (t n) -> p t n", t=NT)
        bKA, bKB, bVA = Buf(), Buf(), Buf()
        self.dma(KA, self.KA_T.ap().rearrange("(j p) t -> p j t", p=128), writes=[bKA])
        for a in range(0, NT, 8):
            self.dma(VA[:, a:a + 8, :], self.VAB.ap().rearrange("(t p) n -> p t n", p=128)[:, a:a + 8, :], writes=[bVA])
        self.dma(KB, self.KB_T.ap().rearrange("(j p) t -> p j t", p=128), writes=[bKB])
        QA = self.alloc(4 * 2 * 512, BF16).rearrange("p (c e t) -> p c e t", c=4, e=2)
        QB = self.alloc(4 * 2 * 512, BF16).rearrange("p (c e t) -> p c e t", c=4, e=2)
        bQA, bQB = Buf(), Buf()
        for (qz, bq) in ((QA, bQA), (QB, bQB)):
            self.memset("pool", qz[64:128, :, 0, :], 0.0, writes=[bq])
            self.memset("pool", qz[0:64, :, 1, :], 0.0, writes=[bq])
        PTF = [(self.alloc(384), Buf()) for _ in range(2)]
        EB = self.alloc(8 * 384, BF16).rearrange("p (h n) -> p h n", h=8)
        bEB = Buf()
        oh1 = self.alloc(640)
        erb = self.alloc(8)
        esink = self.alloc(8)
        boh, berb, bes = Buf(), Buf(), Buf()
        self.dma(oh1[0:32, :], self.c_oh1.ap(), writes=[boh])
        self.dma(erb[0:32, :], self.relb.ap(), writes=[berb])
        self.dma(esink, self.sink.ap().broadcast_to([128, 8]), writes=[bes])
        self.act(erb[0:32, :], erb[0:32, :], AF.Exp, reads=[berb], writes=[berb])
        self.act(esink, esink, AF.Exp, reads=[bes], writes=[bes])
        ps6 = self.PS[:, 0:3072]
        for g in range(3):
            for qq in range(128):
                u0 = (g - 1) * 128 - qq + 256
                idx = g * 128 + qq
                bk = idx // 64
                self.mm(ps6[:, idx * 8:(idx + 1) * 8], oh1[0:32, u0:u0 + 128], erb[0:32, 0:8], True, True,
                        reads=[boh, berb], writes=[self.bb[bk]])
        ps6v = ps6.rearrange("p (n h) -> p n h", h=8)
        for h in range(8):
            self.cp("dve" if h % 2 == 0 else "act", EB[:, h, :], ps6v[:, :, h], reads=self.bb[0:6], writes=[bEB])

        def normalize0(*a, **k):
            return normalize(*a, **k)

        def load_block(b, what):
            sl = slice(b * 512, (b + 1) * 512)
            if what == "QA":
                v = self.QA_T.ap().rearrange("(c p) t -> p c t", p=128)
                self.dma(QA[0:64, :, 0, :], v[0:64, :, sl], writes=[bQA])
                self.dma(QA[64:128, :, 1, :], v[64:128, :, sl], writes=[bQA])
            elif what == "QB":
                v = self.QB_T.ap().rearrange("(c p) t -> p c t", p=128)
                self.dma(QB[0:64, :, 0, :], v[0:64, :, sl], writes=[bQB])
                self.dma(QB[64:128, :, 1, :], v[64:128, :, sl], writes=[bQB])
            elif what == "QM":
                self.dma(QM, self.QM_T.ap().rearrange("(c p) t -> p c t", p=128)[:, :, sl], writes=[bQM])
            else:
                self.dma(G, self.G_T.ap().rearrange("(c p) t -> p c t", p=128)[:, :, sl], writes=[bG])

        for w in ("QA", "G", "QB", "QM"):
            load_block(0, w)
        LA = 2
        for b in range(NB):
            for c in range(4):
                j = c // 2
                obs = obank(2)
                units = [(kt, hh) for kt in range(NT) for hh in range(2)]
                sbs = {}

                def qk(u):
                    kt, hh = u
                    sb, bsb = sbank()
                    self.mm(sb, KA[:, j, kt * 128:(kt + 1) * 128], QA[:, c, hh, :], True, True,
                            reads=[bKA, bQA], writes=[bsb])
                    sbs[u] = (sb, bsb)

                for u in units[:LA]:
                    qk(u)
                for idx, u in enumerate(units):
                    if idx + LA < len(units):
                        qk(units[idx + LA])
                    kt, hh = u
                    sb, bsb = sbs.pop(u)
                    pt, bpt = ptbuf()
                    self.act(pt[:, 0:512], sb, AF.Exp, reads=[bsb], writes=[bpt], scale=0.125)
                    ob, bob = obs[hh]
                    v0 = j * 192 + hh * 64
                    self.mm(ob, VA[:, kt, v0:v0 + 128], pt[:, 0:512], kt == 0, kt == NT - 1, reads=[bVA, bpt], writes=[bob])
                for hh in range(2):
                    ob, bob = obs[hh]
                    normalize(ob, bob, (hh * 64, hh * 64 + 64), ((1 - hh) * 64, (1 - hh) * 64 + 64), c)
            if b + 1 < NB:
                load_block(b + 1, "QA")
            bunits = [(h, ql) for h in range(8) for ql in range(4)]
            bsbs, bobs = {}, {}

            def b_stage1(u):
                h, ql = u
                c, hh, j = h // 2, h % 2, h // 4
                i = b * 4 + ql
                gs = [g for g in range(3) if 0 <= i + g - 1 < NT]
                sb, bsb = sbank()
                for g in gs:
                    kt = i + g - 1
                    self.mm(sb[:, g * 128:(g + 1) * 128], KB[:, j, kt * 128:(kt + 1) * 128],
                            QB[:, c, hh, ql * 128:(ql + 1) * 128], True, True, reads=[bKB, bQB], writes=[bsb])
                bsbs[u] = (sb, bsb, gs)

            def b_stage2(u, n):
                h, ql = u
                c, hh, j = h // 2, h % 2, h // 4
                r0, r1 = hh * 64, hh * 64 + 64
                i = b * 4 + ql
                if ql == 0:
                    bobs[h] = obank(1)[0]
                ob, bob = bobs[h]
                sb, bsb, gs = bsbs.pop(u)
                c0, c1 = gs[0] * 128, (gs[-1] + 1) * 128
                pf, bpf = PTF[n % 2]
                self.act(pf[:, c0:c1], sb[:, c0:c1], AF.Exp, reads=[bsb], writes=[bpf], scale=0.125)
                pt, bpt = ptbuf()
                self.tt("dve", pt[:, c0:c1], pf[:, c0:c1], EB[:, h, c0:c1], ALU.mult, reads=[bpf, bEB], writes=[bpt])
                v0 = (2 + j) * 192 + hh * 64
                for g in gs:
                    kt = i + g - 1
                    self.mm(ob[:, ql * 128:(ql + 1) * 128], VA[:, kt, v0:v0 + 128], pt[:, g * 128:(g + 1) * 128],
                            g == gs[0], g == gs[-1], reads=[bVA, bpt], writes=[bob])
                if ql == 3:
                    d0 = (1 - hh) * 64
                    normalize(ob, bob, (r0, r1), (d0, d0 + 64), 4 + c, add_scalar=(esink[d0:d0 + 64, h:h + 1], bes))

            LB = 3
            for u in bunits[:LB]:
                b_stage1(u)
            for n, u in enumerate(bunits):
                if n + LB < len(bunits):
                    b_stage1(bunits[n + LB])
                b_stage2(u, n)
            if b + 1 < NB:
                load_block(b + 1, "QB")
            mem_attn(b)
            if b + 1 < NB:
                load_block(b + 1, "QM")
            out_proj(b)
            if b + 1 < NB:
                load_block(b + 1, "G")

    def attn_layer1(self, L):
        P = self.P
        QM, G, YT, bQM, bG, bYT = L["QM"], L["G"], L["YT"], L["bQM"], L["bG"], L["bYT"]
        sbank, obank, ptbuf, normalize, mem_attn, out_proj = (L["sbank"], L["obank"], L["ptbuf"], L["normalize"],
                                                              L["mem_attn"], L["out_proj"])
        st = L["st"]
        E = self.alloc(16 * 9 * 128, BF16).rearrange("p (h t q) -> p h t q", h=16, t=9)
        bE = Buf()
        save = self.aoff
        erT = self.alloc(240)
        berT = Buf()
        self.dma(erT[0:31, :], self.rpbT.ap(), writes=[berT])
        self.act(erT[0:31, :], erT[0:31, :], AF.Exp, reads=[berT], writes=[berT])
        self.memset("pool", E.rearrange("p h t q -> p (h t q)"), 0.0, writes=[bE])
        ohp = [(self.alloc(1024), Buf()) for _ in range(2)]
        tiles = [(-3, False), (-2, False), (-2, True), (-1, False), (0, False), (1, False), (2, True), (2, False), (3, False)]
        ps4 = self.PS[:, 0:2048].rearrange("p (q n) -> p q n", n=256)
        cnt = 0
        for r in range(8):
            oh, boh = ohp[r % 2]
            self.dma(oh[0:31, :], self.c_ohc.ap()[:, r * 1024:(r + 1) * 1024], writes=[boh])
            for i in range(8):
                self.mm(self.PS[:, i * 256:i * 256 + 240], oh[0:31, i * 128:(i + 1) * 128], erT[0:31, 0:240], True, True,
                        reads=[boh, berT], writes=[self.bb[i // 2]])
            for ti, (c, masked) in enumerate(tiles):
                for kl in range(2):
                    for ql in range(2):
                        dkr = 2 * c + kl - ql
                        if abs(dkr) > 7:
                            continue
                        if masked and not (-4 <= dkr <= 3):
                            continue
                        dr = dkr + 7
                        src = ps4[kl * 64:(kl + 1) * 64, :, 0:240].rearrange("p q (h r) -> p h q r", r=15)[:, :, :, dr]
                        dst = E[kl * 64:(kl + 1) * 64, :, ti, ql * 64 + r * 8:ql * 64 + r * 8 + 8]
                        self.cp("dve" if cnt % 2 == 0 else "act", dst, src, reads=self.bb[0:4], writes=[bE])
                        cnt += 1
        self.P.barrier()
        self.aoff = save
        R = 12
        KR = self.alloc(8 * R * 128, BF16).rearrange("p (c t) -> p c t", c=8)
        VR = self.alloc(R * 1536, BF16).rearrange("p (s n) -> p s n", s=R)
        bKR = [Buf() for _ in range(R)]
        bVR = [Buf() for _ in range(R)]
        QC = self.alloc(8 * 2 * 512, BF16).rearrange("p (c e t) -> p c e t", c=8, e=2)
        bQC = Buf()
        self.memset("pool", QC[64:128, :, 0, :], 0.0, writes=[bQC])
        self.memset("pool", QC[0:64, :, 1, :], 0.0, writes=[bQC])
        sslot = [Buf() for _ in range(20)]
        PTF = [(self.alloc(640), Buf()) for _ in range(2)]
        kview = self.KC_T.ap().rearrange("(c p) t -> p c t", p=128)

        def load_tile(kt):
            s_ = kt % R
            self.dma(KR[:, :, s_ * 128:(s_ + 1) * 128], kview[:, :, kt * 128:(kt + 1) * 128], writes=[bKR[s_]])
            self.dma(VR[:, s_, :], self.VC.ap()[kt * 128:(kt + 1) * 128, :], writes=[bVR[s_]])

        def load_block(b, what):
            sl = slice(b * 512, (b + 1) * 512)
            if what == "QC":
                v = self.QC_T.ap().rearrange("(c p) t -> p c t", p=128)
                self.dma(QC[0:64, :, 0, :], v[0:64, :, sl], writes=[bQC])
                self.dma(QC[64:128, :, 1, :], v[64:128, :, sl], writes=[bQC])
            elif what == "QM":
                self.dma(QM, self.QM_T.ap().rearrange("(c p) t -> p c t", p=128)[:, :, sl], writes=[bQM])
            else:
                self.dma(G, self.G_T.ap().rearrange("(c p) t -> p c t", p=128)[:, :, sl], writes=[bG])

        def window(qt):
            if qt == 0:
                return [(0, 4), (1, 5), (2, 7), (3, 8)]
            if qt == 1:
                return [(0, 3), (1, 4), (2, 5), (3, 7)]
            if qt == 30:
                return [(28, 1), (29, 3), (30, 4), (31, 5)]
            if qt == 31:
                return [(28, 0), (29, 1), (30, 3), (31, 4)]
            return [(qt - 2 + i, 2 + i) for i in range(5)]

        for w in ("QC", "G", "QM"):
            load_block(0, w)
        for kt in range(0, 6):
            load_tile(kt)
        for b in range(NB):
            if b + 1 < NB:
                for kt in range(4 * b + 6, min(4 * b + 10, NT)):
                    load_tile(kt)
            cunits = [(h, ql) for h in range(16) for ql in range(4)]
            cst, cobs = {}, {}

            def c_stage1(u, n):
                h, ql = u
                c, hh = h // 2, h % 2
                win = window(b * 4 + ql)
                base = (n % 4) * 5
                for i, (kt, ti) in enumerate(win):
                    s_ = kt % R
                    col = (base + i) * 128
                    self.mm(self.PS[:, col:col + 128], KR[:, c, s_ * 128:(s_ + 1) * 128],
                            QC[:, c, hh, ql * 128:(ql + 1) * 128], True, True,
                            reads=[bKR[s_], bQC], writes=[sslot[base + i]])
                cst[u] = (win, base)

            def c_stage2(u, n):
                h, ql = u
                c, hh = h // 2, h % 2
                r0, r1 = hh * 64, hh * 64 + 64
                if ql == 0:
                    cobs[h] = obank(1)[0]
                ob, bob = cobs[h]
                win, base = cst.pop(u)
                nw = len(win)
                sb2 = self.PS[:, base * 128:(base + nw) * 128]
                pf, bpf = PTF[n % 2]
                self.act(pf[:, 0:nw * 128], sb2, AF.Exp, reads=sslot[base:base + nw], writes=[bpf], scale=0.125)
                pt, bpt = ptbuf()
                i0 = 0
                while i0 < nw:
                    i1 = i0
                    while i1 + 1 < nw and win[i1 + 1][1] == win[i1][1] + 1:
                        i1 += 1
                    t0, t1 = win[i0][1], win[i1][1]
                    ev = E[:, h, t0:t1 + 1, :].rearrange("p t q -> p (t q)")
                    self.tt("dve", pt[:, i0 * 128:(i1 + 1) * 128], pf[:, i0 * 128:(i1 + 1) * 128], ev, ALU.mult,
                            reads=[bpf, bE], writes=[bpt])
                    i0 = i1 + 1
                v0 = c * 192 + hh * 64
                for i, (kt, ti) in enumerate(win):
                    s_ = kt % R
                    self.mm(ob[:, ql * 128:(ql + 1) * 128], VR[:, s_, v0:v0 + 128], pt[:, i * 128:(i + 1) * 128],
                            i == 0, i == nw - 1, reads=[bVR[s_], bpt], writes=[bob])
                if ql == 3:
                    d0 = (1 - hh) * 64
                    normalize(ob, bob, (r0, r1), (d0, d0 + 64), c)

            LC = 3
            for n, u in enumerate(cunits[:LC]):
                c_stage1(u, n)
            for n, u in enumerate(cunits):
                if n + LC < len(cunits):
                    c_stage1(cunits[n + LC], n + LC)
                c_stage2(u, n)
            if b + 1 < NB:
                load_block(b + 1, "QC")
            mem_attn(b)
            if b + 1 < NB:
                load_block(b + 1, "QM")
            out_proj(b)
            if b + 1 < NB:
                load_block(b + 1, "G")


def prep_inputs(inputs):
    f = lambda a: np.ascontiguousarray(np.asarray(a, dtype=np.float32))
    x = f(inputs["x"])
    mem = f(inputs["mem"])
    ng = f(inputs["norm_gain"])
    mg = f(inputs["mem_norm_gain"])
    gains = np.concatenate([ng[0].reshape(8, 128).T, ng[1].reshape(8, 128).T, mg.reshape(8, 128).T], axis=1)
    qkg = np.stack([np.tile(f(inputs["q_norm_a"])[0], 2), np.tile(f(inputs["k_norm_a"])[0], 2)], axis=1)
    shared = {
        "w_in_even": f(inputs["w_in_even"])[0], "w_in_odd": f(inputs["w_in_odd"])[0],
        "w_out_even": f(inputs["w_out_even"])[0], "w_out_odd": f(inputs["w_out_odd"])[0],
        "w_mem_kv": f(inputs["w_mem_kv"]),
        "gains_pp": f(gains), "qk_gain": f(qkg),
        "final_gain": f(inputs["final_norm_gain"]).reshape(1, D),
        "sink_b": f(inputs["sink_b"]).reshape(1, 8),
        "rel_bias": f(inputs["rel_bias"]),
        "rpbT": f(np.transpose(f(inputs["rpb_c"])[0], (2, 0, 1)).reshape(31, 240)),
    }
    shared.update(host_consts())
    in_maps = []
    for b in range(8):
        m = dict(shared)
        m["x"] = x[b]
        m["mem"] = mem[b]
        in_maps.append(m)
    return in_maps


def kernel(**inputs):
    bld = Builder()
    nc = bld.build()
    in_maps = prep_inputs(inputs)
    res = run_bass_kernel_spmd(nc, in_maps, core_ids=list(range(8)))
    return np.stack([np.asarray(r["out"]) for r in res.results], axis=0).astype(np.float32)
```

```python
from contextlib import ExitStack
import math
import os
import numpy as np
import concourse.bass as bass
import concourse.mybir as mybir
from concourse.bass_utils import run_bass_kernel_spmd

F32 = mybir.dt.float32
BF16 = mybir.dt.bfloat16
ALU = mybir.AluOpType
AF = mybir.ActivationFunctionType
AX = mybir.AxisListType

ENGS = ("pe", "act", "dve", "pool", "sp")
DMA_SLOTS = 12

S = 4096
D = 1024
NT = 32
NB = 8
EPS = 1e-6


class Buf:
    __slots__ = ("w", "r", "name")

    def __init__(self, name=""):
        self.w = None
        self.r = []
        self.name = name


class Op:
    __slots__ = ("eng", "fn", "deps", "needs_inc", "inc_val", "dma", "slot", "slot_val", "id")

    def __init__(self, eng, fn, dma):
        self.eng = eng
        self.fn = fn
        self.dma = dma
        self.deps = set()
        self.needs_inc = False
        self.inc_val = None
        self.slot = None
        self.slot_val = None
        self.id = None


class Prog:
    def __init__(self, nc):
        self.nc = nc
        self.ops = []
        self.dma_count = {e: 0 for e in ENGS}
        self.dma_hist = {e: [] for e in ENGS}
        self.last_op = {e: None for e in ENGS}

    def add(self, eng, fn, reads=(), writes=(), dma=False):
        op = Op(eng, fn, dma)
        op.id = len(self.ops)
        deps = set()
        for b in reads:
            if b.w is not None:
                deps.add(b.w)
        for b in writes:
            if b.w is not None:
                deps.add(b.w)
            for r in b.r:
                deps.add(r)
        for b in reads:
            b.r.append(op)
        for b in writes:
            b.w = op
            b.r = []
        for d in deps:
            if d is op:
                continue
            if (not d.dma) and (not dma) and d.eng == "pe" and eng == "pe":
                continue
            op.deps.add(d)
            if not d.dma:
                d.needs_inc = True
        if dma:
            n = self.dma_count[eng]
            op.slot = n % DMA_SLOTS
            op.slot_val = 16 * (n // DMA_SLOTS + 1)
            hist = self.dma_hist[eng]
            if n >= DMA_SLOTS:
                op.deps.add(hist[n - DMA_SLOTS])
            hist.append(op)
            self.dma_count[eng] = n + 1
        else:
            self.last_op[eng] = op
        self.ops.append(op)
        return op

    def barrier(self):
        lasts = [self.last_op[e] for e in ENGS if self.last_op[e] is not None]
        dmas = []
        for e in ENGS:
            dmas += self.dma_hist[e][-DMA_SLOTS:]
        for e in ENGS:
            op = Op(e, None, False)
            op.id = len(self.ops)
            for d in lasts:
                if d.eng != e:
                    op.deps.add(d)
                    d.needs_inc = True
            for d in dmas:
                op.deps.add(d)
            self.ops.append(op)

    def emit(self):
        nc = self.nc
        with ExitStack() as st:
            sem = {e: st.enter_context(nc.semaphore("c_" + e)) for e in ENGS}
            dsem = {}
            for e in ENGS:
                if self.dma_count[e] > 0:
                    dsem[e] = [st.enter_context(nc.semaphore("d_%s_%d" % (e, i))) for i in range(DMA_SLOTS)]
            cnt = {e: 0 for e in ENGS}
            for op in self.ops:
                if op.dma or op.fn is None:
                    continue
                if op.needs_inc:
                    cnt[op.eng] += 1
                    op.inc_val = cnt[op.eng]
            per_eng = {e: [] for e in ENGS}
            seen = {e: {} for e in ENGS}
            for op in self.ops:
                waits = {}
                for d in op.deps:
                    if d.dma:
                        key = ("d", d.eng, d.slot)
                        val = d.slot_val
                    else:
                        key = ("c", d.eng)
                        val = d.inc_val
                    if waits.get(key, 0) < val:
                        waits[key] = val
                wl = []
                s = seen[op.eng]
                for key, val in waits.items():
                    if s.get(key, 0) >= val:
                        continue
                    s[key] = val
                    wl.append((key, val))
                per_eng[op.eng].append((op, wl))
            block = st.enter_context(nc.Block())

            def run(engname, e):
                for op, wl in per_eng[engname]:
                    for key, val in wl:
                        if key[0] == "d":
                            e.wait_ge(dsem[key[1]][key[2]], val)
                        else:
                            e.wait_ge(sem[key[1]], val)
                    if op.fn is None:
                        continue
                    ins = op.fn(e)
                    if op.dma:
                        ins.then_inc(dsem[engname][op.slot], 16)
                    elif op.needs_inc:
                        ins.then_inc(sem[engname], 1)

            @block.tensor
            def _(e):
                run("pe", e)

            @block.scalar
            def _(e):
                run("act", e)

            @block.vector
            def _(e):
                run("dve", e)

            @block.gpsimd
            def _(e):
                run("pool", e)

            @block.sync
            def _(e):
                run("sp", e)


def _t5_bucket(rel):
    nb = 16
    max_exact = 8
    ret = np.where(rel > 0, nb, 0)
    n = np.abs(rel)
    nf = np.maximum(n, 1).astype(np.float32)
    large = max_exact + (np.log(nf / np.float32(max_exact)) / np.float32(math.log(128 / max_exact))
                         * np.float32(nb - max_exact)).astype(np.int32)
    large = np.minimum(large, nb - 1)
    return ret + np.where(n < max_exact, n, large)


def host_consts():
    c = {}
    c["ident"] = np.eye(128, dtype=np.float32)
    R = np.zeros((128, 128), np.float32)
    for p in range(128):
        sub = p % 32
        if sub < 16:
            R[p, p + 16] = -1.0
        else:
            R[p, p - 16] = 1.0
    c["rotT"] = np.ascontiguousarray(R.T)
    bd = np.zeros((128, 128), np.float32)
    bd[0:64, 0:64] = 1.0
    bd[64:128, 64:128] = 1.0
    c["bd"] = bd
    t = np.arange(S)
    row = (t // 64).astype(np.float32)
    col = (t % 64).astype(np.float32)
    freqs = np.power(np.float32(10000.0), -np.arange(16, dtype=np.float32) / np.float32(16)).astype(np.float32)
    cosT = np.zeros((128, S), np.float32)
    sinT = np.zeros((128, S), np.float32)
    for p in range(128):
        dh = p % 64
        pos = row if dh < 32 else col
        ang = (pos * freqs[dh % 16]).astype(np.float32)
        cosT[p] = np.cos(ang)
        sinT[p] = np.sin(ang)
    c["cosT"] = cosT
    c["sinT"] = sinT
    rel = np.arange(-256, 384)
    bk = _t5_bucket(rel)
    oh1 = np.zeros((32, 640), np.float32)
    for u, r in enumerate(rel):
        if abs(r) <= 128:
            oh1[bk[u], u] = 1.0
    c["oh1"] = oh1
    ohc = np.zeros((31, 64, 128), np.float32)
    for qc in range(64):
        cs = min(max(qc - 8, 0), 48)
        for kc in range(cs, cs + 16):
            dc = kc - qc + 15
            ohc[dc, qc, kc] = 1.0
            ohc[dc, qc, 64 + kc] = 1.0
    c["ohc"] = ohc.reshape(31, 64 * 128)
    return c


class Builder:
    def __init__(self, debug=None, stop_after=None):
        self.debug = debug or ()
        self.stop_after = stop_after
        self.nc = bass.Bass("TRN2", target_bir_lowering=False)
        self.P = Prog(self.nc)
        self.dram = {}

    def din(self, name, shape, dt=F32):
        t = self.nc.dram_tensor(name, list(shape), dt, kind="ExternalInput")
        self.dram[name] = t
        return t

    def dscratch(self, name, shape, dt):
        kind = "ExternalOutput" if name in self.debug else "Internal"
        t = self.nc.dram_tensor(name, list(shape), dt, kind=kind)
        self.dram[name] = t
        return t

    def reset_arena(self):
        self.aoff = 0

    def alloc(self, ncols, dt=F32):
        nbytes = ncols * (4 if dt == F32 else 2)
        n32 = (nbytes + 3) // 4
        n32 = (n32 + 1) // 2 * 2
        a = self.ARENA[:, self.aoff:self.aoff + n32]
        self.aoff += n32
        assert self.aoff <= self.ARENA_N, "arena overflow %d > %d" % (self.aoff, self.ARENA_N)
        if dt == F32:
            return a[:, 0:ncols]
        return a.bitcast(BF16)[:, 0:ncols]

    def dma(self, out, in_, reads=(), writes=(), q="sp", **kw):
        return self.P.add(q, lambda e: e.dma_start(out=out, in_=in_, **kw), reads, writes, dma=True)

    def mm(self, out, lhsT, rhs, start, stop, reads=(), writes=()):
        return self.P.add("pe", lambda e: e.matmul(out, lhsT=lhsT, rhs=rhs, start=start, stop=stop), reads, writes)

    def tr(self, out, in_, ident, reads=(), writes=()):
        return self.P.add("pe", lambda e: e.transpose(out, in_, ident), reads, writes)

    def act(self, out, in_, func, reads=(), writes=(), **kw):
        return self.P.add("act", lambda e: e.activation(out=out, in_=in_, func=func, **kw), reads, writes)

    def tt(self, eng, out, in0, in1, op, reads=(), writes=()):
        return self.P.add(eng, lambda e: e.tensor_tensor(out=out, in0=in0, in1=in1, op=op), reads, writes)

    def ts(self, eng, out, in0, s1, op0, reads=(), writes=(), s2=None, op1=None):
        if op1 is None:
            return self.P.add(eng, lambda e: e.tensor_scalar(out=out, in0=in0, scalar1=s1, scalar2=None, op0=op0), reads, writes)
        return self.P.add(eng, lambda e: e.tensor_scalar(out=out, in0=in0, scalar1=s1, scalar2=s2, op0=op0, op1=op1), reads, writes)

    def cp(self, eng, out, in_, reads=(), writes=()):
        if eng == "act":
            return self.P.add("act", lambda e: e.copy(out=out, in_=in_), reads, writes)
        return self.P.add(eng, lambda e: e.tensor_copy(out=out, in_=in_), reads, writes)

    def memset(self, eng, ap, val, writes=()):
        return self.P.add(eng, lambda e: e.memset(ap, val), (), writes)

    def recip(self, out, in_, reads=(), writes=()):
        return self.P.add("dve", lambda e: e.reciprocal(out=out, in_=in_), reads, writes)

    def build(self):
        nc = self.nc
        self.x = self.din("x", [S, D])
        self.mem = self.din("mem", [256, D])
        self.w_in = [self.din("w_in_even", [D, 3584]), self.din("w_in_odd", [D, 5120])]
        self.w_out = [self.din("w_out_even", [1536, D]), self.din("w_out_odd", [1536, D])]
        self.w_mem = self.din("w_mem_kv", [2, D, 1024])
        self.gains = self.din("gains_pp", [128, 24])
        self.qkg = self.din("qk_gain", [128, 2])
        self.fgain = self.din("final_gain", [1, D])
        self.sink = self.din("sink_b", [1, 8])
        self.relb = self.din("rel_bias", [32, 8])
        self.rpbT = self.din("rpbT", [31, 240])
        self.c_ident = self.din("ident", [128, 128])
        self.c_rotT = self.din("rotT", [128, 128])
        self.c_bd = self.din("bd", [128, 128])
        self.c_cos = self.din("cosT", [128, S])
        self.c_sin = self.din("sinT", [128, S])
        self.c_oh1 = self.din("oh1", [32, 640])
        self.c_ohc = self.din("ohc", [31, 64 * 128])
        self.out = nc.dram_tensor("out", [S, D], F32, kind="ExternalOutput")
        self.QA_T = self.dscratch("QA_T", [512, S], BF16)
        self.KA_T = self.dscratch("KA_T", [256, S], BF16)
        self.QB_T = self.dscratch("QB_T", [512, S], BF16)
        self.KB_T = self.dscratch("KB_T", [256, S], BF16)
        self.QM_T = self.dscratch("QM_T", [512, S], BF16)
        self.G_T = self.dscratch("G_T", [1536, S], BF16)
        self.VAB = self.dscratch("VAB", [S, 768], BF16)
        self.X1 = self.dscratch("X1", [S, D], F32)
        self.QC_T = self.dscratch("QC_T", [1024, S], BF16)
        self.KC_T = self.dscratch("KC_T", [1024, S], BF16)
        self.VC = self.dscratch("VC", [S, 1536], BF16)
        self.MK_T = self.dscratch("MK_T", [2, 512, 256], BF16)
        self.MV = self.dscratch("MV", [2, 256, 512], BF16)

        with ExitStack() as st:
            self.ARENA_N = 52000
            self.ARENA = st.enter_context(nc.sbuf_tensor("arena", [128, self.ARENA_N], F32))
            self.PS = st.enter_context(nc.psum_tensor("ps", [128, 4096], F32))
            self.bank = [self.PS[:, i * 512:(i + 1) * 512] for i in range(8)]
            self.bb = [Buf("bank%d" % i) for i in range(8)]
            self.phase_mem()
            self.P.barrier()
            for layer in range(2):
                self.phase_proj(layer)
                self.P.barrier()
                if self.stop_after == ("proj", layer):
                    break
                self.phase_attn(layer)
                self.P.barrier()
                if self.stop_after == ("attn", layer):
                    break
            self.P.emit()
        return nc

    def load_consts(self):
        c = {}
        c["ident"] = self.alloc(128)
        c["b_ident"] = Buf()
        self.dma(c["ident"], self.c_ident.ap(), writes=[c["b_ident"]])
        c["gains"] = self.alloc(24)
        c["b_gains"] = Buf()
        self.dma(c["gains"], self.gains.ap(), writes=[c["b_gains"]])
        return c

    def norm_transpose(self, c, xt, bx, ht3, bht, j, gcol, tp_banks, btp, scr):
        junk, bjunk, stat, bstat = scr
        self.act(junk, xt, AF.Square, reads=[bx], writes=[bjunk, bstat], accum_out=stat[:, 0:1])
        self.ts("dve", stat[:, 1:2], stat[:, 0:1], 1.0 / D, ALU.mult, reads=[bstat], writes=[bstat], s2=EPS, op1=ALU.add)
        if os.environ.get("LN_STAT", "0") == "1":
            self.act(stat[:, 2:3], stat[:, 1:2], AF.Ln, reads=[bstat], writes=[bstat])
            self.act(stat[:, 3:4], stat[:, 2:3], AF.Exp, reads=[bstat], writes=[bstat], scale=-0.5)
        else:
            self.act(stat[:, 2:3], stat[:, 1:2], AF.Sqrt, reads=[bstat], writes=[bstat])
            self.recip(stat[:, 3:4], stat[:, 2:3], reads=[bstat], writes=[bstat])
        self.ts("dve", xt, xt, stat[:, 3:4], ALU.mult, reads=[bx, bstat], writes=[bx])
        tp = tp_banks
        for kc in range(8):
            self.tr(tp[:, kc * 128:(kc + 1) * 128], xt[:, kc * 128:(kc + 1) * 128], c["ident"],
                    reads=[bx, c["b_ident"]], writes=list(btp))
        g = c["gains"][:, gcol:gcol + 8]
        gb = g.unsqueeze(2).broadcast_to([128, 8, 128])
        tp3 = tp.rearrange("p (a b) -> p a b", a=8)
        self.tt("dve", ht3[:, :, j * 128:(j + 1) * 128], tp3, gb, ALU.mult,
                reads=list(btp) + [c["b_gains"]], writes=[bht])

    def phase_mem(self):
        self.reset_arena()
        c = self.load_consts()
        W = self.alloc(2 * 8 * 1024, BF16).rearrange("p (l k n) -> p l k n", l=2, k=8)
        bW = Buf()
        for l in range(2):
            self.dma(W[:, l], self.w_mem.ap()[l].rearrange("(k p) n -> p k n", p=128), writes=[bW], q="pool")
        ht3 = self.alloc(8 * 256, BF16).rearrange("p (k t) -> p k t", k=8)
        bht = Buf()
        junk = self.alloc(1024, BF16)
        scr = (junk, Buf(), self.alloc(4), Buf())
        tpb = self.PS[:, 0:1024]
        btp = [self.bb[0], self.bb[1]]
        for j in range(2):
            xt = self.alloc(1024)
            bx = Buf()
            self.dma(xt, self.mem.ap()[j * 128:(j + 1) * 128, :], writes=[bx])
            self.norm_transpose(c, xt, bx, ht3, bht, j, 16, tpb, btp, scr)
        stg = self.alloc(4 * 256, BF16).rearrange("p (c t) -> p c t", c=4)
        stv = self.alloc(2 * 512, BF16).rearrange("p (c t) -> p c t", c=2)
        bst, bsv = Buf(), Buf()
        for l in range(2):
            for hm in range(4):
                pb = self.bank[2 + hm % 2]
                bpb = self.bb[2 + hm % 2]
                for kc in range(8):
                    self.mm(pb[:, 0:256], W[:, l, kc, hm * 128:(hm + 1) * 128], ht3[:, kc, :], kc == 0, kc == 7,
                            reads=[bW, bht], writes=[bpb])
                self.cp("dve", stg[:, hm, :], pb[:, 0:256], reads=[bpb], writes=[bst])
            self.dma(self.MK_T.ap()[l].rearrange("(c p) t -> p c t", p=128), stg, reads=[bst])
            for mt in range(2):
                pb = self.bank[4 + mt]
                bpb = self.bb[4 + mt]
                for kc in range(8):
                    self.mm(pb, ht3[:, kc, mt * 128:(mt + 1) * 128], W[:, l, kc, 512:1024], kc == 0, kc == 7,
                            reads=[bW, bht], writes=[bpb])
                self.cp("act", stv[:, mt, :], pb, reads=[bpb], writes=[bsv])
            self.dma(self.MV.ap()[l].rearrange("(c p) n -> p c n", p=128), stv, reads=[bsv])

    def phase_proj(self, layer):
        self.reset_arena()
        c = self.load_consts()
        if layer == 0:
            wl = [(0, 512, 0)]
            wl += [(512, 64, 512), (512, 64, 576), (576, 64, 640), (576, 64, 704)]
            wl += [(768, 512, 768)]
            wl += [(1280, 64, 1280), (1280, 64, 1344), (1344, 64, 1408), (1344, 64, 1472)]
            wl += [(1536, 512, 1536), (2048, 1536, 2048), (640, 128, 3584), (1408, 128, 3712)]
            NC = 3840
            groups = [("qa", 0, 4, self.QA_T, "rope_q"), ("ka", 512, 2, self.KA_T, "rope_k"),
                      ("qb", 768, 4, self.QB_T, "copy"), ("kb", 1280, 2, self.KB_T, "copy"),
                      ("qm", 1536, 4, self.QM_T, "copy"), ("gate", 2048, 12, self.G_T, "silu")]
            vcol, vn = 3584, 256
        else:
            wl = [(0, 2048, 0), (3072, 2048, 2048), (2048, 1024, 4096)]
            NC = 5120
            groups = [("qc", 0, 8, self.QC_T, "copy"), ("kc", 1024, 8, self.KC_T, "copy"),
                      ("qm", 2048, 4, self.QM_T, "copy"), ("gate", 2560, 12, self.G_T, "silu")]
            vcol, vn = 4096, 1024
        W = self.alloc(8 * NC, BF16).rearrange("p (k n) -> p k n", k=8)
        bW = Buf()
        wsrc = self.w_in[layer].ap().rearrange("(k p) n -> p k n", p=128)
        for (s0, n, d0) in wl:
            for a in range(0, n, 1024):
                m = min(1024, n - a)
                self.dma(W[:, :, d0 + a:d0 + a + m], wsrc[:, :, s0 + a:s0 + a + m], writes=[bW], q="pool")
        if layer == 0:
            rotT = self.alloc(128)
            bd = self.alloc(128)
            qkg = self.alloc(2)
            bcst = Buf()
            self.dma(rotT, self.c_rotT.ap(), writes=[bcst])
            self.dma(bd, self.c_bd.ap(), writes=[bcst])
            self.dma(qkg, self.qkg.ap(), writes=[bcst])
            epsT = self.alloc(2)
            self.memset("pool", epsT, EPS, writes=[bcst])
            cs = [(self.alloc(512), self.alloc(512), Buf()) for _ in range(2)]
            ropeT = [(self.alloc(512), self.alloc(512), self.alloc(512), self.alloc(512), Buf(), Buf(), Buf(), Buf())
                     for _ in range(2)]
        XT = [self.alloc(1024) for _ in range(4)]
        bXT = [Buf() for _ in range(4)]
        HT = [self.alloc(8 * 512, BF16).rearrange("p (k t) -> p k t", k=8) for _ in range(2)]
        bHT = [Buf() for _ in range(2)]
        junk = self.alloc(1024, BF16)
        scr = (junk, Buf(), self.alloc(4), Buf())
        tpb = self.PS[:, 0:1024]
        btp = [self.bb[0], self.bb[1]]
        stg = {}
        for (name, col, nch, dst, kind) in groups:
            stg[name] = [(self.alloc(nch * 512, BF16).rearrange("p (c t) -> p c t", c=nch), Buf()) for _ in range(2)]
        if layer == 0:
            VST = [(self.alloc(768, BF16), Buf()) for _ in range(2)]
        else:
            VST = [(self.alloc(1536, BF16), Buf()) for _ in range(2)]
        for (v, bv) in VST:
            self.memset("pool", v, 1.0, writes=[bv])
        pbanks = [self.bank[2], self.bank[3], self.bank[4]]
        bpb = [self.bb[2], self.bb[3], self.bb[4]]
        aux = [self.bank[5], self.bank[6]]
        baux = [self.bb[5], self.bb[6]]
        vbank = self.bank[7]
        bvb = self.bb[7]
        xsrc = self.x if layer == 0 else self.X1
        gcol = 0 if layer == 0 else 8

        def load_x(b):
            for j in range(4):
                t = b * 4 + j
                self.dma(XT[j], xsrc.ap()[t * 128:(t + 1) * 128, :], writes=[bXT[j]])

        def norm_tr(b):
            for j in range(4):
                self.norm_transpose(c, XT[j], bXT[j], HT[b % 2], bHT[b % 2], j, gcol, tpb, btp, scr)

        state = {"pi": 0, "vi": 0, "ri": 0}
        pending = []

        def do_chunk(b, name, col, ci, kind, sbuf, bs):
            i = state["pi"] % 3
            state["pi"] += 1
            pb, bp = pbanks[i], bpb[i]
            ht, bh = HT[b % 2], bHT[b % 2]
            for kc in range(8):
                self.mm(pb, W[:, kc, col + ci * 128:col + (ci + 1) * 128], ht[:, kc, :], kc == 0, kc == 7,
                        reads=[bW, bh], writes=[bp])
            while pending:
                pending.pop(0)()
            dst = sbuf[:, ci, :]
            if kind == "copy":
                self.cp("act" if (state["pi"] % 2 == 0) else "dve", dst, pb, reads=[bp], writes=[bs])
            elif kind == "silu":
                self.act(dst, pb, AF.Silu, reads=[bp], writes=[bs])
            else:
                gi = 0 if kind == "rope_q" else 1
                cosb, sinb, bcs = cs[b % 2]
                tq, tsq, t1, trs, btq, btsq, bt1, btrs = ropeT[state["ri"] % 2]
                state["ri"] += 1
                self.act(tq, pb, AF.Copy, reads=[bp, bcst], writes=[btq], scale=qkg[:, gi:gi + 1])
                self.act(tsq, pb, AF.Square, reads=[bp], writes=[btsq])

                def tail():
                    self.mm(aux[0], bd, tsq, True, True, reads=[bcst, btsq], writes=[baux[0]])
                    self.mm(aux[1], rotT, tq, True, True, reads=[bcst, btq], writes=[baux[1]])
                    if os.environ.get("LN_ROPE", "1") == "1":
                        self.act(trs, aux[0], AF.Ln, reads=[baux[0], bcst], writes=[btrs], scale=1.0 / 64, bias=epsT[:, 0:1])
                        self.act(trs, trs, AF.Exp, reads=[btrs], writes=[btrs], scale=-0.5)
                    else:
                        self.ts("dve", trs, aux[0], 1.0 / 64, ALU.mult, reads=[baux[0]], writes=[btrs], s2=EPS, op1=ALU.add)
                        self.act(trs, trs, AF.Sqrt, reads=[btrs], writes=[btrs])
                        self.recip(trs, trs, reads=[btrs], writes=[btrs])
                    self.tt("dve", t1, tq, cosb, ALU.mult, reads=[btq, bcs], writes=[bt1])
                    self.tt("dve", tsq, aux[1], sinb, ALU.mult, reads=[baux[1], bcs, btsq], writes=[btsq])
                    self.tt("dve", t1, t1, tsq, ALU.add, reads=[bt1, btsq], writes=[bt1])
                    self.tt("dve", dst, t1, trs, ALU.mult, reads=[bt1, btrs], writes=[bs])

                pending.append(tail)

        def do_v(b):
            ht, bh = HT[b % 2], bHT[b % 2]
            for j in range(4):
                t = b * 4 + j
                v, bv = VST[state["vi"] % 2]
                state["vi"] += 1
                if layer == 0:
                    for kc in range(8):
                        self.mm(vbank[:, 0:256], ht[:, kc, j * 128:(j + 1) * 128], W[:, kc, vcol:vcol + 256],
                                kc == 0, kc == 7, reads=[bW, bh], writes=[bvb])
                    v3 = v.rearrange("p (g s) -> p g s", g=4)
                    src = vbank[:, 0:256].rearrange("p (g d) -> p g d", g=4)
                    self.cp("dve", v3[:, :, 0:64], src, reads=[bvb], writes=[bv])
                    self.cp("act", v3[:, :, 128:192], src, reads=[bvb], writes=[bv])
                    self.dma(self.VAB.ap()[t * 128:(t + 1) * 128, :], v, reads=[bv])
                else:
                    v3 = v.rearrange("p (g s) -> p g s", g=8)
                    for half in range(2):
                        for kc in range(8):
                            self.mm(vbank, ht[:, kc, j * 128:(j + 1) * 128],
                                    W[:, kc, vcol + half * 512:vcol + (half + 1) * 512],
                                    kc == 0, kc == 7, reads=[bW, bh], writes=[bvb])
                        src = vbank.rearrange("p (g e d) -> p g e d", g=4, e=2)
                        self.cp("dve", v3[:, half * 4:(half + 1) * 4, 0:64], src[:, :, 0, :], reads=[bvb], writes=[bv])
                        self.cp("act", v3[:, half * 4:(half + 1) * 4, 128:192], src[:, :, 1, :], reads=[bvb], writes=[bv])
                    self.dma(self.VC.ap()[t * 128:(t + 1) * 128, :], v, reads=[bv])

        chunks = []
        for (name, col, nch, dst, kind) in groups:
            for ci in range(nch):
                chunks.append((name, col, ci, nch, dst, kind))
        if layer == 0:
            rope = [ch for ch in chunks if ch[5].startswith("rope")]
            plain = [ch for ch in chunks if not ch[5].startswith("rope")]
            order = []
            while rope or plain:
                if rope:
                    order.append(rope.pop(0))
                for _ in range(3):
                    if plain:
                        order.append(plain.pop(0))
            chunks = order
        load_x(0)
        norm_tr(0)
        for b in range(NB):
            if layer == 0:
                cosb, sinb, bcs = cs[b % 2]
                self.dma(cosb, self.c_cos.ap()[:, b * 512:(b + 1) * 512], writes=[bcs])
                self.dma(sinb, self.c_sin.ap()[:, b * 512:(b + 1) * 512], writes=[bcs])
            if b + 1 < NB:
                load_x(b + 1)
            half_n = len(chunks) // 2
            done = {}
            stores = []
            for idx, (name, col, ci, nch, dst, kind) in enumerate(chunks):
                if idx == half_n and b + 1 < NB:
                    norm_tr(b + 1)
                sbuf, bs = stg[name][b % 2]
                do_chunk(b, name, col, ci, kind, sbuf, bs)
                done[name] = done.get(name, 0) + 1
                if done[name] == nch:
                    stores.append((dst, sbuf, bs, kind))
                keep = []
                for (d_, sb_, bs_, k_) in stores:
                    if k_.startswith("rope") and pending:
                        keep.append((d_, sb_, bs_, k_))
                    else:
                        self.dma(d_.ap().rearrange("(c p) t -> p c t", p=128)[:, :, b * 512:(b + 1) * 512], sb_, reads=[bs_])
                stores = keep
            while pending:
                pending.pop(0)()
            for (d_, sb_, bs_, k_) in stores:
                self.dma(d_.ap().rearrange("(c p) t -> p c t", p=128)[:, :, b * 512:(b + 1) * 512], sb_, reads=[bs_])
            do_v(b)

    def phase_attn(self, layer):
        self.reset_arena()
        P = self.P
        WOUT = self.alloc(12 * 1024, BF16).rearrange("p (c n) -> p c n", c=12)
        bWOUT = Buf()
        self.dma(WOUT, self.w_out[layer].ap().rearrange("(c p) n -> p c n", p=128), writes=[bWOUT], q="pool")
        ones_bf = self.alloc(128, BF16)
        bones = Buf()
        self.memset("pool", ones_bf, 1.0, writes=[bones])
        MKT = self.alloc(4 * 256, BF16).rearrange("p (c t) -> p c t", c=4)
        MVs = self.alloc(2 * 512, BF16).rearrange("p (c n) -> p c n", c=2)
        bMK, bMV = Buf(), Buf()
        self.dma(MKT, self.MK_T.ap()[layer].rearrange("(c p) t -> p c t", p=128), writes=[bMK])
        self.dma(MVs, self.MV.ap()[layer].rearrange("(c p) n -> p c n", p=128), writes=[bMV])
        QM = self.alloc(4 * 512, BF16).rearrange("p (c t) -> p c t", c=4)
        G = self.alloc(12 * 512, BF16).rearrange("p (c t) -> p c t", c=12)
        YT = self.alloc(12 * 512, BF16).rearrange("p (c t) -> p c t", c=12)
        bQM, bG = Buf(), Buf()
        bYT = [Buf() for _ in range(12)]
        nxb = 2 if layer == 0 else 1
        XR = [(self.alloc(1024), Buf()) for _ in range(nxb)]
        OUTT = [(self.alloc(1024), Buf()) for _ in range(nxb)]
        PT = [(self.alloc(640, BF16), Buf()) for _ in range(6)]
        RD = [(self.alloc(512), Buf()) for _ in range(2)]
        TN = [(self.alloc(512), Buf()) for _ in range(2)]
        st = {"s": 0, "pt": 0, "o": 0, "rd": 0, "x": 0}

        SB = [0, 1, 2, 3] if layer == 0 else [5]
        OB = [4, 5, 6, 7] if layer == 0 else [6, 7]

        def sbank():
            i = SB[st["s"] % len(SB)]
            st["s"] += 1
            return self.bank[i], self.bb[i]

        def obank(n=1):
            if n == 2 and st["o"] % 2 == 1:
                st["o"] += 1
            r = []
            for _ in range(n):
                i = OB[st["o"] % len(OB)]
                st["o"] += 1
                r.append((self.bank[i], self.bb[i]))
            return r

        def ptbuf():
            i = st["pt"] % 6
            st["pt"] += 1
            return PT[i]

        def normalize(ob, bob, o_rows, d_rows, ci, add_scalar=None, mode="act", yeng="dve"):
            i = st["rd"] % 2
            st["rd"] += 1
            rd, brd = RD[i]
            tn, btn = TN[i]
            o0, o1 = o_rows
            d0, d1 = d_rows
            if mode == "act":
                if add_scalar is not None:
                    self.act(rd[d0:d1, :], ob[d0:d1, :], AF.Ln, reads=[bob, add_scalar[1]], writes=[brd], bias=add_scalar[0])
                else:
                    self.act(rd[d0:d1, :], ob[d0:d1, :], AF.Ln, reads=[bob], writes=[brd])
                self.act(rd[d0:d1, :], rd[d0:d1, :], AF.Exp, reads=[brd], writes=[brd], scale=-1.0)
            else:
                if add_scalar is not None:
                    self.ts("dve", rd[d0:d1, :], ob[d0:d1, :], add_scalar[0], ALU.add, reads=[bob, add_scalar[1]], writes=[brd])
                    self.recip(rd[d0:d1, :], rd[d0:d1, :], reads=[brd], writes=[brd])
                else:
                    self.recip(rd[d0:d1, :], ob[d0:d1, :], reads=[bob], writes=[brd])
            self.tt("dve", tn[o0:o1, :], ob[o0:o1, :], rd[d0:d1, :], ALU.mult, reads=[bob, brd], writes=[btn])
            self.tt(yeng, YT[o0:o1, ci, :], tn[o0:o1, :], G[o0:o1, ci, :], ALU.mult, reads=[btn, bG], writes=[bYT[ci]])

        def mem_attn(b):
            sc = 128.0 ** -0.5
            for hm in range(4):
                (ob, bob), (db, bdb) = obank(2)
                pts = []
                for mt in range(2):
                    sb, bsb = sbank()
                    self.mm(sb, MKT[:, hm, mt * 128:(mt + 1) * 128], QM[:, hm, :], True, True, reads=[bMK, bQM], writes=[bsb])
                    pt, bpt = ptbuf()
                    self.act(pt[:, 0:512], sb, AF.Exp, reads=[bsb], writes=[bpt], scale=sc)
                    pts.append((pt, bpt))
                for mt in range(2):
                    pt, bpt = pts[mt]
                    self.mm(ob, MVs[:, mt, hm * 128:(hm + 1) * 128], pt[:, 0:512], mt == 0, mt == 1, reads=[bMV, bpt], writes=[bob])
                for mt in range(2):
                    pt, bpt = pts[mt]
                    self.mm(db, ones_bf, pt[:, 0:512], mt == 0, mt == 1, reads=[bones, bpt], writes=[bdb])
                i = st["rd"] % 2
                st["rd"] += 1
                rd, brd = RD[i]
                tn, btn = TN[i]
                self.act(rd, db, AF.Ln, reads=[bdb], writes=[brd])
                self.act(rd, rd, AF.Exp, reads=[brd], writes=[brd], scale=-1.0)
                self.tt("dve", tn, ob, rd, ALU.mult, reads=[bob, brd], writes=[btn])
                self.tt("dve" if layer == 0 else os.environ.get("YENG_C", "pool"), YT[:, 8 + hm, :], tn, G[:, 8 + hm, :], ALU.mult,
                        reads=[btn, bG], writes=[bYT[8 + hm]])

        def out_proj(b):
            for jt in range(4):
                t = b * 4 + jt
                xr, bxr = XR[st["x"] % nxb]
                ot, bot = OUTT[st["x"] % nxb]
                st["x"] += 1
                src = self.x if layer == 0 else self.X1
                self.dma(xr, src.ap()[t * 128:(t + 1) * 128, :], writes=[bxr])
                obs = obank(2)
                for nh in range(2):
                    ob, bob = obs[nh]
                    for cc in range(12):
                        self.mm(ob, YT[:, cc, jt * 128:(jt + 1) * 128], WOUT[:, cc, nh * 512:(nh + 1) * 512],
                                cc == 0, cc == 11, reads=[bYT[cc], bWOUT], writes=[bob])
                    self.tt("dve", ot[:, nh * 512:(nh + 1) * 512], ob, xr[:, nh * 512:(nh + 1) * 512], ALU.add,
                            reads=[bob, bxr], writes=[bot])
                if layer == 0:
                    self.dma(self.X1.ap()[t * 128:(t + 1) * 128, :], ot, reads=[bot])
                else:
                    self.act(fjunk, ot, AF.Square, reads=[bot], writes=[bfj, bfs], accum_out=fstat[:, 0:1])
                    self.ts("dve", fstat[:, 1:2], fstat[:, 0:1], 1.0 / D, ALU.mult, reads=[bfs], writes=[bfs], s2=EPS, op1=ALU.add)
                    if os.environ.get("LN_STAT", "0") == "1":
                        self.act(fstat[:, 2:3], fstat[:, 1:2], AF.Ln, reads=[bfs], writes=[bfs])
                        self.act(fstat[:, 3:4], fstat[:, 2:3], AF.Exp, reads=[bfs], writes=[bfs], scale=-0.5)
                    else:
                        self.act(fstat[:, 2:3], fstat[:, 1:2], AF.Sqrt, reads=[bfs], writes=[bfs])
                        self.recip(fstat[:, 3:4], fstat[:, 2:3], reads=[bfs], writes=[bfs])
                    P.add("dve", lambda e, ot=ot: e.scalar_tensor_tensor(out=ot, in0=ot, scalar=fstat[:, 3:4], in1=FG,
                                                                       op0=ALU.mult, op1=ALU.mult),
                          reads=[bot, bfs, bFG], writes=[bot])
                    self.dma(self.out.ap()[t * 128:(t + 1) * 128, :], ot, reads=[bot])

        if layer == 0:
            self.attn_layer0(locals())
        else:
            FG = self.alloc(1024)
            bFG = Buf()
            self.dma(FG, self.fgain.ap().broadcast_to([128, D]), writes=[bFG])
            fjunk = self.alloc(1024, BF16)
            fstat = self.alloc(4)
            bfj, bfs = Buf(), Buf()
            self.attn_layer1(locals())

    def attn_layer0(self, L):
        P = self.P
        QM, G, YT, bQM, bG, bYT = L["QM"], L["G"], L["YT"], L["bQM"], L["bG"], L["bYT"]
        sbank, obank, ptbuf, normalize, mem_attn, out_proj = (L["sbank"], L["obank"], L["ptbuf"], L["normalize"],
                                                              L["mem_attn"], L["out_proj"])
        KA = self.alloc(2 * S, BF16).rearrange("p (j t) -> p j t", j=2)
        KB = self.alloc(2 * S, BF16).rearrange("p (j t) -> p j t", j=2)
        VA = self.alloc(NT * 768, BF16).rearrange("p (t n) -> p t n", t=NT)
        bKA, bKB, bVA = Buf(), Buf(), Buf()
        self.dma(KA, self.KA_T.ap().rearrange("(j p) t -> p j t", p=128), writes=[bKA])
        for a in range(0, NT, 8):
            self.dma(VA[:, a:a + 8, :], self.VAB.ap().rearrange("(t p) n -> p t n", p=128)[:, a:a + 8, :], writes=[bVA])
        self.dma(KB, self.KB_T.ap().rearrange("(j p) t -> p j t", p=128), writes=[bKB])
        QA = self.alloc(4 * 2 * 512, BF16).rearrange("p (c e t) -> p c e t", c=4, e=2)
        QB = self.alloc(4 * 2 * 512, BF16).rearrange("p (c e t) -> p c e t", c=4, e=2)
        bQA, bQB = Buf(), Buf()
        for (qz, bq) in ((QA, bQA), (QB, bQB)):
            self.memset("pool", qz[64:128, :, 0, :], 0.0, writes=[bq])
            self.memset("pool", qz[0:64, :, 1, :], 0.0, writes=[bq])
        PTF = [(self.alloc(384), Buf()) for _ in range(2)]
        EB = self.alloc(8 * 384, BF16).rearrange("p (h n) -> p h n", h=8)
        bEB = Buf()
        oh1 = self.alloc(640)
        erb = self.alloc(8)
        esink = self.alloc(8)
        boh, berb, bes = Buf(), Buf(), Buf()
        self.dma(oh1[0:32, :], self.c_oh1.ap(), writes=[boh])
        self.dma(erb[0:32, :], self.relb.ap(), writes=[berb])
        self.dma(esink, self.sink.ap().broadcast_to([128, 8]), writes=[bes])
        self.act(erb[0:32, :], erb[0:32, :], AF.Exp, reads=[berb], writes=[berb])
        self.act(esink, esink, AF.Exp, reads=[bes], writes=[bes])
        ps6 = self.PS[:, 0:3072]
        for g in range(3):
            for qq in range(128):
                u0 = (g - 1) * 128 - qq + 256
                idx = g * 128 + qq
                bk = idx // 64
                self.mm(ps6[:, idx * 8:(idx + 1) * 8], oh1[0:32, u0:u0 + 128], erb[0:32, 0:8], True, True,
                        reads=[boh, berb], writes=[self.bb[bk]])
        ps6v = ps6.rearrange("p (n h) -> p n h", h=8)
        for h in range(8):
            self.cp("dve" if h % 2 == 0 else "act", EB[:, h, :], ps6v[:, :, h], reads=self.bb[0:6], writes=[bEB])

        def normalize0(*a, **k):
            return normalize(*a, **k)

        def load_block(b, what):
            sl = slice(b * 512, (b + 1) * 512)
            if what == "QA":
                v = self.QA_T.ap().rearrange("(c p) t -> p c t", p=128)
                self.dma(QA[0:64, :, 0, :], v[0:64, :, sl], writes=[bQA])
                self.dma(QA[64:128, :, 1, :], v[64:128, :, sl], writes=[bQA])
            elif what == "QB":
                v = self.QB_T.ap().rearrange("(c p) t -> p c t", p=128)
                self.dma(QB[0:64, :, 0, :], v[0:64, :, sl], writes=[bQB])
                self.dma(QB[64:128, :, 1, :], v[64:128, :, sl], writes=[bQB])
            elif what == "QM":
                self.dma(QM, self.QM_T.ap().rearrange("(c p) t -> p c t", p=128)[:, :, sl], writes=[bQM])
            else:
                self.dma(G, self.G_T.ap().rearrange("(c p) t -> p c t", p=128)[:, :, sl], writes=[bG])

        for w in ("QA", "G", "QB", "QM"):
            load_block(0, w)
        LA = 2
        for b in range(NB):
            for c in range(4):
                j = c // 2
                obs = obank(2)
                units = [(kt, hh) for kt in range(NT) for hh in range(2)]
                sbs = {}

                def qk(u):
                    kt, hh = u
                    sb, bsb = sbank()
                    self.mm(sb, KA[:, j, kt * 128:(kt + 1) * 128], QA[:, c, hh, :], True, True,
                            reads=[bKA, bQA], writes=[bsb])
                    sbs[u] = (sb, bsb)

                for u in units[:LA]:
                    qk(u)
                for idx, u in enumerate(units):
                    if idx + LA < len(units):
                        qk(units[idx + LA])
                    kt, hh = u
                    sb, bsb = sbs.pop(u)
                    pt, bpt = ptbuf()
                    self.act(pt[:, 0:512], sb, AF.Exp, reads=[bsb], writes=[bpt], scale=0.125)
                    ob, bob = obs[hh]
                    v0 = j * 192 + hh * 64
                    self.mm(ob, VA[:, kt, v0:v0 + 128], pt[:, 0:512], kt == 0, kt == NT - 1, reads=[bVA, bpt], writes=[bob])
                for hh in range(2):
                    ob, bob = obs[hh]
                    normalize(ob, bob, (hh * 64, hh * 64 + 64), ((1 - hh) * 64, (1 - hh) * 64 + 64), c, mode="dve", yeng="pool")
            if b + 1 < NB:
                load_block(b + 1, "QA")
            bunits = [(h, ql) for h in range(8) for ql in range(4)]
            bsbs, bobs = {}, {}

            def b_stage1(u):
                h, ql = u
                c, hh, j = h // 2, h % 2, h // 4
                i = b * 4 + ql
                gs = [g for g in range(3) if 0 <= i + g - 1 < NT]
                sb, bsb = sbank()
                for g in gs:
                    kt = i + g - 1
                    self.mm(sb[:, g * 128:(g + 1) * 128], KB[:, j, kt * 128:(kt + 1) * 128],
                            QB[:, c, hh, ql * 128:(ql + 1) * 128], True, True, reads=[bKB, bQB], writes=[bsb])
                bsbs[u] = (sb, bsb, gs)

            def b_stage2(u, n):
                h, ql = u
                c, hh, j = h // 2, h % 2, h // 4
                r0, r1 = hh * 64, hh * 64 + 64
                i = b * 4 + ql
                if ql == 0:
                    bobs[h] = obank(1)[0]
                ob, bob = bobs[h]
                sb, bsb, gs = bsbs.pop(u)
                c0, c1 = gs[0] * 128, (gs[-1] + 1) * 128
                pf, bpf = PTF[n % 2]
                self.act(pf[:, c0:c1], sb[:, c0:c1], AF.Exp, reads=[bsb], writes=[bpf], scale=0.125)
                pt, bpt = ptbuf()
                self.tt("dve", pt[:, c0:c1], pf[:, c0:c1], EB[:, h, c0:c1], ALU.mult, reads=[bpf, bEB], writes=[bpt])
                v0 = (2 + j) * 192 + hh * 64
                for g in gs:
                    kt = i + g - 1
                    self.mm(ob[:, ql * 128:(ql + 1) * 128], VA[:, kt, v0:v0 + 128], pt[:, g * 128:(g + 1) * 128],
                            g == gs[0], g == gs[-1], reads=[bVA, bpt], writes=[bob])
                if ql == 3:
                    d0 = (1 - hh) * 64
                    pend.append((n + 2, lambda: normalize(ob, bob, (r0, r1), (d0, d0 + 64), 4 + c,
                                                          add_scalar=(esink[d0:d0 + 64, h:h + 1], bes))))

            LB = 3
            pend = []
            for u in bunits[:LB]:
                b_stage1(u)
            for n, u in enumerate(bunits):
                if n + LB < len(bunits):
                    b_stage1(bunits[n + LB])
                b_stage2(u, n)
                while pend and pend[0][0] <= n:
                    pend.pop(0)[1]()
            while pend:
                pend.pop(0)[1]()
            if b + 1 < NB:
                load_block(b + 1, "QB")
            mem_attn(b)
            if b + 1 < NB:
                load_block(b + 1, "QM")
            out_proj(b)
            if b + 1 < NB:
                load_block(b + 1, "G")

    def attn_layer1(self, L):
        P = self.P
        QM, G, YT, bQM, bG, bYT = L["QM"], L["G"], L["YT"], L["bQM"], L["bG"], L["bYT"]
        sbank, obank, ptbuf, normalize, mem_attn, out_proj = (L["sbank"], L["obank"], L["ptbuf"], L["normalize"],
                                                              L["mem_attn"], L["out_proj"])
        st = L["st"]
        E = self.alloc(16 * 9 * 128, BF16).rearrange("p (h t q) -> p h t q", h=16, t=9)
        bE = Buf()
        save = self.aoff
        erT = self.alloc(240)
        berT = Buf()
        self.dma(erT[0:31, :], self.rpbT.ap(), writes=[berT])
        self.act(erT[0:31, :], erT[0:31, :], AF.Exp, reads=[berT], writes=[berT])
        self.memset("pool", E.rearrange("p h t q -> p (h t q)"), 0.0, writes=[bE])
        ohp = [(self.alloc(1024), Buf()) for _ in range(2)]
        tiles = [(-3, False), (-2, False), (-2, True), (-1, False), (0, False), (1, False), (2, True), (2, False), (3, False)]
        ps4 = self.PS[:, 0:2048].rearrange("p (q n) -> p q n", n=256)
        cnt = 0
        for r in range(8):
            oh, boh = ohp[r % 2]
            self.dma(oh[0:31, :], self.c_ohc.ap()[:, r * 1024:(r + 1) * 1024], writes=[boh])
            for i in range(8):
                self.mm(self.PS[:, i * 256:i * 256 + 240], oh[0:31, i * 128:(i + 1) * 128], erT[0:31, 0:240], True, True,
                        reads=[boh, berT], writes=[self.bb[i // 2]])
            for ti, (c, masked) in enumerate(tiles):
                for kl in range(2):
                    for ql in range(2):
                        dkr = 2 * c + kl - ql
                        if abs(dkr) > 7:
                            continue
                        if masked and not (-4 <= dkr <= 3):
                            continue
                        dr = dkr + 7
                        src = ps4[kl * 64:(kl + 1) * 64, :, 0:240].rearrange("p q (h r) -> p h q r", r=15)[:, :, :, dr]
                        dst = E[kl * 64:(kl + 1) * 64, :, ti, ql * 64 + r * 8:ql * 64 + r * 8 + 8]
                        self.cp("dve" if cnt % 2 == 0 else "act", dst, src, reads=self.bb[0:4], writes=[bE])
                        cnt += 1
        self.P.barrier()
        self.aoff = save
        R = 12
        KR = self.alloc(8 * R * 128, BF16).rearrange("p (c t) -> p c t", c=8)
        VR = self.alloc(R * 1536, BF16).rearrange("p (s n) -> p s n", s=R)
        bKR = [Buf() for _ in range(R)]
        bVR = [Buf() for _ in range(R)]
        QC = self.alloc(8 * 2 * 512, BF16).rearrange("p (c e t) -> p c e t", c=8, e=2)
        bQC = Buf()
        self.memset("pool", QC[64:128, :, 0, :], 0.0, writes=[bQC])
        self.memset("pool", QC[0:64, :, 1, :], 0.0, writes=[bQC])
        sslot = [Buf() for _ in range(20)]
        PTF = [(self.alloc(640), Buf()) for _ in range(2)]
        kview = self.KC_T.ap().rearrange("(c p) t -> p c t", p=128)

        def load_tile(kt):
            s_ = kt % R
            self.dma(KR[:, :, s_ * 128:(s_ + 1) * 128], kview[:, :, kt * 128:(kt + 1) * 128], writes=[bKR[s_]])
            self.dma(VR[:, s_, :], self.VC.ap()[kt * 128:(kt + 1) * 128, :], writes=[bVR[s_]])

        def load_block(b, what):
            sl = slice(b * 512, (b + 1) * 512)
            if what == "QC":
                v = self.QC_T.ap().rearrange("(c p) t -> p c t", p=128)
                self.dma(QC[0:64, :, 0, :], v[0:64, :, sl], writes=[bQC])
                self.dma(QC[64:128, :, 1, :], v[64:128, :, sl], writes=[bQC])
            elif what == "QM":
                self.dma(QM, self.QM_T.ap().rearrange("(c p) t -> p c t", p=128)[:, :, sl], writes=[bQM])
            else:
                self.dma(G, self.G_T.ap().rearrange("(c p) t -> p c t", p=128)[:, :, sl], writes=[bG])

        def window(qt):
            if qt == 0:
                return [(0, 4), (1, 5), (2, 7), (3, 8)]
            if qt == 1:
                return [(0, 3), (1, 4), (2, 5), (3, 7)]
            if qt == 30:
                return [(28, 1), (29, 3), (30, 4), (31, 5)]
            if qt == 31:
                return [(28, 0), (29, 1), (30, 3), (31, 4)]
            return [(qt - 2 + i, 2 + i) for i in range(5)]

        for w in ("QC", "G", "QM"):
            load_block(0, w)
        for kt in range(0, 6):
            load_tile(kt)
        for b in range(NB):
            if b + 1 < NB:
                for kt in range(4 * b + 6, min(4 * b + 10, NT)):
                    load_tile(kt)
            cunits = [(h, ql) for h in range(16) for ql in range(4)]
            cst, cobs = {}, {}

            def c_stage1(u, n):
                h, ql = u
                c, hh = h // 2, h % 2
                win = window(b * 4 + ql)
                base = (n % 4) * 5
                for i, (kt, ti) in enumerate(win):
                    s_ = kt % R
                    col = (base + i) * 128
                    self.mm(self.PS[:, col:col + 128], KR[:, c, s_ * 128:(s_ + 1) * 128],
                            QC[:, c, hh, ql * 128:(ql + 1) * 128], True, True,
                            reads=[bKR[s_], bQC], writes=[sslot[base + i]])
                cst[u] = (win, base)

            def c_stage2(u, n):
                h, ql = u
                c, hh = h // 2, h % 2
                r0, r1 = hh * 64, hh * 64 + 64
                if ql == 0:
                    cobs[h] = obank(1)[0]
                ob, bob = cobs[h]
                win, base = cst.pop(u)
                nw = len(win)
                sb2 = self.PS[:, base * 128:(base + nw) * 128]
                pf, bpf = PTF[n % 2]
                self.act(pf[:, 0:nw * 128], sb2, AF.Exp, reads=sslot[base:base + nw], writes=[bpf], scale=0.125)
                pt, bpt = ptbuf()
                i0 = 0
                while i0 < nw:
                    i1 = i0
                    while i1 + 1 < nw and win[i1 + 1][1] == win[i1][1] + 1:
                        i1 += 1
                    t0, t1 = win[i0][1], win[i1][1]
                    ev = E[:, h, t0:t1 + 1, :].rearrange("p t q -> p (t q)")
                    self.tt("dve", pt[:, i0 * 128:(i1 + 1) * 128], pf[:, i0 * 128:(i1 + 1) * 128], ev, ALU.mult,
                            reads=[bpf, bE], writes=[bpt])
                    i0 = i1 + 1
                v0 = c * 192 + hh * 64
                for i, (kt, ti) in enumerate(win):
                    s_ = kt % R
                    self.mm(ob[:, ql * 128:(ql + 1) * 128], VR[:, s_, v0:v0 + 128], pt[:, i * 128:(i + 1) * 128],
                            i == 0, i == nw - 1, reads=[bVR[s_], bpt], writes=[bob])
                if ql == 3:
                    d0 = (1 - hh) * 64
                    pend.append((n + int(os.environ.get("DEFER_C", "0")),
                                 lambda: normalize(ob, bob, (r0, r1), (d0, d0 + 64), c, yeng=os.environ.get("YENG_C", "pool"))))

            LC = 3
            pend = []
            for n, u in enumerate(cunits[:LC]):
                c_stage1(u, n)
            for n, u in enumerate(cunits):
                if n + LC < len(cunits):
                    c_stage1(cunits[n + LC], n + LC)
                c_stage2(u, n)
                while pend and pend[0][0] <= n:
                    pend.pop(0)[1]()
            while pend:
                pend.pop(0)[1]()
            if b + 1 < NB:
                load_block(b + 1, "QC")
            mem_attn(b)
            if b + 1 < NB:
                load_block(b + 1, "QM")
            out_proj(b)
            if b + 1 < NB:
                load_block(b + 1, "G")


def prep_inputs(inputs):
    f = lambda a: np.ascontiguousarray(np.asarray(a, dtype=np.float32))
    x = f(inputs["x"])
    mem = f(inputs["mem"])
    ng = f(inputs["norm_gain"])
    mg = f(inputs["mem_norm_gain"])
    gains = np.concatenate([ng[0].reshape(8, 128).T, ng[1].reshape(8, 128).T, mg.reshape(8, 128).T], axis=1)
    qkg = np.stack([np.tile(f(inputs["q_norm_a"])[0], 2), np.tile(f(inputs["k_norm_a"])[0], 2)], axis=1)
    shared = {
        "w_in_even": f(inputs["w_in_even"])[0], "w_in_odd": f(inputs["w_in_odd"])[0],
        "w_out_even": f(inputs["w_out_even"])[0], "w_out_odd": f(inputs["w_out_odd"])[0],
        "w_mem_kv": f(inputs["w_mem_kv"]),
        "gains_pp": f(gains), "qk_gain": f(qkg),
        "final_gain": f(inputs["final_norm_gain"]).reshape(1, D),
        "sink_b": f(inputs["sink_b"]).reshape(1, 8),
        "rel_bias": f(inputs["rel_bias"]),
        "rpbT": f(np.transpose(f(inputs["rpb_c"])[0], (2, 0, 1)).reshape(31, 240)),
    }
    shared.update(host_consts())
    in_maps = []
    for b in range(8):
        m = dict(shared)
        m["x"] = x[b]
        m["mem"] = mem[b]
        in_maps.append(m)
    return in_maps


def kernel(**inputs):
    bld = Builder()
    nc = bld.build()
    in_maps = prep_inputs(inputs)
    res = run_bass_kernel_spmd(nc, in_maps, core_ids=list(range(8)))
    return np.stack([np.asarray(r["out"]) for r in res.results], axis=0).astype(np.float32)
```

```python
from contextlib import ExitStack
import math
import numpy as np
import concourse.bass as bass
import concourse.mybir as mybir
from concourse.bass_utils import run_bass_kernel_spmd

F32 = mybir.dt.float32
BF16 = mybir.dt.bfloat16
ALU = mybir.AluOpType
AF = mybir.ActivationFunctionType
AX = mybir.AxisListType

ENGS = ("pe", "act", "dve", "pool", "sp")
DMA_SLOTS = 12

S = 4096
D = 1024
NT = 32
NB = 8
EPS = 1e-6


class Buf:
    __slots__ = ("w", "r", "name")

    def __init__(self, name=""):
        self.w = None
        self.r = []
        self.name = name


class Op:
    __slots__ = ("eng", "fn", "deps", "needs_inc", "inc_val", "dma", "slot", "slot_val", "id")

    def __init__(self, eng, fn, dma):
        self.eng = eng
        self.fn = fn
        self.dma = dma
        self.deps = set()
        self.needs_inc = False
        self.inc_val = None
        self.slot = None
        self.slot_val = None
        self.id = None


class Prog:
    def __init__(self, nc):
        self.nc = nc
        self.ops = []
        self.dma_count = {e: 0 for e in ENGS}
        self.dma_hist = {e: [] for e in ENGS}
        self.last_op = {e: None for e in ENGS}

    def add(self, eng, fn, reads=(), writes=(), dma=False):
        op = Op(eng, fn, dma)
        op.id = len(self.ops)
        deps = set()
        for b in reads:
            if b.w is not None:
                deps.add(b.w)
        for b in writes:
            if b.w is not None:
                deps.add(b.w)
            for r in b.r:
                deps.add(r)
        for b in reads:
            b.r.append(op)
        for b in writes:
            b.w = op
            b.r = []
        for d in deps:
            if d is op:
                continue
            if (not d.dma) and (not dma) and d.eng == "pe" and eng == "pe":
                continue
            op.deps.add(d)
            if not d.dma:
                d.needs_inc = True
        if dma:
            n = self.dma_count[eng]
            op.slot = n % DMA_SLOTS
            op.slot_val = 16 * (n // DMA_SLOTS + 1)
            hist = self.dma_hist[eng]
            if n >= DMA_SLOTS:
                op.deps.add(hist[n - DMA_SLOTS])
            hist.append(op)
            self.dma_count[eng] = n + 1
        else:
            self.last_op[eng] = op
        self.ops.append(op)
        return op

    def barrier(self):
        lasts = [self.last_op[e] for e in ENGS if self.last_op[e] is not None]
        dmas = []
        for e in ENGS:
            dmas += self.dma_hist[e][-DMA_SLOTS:]
        for e in ENGS:
            op = Op(e, None, False)
            op.id = len(self.ops)
            for d in lasts:
                if d.eng != e:
                    op.deps.add(d)
                    d.needs_inc = True
            for d in dmas:
                op.deps.add(d)
            self.ops.append(op)

    def emit(self):
        nc = self.nc
        with ExitStack() as st:
            sem = {e: st.enter_context(nc.semaphore("c_" + e)) for e in ENGS}
            dsem = {}
            for e in ENGS:
                if self.dma_count[e] > 0:
                    dsem[e] = [st.enter_context(nc.semaphore("d_%s_%d" % (e, i))) for i in range(DMA_SLOTS)]
            cnt = {e: 0 for e in ENGS}
            for op in self.ops:
                if op.dma or op.fn is None:
                    continue
                if op.needs_inc:
                    cnt[op.eng] += 1
                    op.inc_val = cnt[op.eng]
            per_eng = {e: [] for e in ENGS}
            seen = {e: {} for e in ENGS}
            for op in self.ops:
                waits = {}
                for d in op.deps:
                    if d.dma:
                        key = ("d", d.eng, d.slot)
                        val = d.slot_val
                    else:
                        key = ("c", d.eng)
                        val = d.inc_val
                    if waits.get(key, 0) < val:
                        waits[key] = val
                wl = []
                s = seen[op.eng]
                for key, val in waits.items():
                    if s.get(key, 0) >= val:
                        continue
                    s[key] = val
                    wl.append((key, val))
                per_eng[op.eng].append((op, wl))
            block = st.enter_context(nc.Block())

            def run(engname, e):
                for op, wl in per_eng[engname]:
                    for key, val in wl:
                        if key[0] == "d":
                            e.wait_ge(dsem[key[1]][key[2]], val)
                        else:
                            e.wait_ge(sem[key[1]], val)
                    if op.fn is None:
                        continue
                    ins = op.fn(e)
                    if op.dma:
                        ins.then_inc(dsem[engname][op.slot], 16)
                    elif op.needs_inc:
                        ins.then_inc(sem[engname], 1)

            @block.tensor
            def _(e):
                run("pe", e)

            @block.scalar
            def _(e):
                run("act", e)

            @block.vector
            def _(e):
                run("dve", e)

            @block.gpsimd
            def _(e):
                run("pool", e)

            @block.sync
            def _(e):
                run("sp", e)


def _t5_bucket(rel):
    nb = 16
    max_exact = 8
    ret = np.where(rel > 0, nb, 0)
    n = np.abs(rel)
    nf = np.maximum(n, 1).astype(np.float32)
    large = max_exact + (np.log(nf / np.float32(max_exact)) / np.float32(math.log(128 / max_exact))
                         * np.float32(nb - max_exact)).astype(np.int32)
    large = np.minimum(large, nb - 1)
    return ret + np.where(n < max_exact, n, large)


def host_consts():
    c = {}
    c["ident"] = np.eye(128, dtype=np.float32)
    R = np.zeros((128, 128), np.float32)
    for p in range(128):
        sub = p % 32
        if sub < 16:
            R[p, p + 16] = -1.0
        else:
            R[p, p - 16] = 1.0
    c["rotT"] = np.ascontiguousarray(R.T)
    bd = np.zeros((128, 128), np.float32)
    bd[0:64, 0:64] = 1.0
    bd[64:128, 64:128] = 1.0
    c["bd"] = bd
    t = np.arange(S)
    row = (t // 64).astype(np.float32)
    col = (t % 64).astype(np.float32)
    freqs = np.power(np.float32(10000.0), -np.arange(16, dtype=np.float32) / np.float32(16)).astype(np.float32)
    cosT = np.zeros((128, S), np.float32)
    sinT = np.zeros((128, S), np.float32)
    for p in range(128):
        dh = p % 64
        pos = row if dh < 32 else col
        ang = (pos * freqs[dh % 16]).astype(np.float32)
        cosT[p] = np.cos(ang)
        sinT[p] = np.sin(ang)
    c["cosT"] = cosT
    c["sinT"] = sinT
    rel = np.arange(-256, 384)
    bk = _t5_bucket(rel)
    oh1 = np.zeros((32, 640), np.float32)
    for u, r in enumerate(rel):
        if abs(r) <= 128:
            oh1[bk[u], u] = 1.0
    c["oh1"] = oh1
    ohc = np.zeros((31, 64, 128), np.float32)
    for qc in range(64):
        cs = min(max(qc - 8, 0), 48)
        for kc in range(cs, cs + 16):
            dc = kc - qc + 15
            ohc[dc, qc, kc] = 1.0
            ohc[dc, qc, 64 + kc] = 1.0
    c["ohc"] = ohc.reshape(31, 64 * 128)
    return c


class Builder:
    def __init__(self, debug=None, stop_after=None):
        self.debug = debug or ()
        self.stop_after = stop_after
        self.nc = bass.Bass("TRN2", target_bir_lowering=False)
        self.P = Prog(self.nc)
        self.dram = {}

    def din(self, name, shape, dt=F32):
        t = self.nc.dram_tensor(name, list(shape), dt, kind="ExternalInput")
        self.dram[name] = t
        return t

    def dscratch(self, name, shape, dt):
        kind = "ExternalOutput" if name in self.debug else "Internal"
        t = self.nc.dram_tensor(name, list(shape), dt, kind=kind)
        self.dram[name] = t
        return t

    def reset_arena(self):
        self.aoff = 0

    def alloc(self, ncols, dt=F32):
        nbytes = ncols * (4 if dt == F32 else 2)
        n32 = (nbytes + 3) // 4
        n32 = (n32 + 1) // 2 * 2
        a = self.ARENA[:, self.aoff:self.aoff + n32]
        self.aoff += n32
        assert self.aoff <= self.ARENA_N, "arena overflow %d > %d" % (self.aoff, self.ARENA_N)
        if dt == F32:
            return a[:, 0:ncols]
        return a.bitcast(BF16)[:, 0:ncols]

    def dma(self, out, in_, reads=(), writes=(), q="sp", **kw):
        return self.P.add(q, lambda e: e.dma_start(out=out, in_=in_, **kw), reads, writes, dma=True)

    def mm(self, out, lhsT, rhs, start, stop, reads=(), writes=()):
        return self.P.add("pe", lambda e: e.matmul(out, lhsT=lhsT, rhs=rhs, start=start, stop=stop), reads, writes)

    def tr(self, out, in_, ident, reads=(), writes=()):
        return self.P.add("pe", lambda e: e.transpose(out, in_, ident), reads, writes)

    def act(self, out, in_, func, reads=(), writes=(), **kw):
        return self.P.add("act", lambda e: e.activation(out=out, in_=in_, func=func, **kw), reads, writes)

    def tt(self, eng, out, in0, in1, op, reads=(), writes=()):
        return self.P.add(eng, lambda e: e.tensor_tensor(out=out, in0=in0, in1=in1, op=op), reads, writes)

    def ts(self, eng, out, in0, s1, op0, reads=(), writes=(), s2=None, op1=None):
        if op1 is None:
            return self.P.add(eng, lambda e: e.tensor_scalar(out=out, in0=in0, scalar1=s1, scalar2=None, op0=op0), reads, writes)
        return self.P.add(eng, lambda e: e.tensor_scalar(out=out, in0=in0, scalar1=s1, scalar2=s2, op0=op0, op1=op1), reads, writes)

    def cp(self, eng, out, in_, reads=(), writes=()):
        if eng == "act":
            return self.P.add("act", lambda e: e.copy(out=out, in_=in_), reads, writes)
        return self.P.add(eng, lambda e: e.tensor_copy(out=out, in_=in_), reads, writes)

    def memset(self, eng, ap, val, writes=()):
        return self.P.add(eng, lambda e: e.memset(ap, val), (), writes)

    def recip(self, out, in_, reads=(), writes=()):
        return self.P.add("dve", lambda e: e.reciprocal(out=out, in_=in_), reads, writes)

    def build(self):
        nc = self.nc
        self.x = self.din("x", [S, D])
        self.mem = self.din("mem", [256, D])
        self.w_in = [self.din("w_in_even", [D, 3584]), self.din("w_in_odd", [D, 5120])]
        self.w_out = [self.din("w_out_even", [1536, D]), self.din("w_out_odd", [1536, D])]
        self.w_mem = self.din("w_mem_kv", [2, D, 1024])
        self.gains = self.din("gains_pp", [128, 24])
        self.qkg = self.din("qk_gain", [128, 2])
        self.fgain = self.din("final_gain", [1, D])
        self.sink = self.din("sink_b", [1, 8])
        self.relb = self.din("rel_bias", [32, 8])
        self.rpbT = self.din("rpbT", [31, 240])
        self.c_ident = self.din("ident", [128, 128])
        self.c_rotT = self.din("rotT", [128, 128])
        self.c_bd = self.din("bd", [128, 128])
        self.c_cos = self.din("cosT", [128, S])
        self.c_sin = self.din("sinT", [128, S])
        self.c_oh1 = self.din("oh1", [32, 640])
        self.c_ohc = self.din("ohc", [31, 64 * 128])
        self.out = nc.dram_tensor("out", [S, D], F32, kind="ExternalOutput")
        self.QA_T = self.dscratch("QA_T", [512, S], BF16)
        self.KA_T = self.dscratch("KA_T", [256, S], BF16)
        self.QB_T = self.dscratch("QB_T", [512, S], BF16)
        self.KB_T = self.dscratch("KB_T", [256, S], BF16)
        self.QM_T = self.dscratch("QM_T", [512, S], BF16)
        self.G_T = self.dscratch("G_T", [1536, S], BF16)
        self.VAB = self.dscratch("VAB", [S, 768], BF16)
        self.X1 = self.dscratch("X1", [S, D], F32)
        self.QC_T = self.dscratch("QC_T", [1024, S], BF16)
        self.KC_T = self.dscratch("KC_T", [1024, S], BF16)
        self.VC = self.dscratch("VC", [S, 1536], BF16)
        self.MK_T = self.dscratch("MK_T", [2, 512, 256], BF16)
        self.MV = self.dscratch("MV", [2, 256, 512], BF16)

        with ExitStack() as st:
            self.ARENA_N = 52000
            self.ARENA = st.enter_context(nc.sbuf_tensor("arena", [128, self.ARENA_N], F32))
            self.PS = st.enter_context(nc.psum_tensor("ps", [128, 4096], F32))
            self.bank = [self.PS[:, i * 512:(i + 1) * 512] for i in range(8)]
            self.bb = [Buf("bank%d" % i) for i in range(8)]
            self.phase_mem()
            self.P.barrier()
            for layer in range(2):
                self.phase_proj(layer)
                self.P.barrier()
                if self.stop_after == ("proj", layer):
                    break
                self.phase_attn(layer)
                self.P.barrier()
                if self.stop_after == ("attn", layer):
                    break
            self.P.emit()
        return nc

    def load_consts(self):
        c = {}
        c["ident"] = self.alloc(128)
        c["b_ident"] = Buf()
        self.dma(c["ident"], self.c_ident.ap(), writes=[c["b_ident"]])
        c["gains"] = self.alloc(24)
        c["b_gains"] = Buf()
        self.dma(c["gains"], self.gains.ap(), writes=[c["b_gains"]])
        return c

    def norm_transpose(self, c, xt, bx, ht3, bht, j, gcol, tp_banks, btp, scr):
        junk, bjunk, stat, bstat = scr
        self.act(junk, xt, AF.Square, reads=[bx], writes=[bjunk, bstat], accum_out=stat[:, 0:1])
        self.ts("dve", stat[:, 1:2], stat[:, 0:1], 1.0 / D, ALU.mult, reads=[bstat], writes=[bstat], s2=EPS, op1=ALU.add)
        if False:
            self.act(stat[:, 2:3], stat[:, 1:2], AF.Ln, reads=[bstat], writes=[bstat])
            self.act(stat[:, 3:4], stat[:, 2:3], AF.Exp, reads=[bstat], writes=[bstat], scale=-0.5)
        else:
            self.act(stat[:, 2:3], stat[:, 1:2], AF.Sqrt, reads=[bstat], writes=[bstat])
            self.recip(stat[:, 3:4], stat[:, 2:3], reads=[bstat], writes=[bstat])
        self.ts("dve", xt, xt, stat[:, 3:4], ALU.mult, reads=[bx, bstat], writes=[bx])
        tp = tp_banks
        for kc in range(8):
            self.tr(tp[:, kc * 128:(kc + 1) * 128], xt[:, kc * 128:(kc + 1) * 128], c["ident"],
                    reads=[bx, c["b_ident"]], writes=list(btp))
        g = c["gains"][:, gcol:gcol + 8]
        gb = g.unsqueeze(2).broadcast_to([128, 8, 128])
        tp3 = tp.rearrange("p (a b) -> p a b", a=8)
        self.tt("dve", ht3[:, :, j * 128:(j + 1) * 128], tp3, gb, ALU.mult,
                reads=list(btp) + [c["b_gains"]], writes=[bht])

    def phase_mem(self):
        self.reset_arena()
        c = self.load_consts()
        W = self.alloc(2 * 8 * 1024, BF16).rearrange("p (l k n) -> p l k n", l=2, k=8)
        bW = Buf()
        for l in range(2):
            self.dma(W[:, l], self.w_mem.ap()[l].rearrange("(k p) n -> p k n", p=128), writes=[bW], q="pool")
        ht3 = self.alloc(8 * 256, BF16).rearrange("p (k t) -> p k t", k=8)
        bht = Buf()
        junk = self.alloc(1024, BF16)
        scr = (junk, Buf(), self.alloc(4), Buf())
        tpb = self.PS[:, 0:1024]
        btp = [self.bb[0], self.bb[1]]
        for j in range(2):
            xt = self.alloc(1024)
            bx = Buf()
            self.dma(xt, self.mem.ap()[j * 128:(j + 1) * 128, :], writes=[bx])
            self.norm_transpose(c, xt, bx, ht3, bht, j, 16, tpb, btp, scr)
        stg = self.alloc(4 * 256, BF16).rearrange("p (c t) -> p c t", c=4)
        stv = self.alloc(2 * 512, BF16).rearrange("p (c t) -> p c t", c=2)
        bst, bsv = Buf(), Buf()
        for l in range(2):
            for hm in range(4):
                pb = self.bank[2 + hm % 2]
                bpb = self.bb[2 + hm % 2]
                for kc in range(8):
                    self.mm(pb[:, 0:256], W[:, l, kc, hm * 128:(hm + 1) * 128], ht3[:, kc, :], kc == 0, kc == 7,
                            reads=[bW, bht], writes=[bpb])
                self.cp("dve", stg[:, hm, :], pb[:, 0:256], reads=[bpb], writes=[bst])
            self.dma(self.MK_T.ap()[l].rearrange("(c p) t -> p c t", p=128), stg, reads=[bst])
            for mt in range(2):
                pb = self.bank[4 + mt]
                bpb = self.bb[4 + mt]
                for kc in range(8):
                    self.mm(pb, ht3[:, kc, mt * 128:(mt + 1) * 128], W[:, l, kc, 512:1024], kc == 0, kc == 7,
                            reads=[bW, bht], writes=[bpb])
                self.cp("act", stv[:, mt, :], pb, reads=[bpb], writes=[bsv])
            self.dma(self.MV.ap()[l].rearrange("(c p) n -> p c n", p=128), stv, reads=[bsv])

    def phase_proj(self, layer):
        self.reset_arena()
        c = self.load_consts()
        if layer == 0:
            wl = [(0, 512, 0)]
            wl += [(512, 64, 512), (512, 64, 576), (576, 64, 640), (576, 64, 704)]
            wl += [(768, 512, 768)]
            wl += [(1280, 64, 1280), (1280, 64, 1344), (1344, 64, 1408), (1344, 64, 1472)]
            wl += [(1536, 512, 1536), (2048, 1536, 2048), (640, 128, 3584), (1408, 128, 3712)]
            NC = 3840
            groups = [("qa", 0, 4, self.QA_T, "rope_q"), ("ka", 512, 2, self.KA_T, "rope_k"),
                      ("qb", 768, 4, self.QB_T, "copy"), ("kb", 1280, 2, self.KB_T, "copy"),
                      ("qm", 1536, 4, self.QM_T, "copy"), ("gate", 2048, 12, self.G_T, "silu")]
            vcol, vn = 3584, 256
        else:
            wl = [(0, 2048, 0), (3072, 2048, 2048), (2048, 1024, 4096)]
            NC = 5120
            groups = [("qc", 0, 8, self.QC_T, "copy"), ("kc", 1024, 8, self.KC_T, "copy"),
                      ("qm", 2048, 4, self.QM_T, "copy"), ("gate", 2560, 12, self.G_T, "silu")]
            vcol, vn = 4096, 1024
        W = self.alloc(8 * NC, BF16).rearrange("p (k n) -> p k n", k=8)
        bW = Buf()
        wsrc = self.w_in[layer].ap().rearrange("(k p) n -> p k n", p=128)
        for (s0, n, d0) in wl:
            for a in range(0, n, 1024):
                m = min(1024, n - a)
                self.dma(W[:, :, d0 + a:d0 + a + m], wsrc[:, :, s0 + a:s0 + a + m], writes=[bW], q="pool")
        if layer == 0:
            rotT = self.alloc(128)
            bd = self.alloc(128)
            qkg = self.alloc(2)
            bcst = Buf()
            self.dma(rotT, self.c_rotT.ap(), writes=[bcst])
            self.dma(bd, self.c_bd.ap(), writes=[bcst])
            self.dma(qkg, self.qkg.ap(), writes=[bcst])
            epsT = self.alloc(2)
            self.memset("pool", epsT, EPS, writes=[bcst])
            cs = [(self.alloc(512), self.alloc(512), Buf()) for _ in range(2)]
            ropeT = [(self.alloc(512), self.alloc(512), self.alloc(512), self.alloc(512), Buf(), Buf(), Buf(), Buf())
                     for _ in range(2)]
        XT = [self.alloc(1024) for _ in range(4)]
        bXT = [Buf() for _ in range(4)]
        HT = [self.alloc(8 * 512, BF16).rearrange("p (k t) -> p k t", k=8) for _ in range(2)]
        bHT = [Buf() for _ in range(2)]
        junk = self.alloc(1024, BF16)
        scr = (junk, Buf(), self.alloc(4), Buf())
        tpb = self.PS[:, 0:1024]
        btp = [self.bb[0], self.bb[1]]
        stg = {}
        for (name, col, nch, dst, kind) in groups:
            stg[name] = [(self.alloc(nch * 512, BF16).rearrange("p (c t) -> p c t", c=nch), Buf()) for _ in range(2)]
        if layer == 0:
            VST = [(self.alloc(768, BF16), Buf()) for _ in range(2)]
        else:
            VST = [(self.alloc(1536, BF16), Buf()) for _ in range(2)]
        for (v, bv) in VST:
            self.memset("pool", v, 1.0, writes=[bv])
        pbanks = [self.bank[2], self.bank[3], self.bank[4]]
        bpb = [self.bb[2], self.bb[3], self.bb[4]]
        aux = [self.bank[5], self.bank[6]]
        baux = [self.bb[5], self.bb[6]]
        vbank = self.bank[7]
        bvb = self.bb[7]
        xsrc = self.x if layer == 0 else self.X1
        gcol = 0 if layer == 0 else 8

        def load_x(b):
            for j in range(4):
                t = b * 4 + j
                self.dma(XT[j], xsrc.ap()[t * 128:(t + 1) * 128, :], writes=[bXT[j]])

        def norm_tr(b):
            for j in range(4):
                self.norm_transpose(c, XT[j], bXT[j], HT[b % 2], bHT[b % 2], j, gcol, tpb, btp, scr)

        state = {"pi": 0, "vi": 0, "ri": 0, "vb": 0}
        pending = []

        def do_chunk(b, name, col, ci, kind, sbuf, bs):
            i = state["pi"] % 3
            state["pi"] += 1
            pb, bp = pbanks[i], bpb[i]
            ht, bh = HT[b % 2], bHT[b % 2]
            for kc in range(8):
                self.mm(pb, W[:, kc, col + ci * 128:col + (ci + 1) * 128], ht[:, kc, :], kc == 0, kc == 7,
                        reads=[bW, bh], writes=[bp])
            while pending:
                pending.pop(0)()
            dst = sbuf[:, ci, :]
            if kind == "copy":
                self.cp("act" if (state["pi"] % 2 == 0) else "dve", dst, pb, reads=[bp], writes=[bs])
            elif kind == "silu":
                self.act(dst, pb, AF.Silu, reads=[bp], writes=[bs])
            else:
                gi = 0 if kind == "rope_q" else 1
                cosb, sinb, bcs = cs[b % 2]
                tq, tsq, t1, trs, btq, btsq, bt1, btrs = ropeT[state["ri"] % 2]
                state["ri"] += 1
                self.act(tq, pb, AF.Copy, reads=[bp, bcst], writes=[btq], scale=qkg[:, gi:gi + 1])
                self.act(tsq, pb, AF.Square, reads=[bp], writes=[btsq])

                def tail():
                    self.mm(aux[0], bd, tsq, True, True, reads=[bcst, btsq], writes=[baux[0]])
                    self.mm(aux[1], rotT, tq, True, True, reads=[bcst, btq], writes=[baux[1]])
                    if True:
                        self.act(trs, aux[0], AF.Ln, reads=[baux[0], bcst], writes=[btrs], scale=1.0 / 64, bias=epsT[:, 0:1])
                        self.act(trs, trs, AF.Exp, reads=[btrs], writes=[btrs], scale=-0.5)
                    else:
                        self.ts("dve", trs, aux[0], 1.0 / 64, ALU.mult, reads=[baux[0]], writes=[btrs], s2=EPS, op1=ALU.add)
                        self.act(trs, trs, AF.Sqrt, reads=[btrs], writes=[btrs])
                        self.recip(trs, trs, reads=[btrs], writes=[btrs])
                    self.tt("dve", t1, tq, cosb, ALU.mult, reads=[btq, bcs], writes=[bt1])
                    self.tt("dve", tsq, aux[1], sinb, ALU.mult, reads=[baux[1], bcs, btsq], writes=[btsq])
                    self.tt("dve", t1, t1, tsq, ALU.add, reads=[bt1, btsq], writes=[bt1])
                    self.tt("dve", dst, t1, trs, ALU.mult, reads=[bt1, btrs], writes=[bs])

                pending.append(tail)

        def do_v(b):
            ht, bh = HT[b % 2], bHT[b % 2]
            for j in range(4):
                t = b * 4 + j
                v, bv = VST[state["vi"] % 2]
                state["vi"] += 1
                if layer == 0:
                    for kc in range(8):
                        self.mm(vbank[:, 0:256], ht[:, kc, j * 128:(j + 1) * 128], W[:, kc, vcol:vcol + 256],
                                kc == 0, kc == 7, reads=[bW, bh], writes=[bvb])
                    v3 = v.rearrange("p (g s) -> p g s", g=4)
                    src = vbank[:, 0:256].rearrange("p (g d) -> p g d", g=4)
                    self.cp("dve", v3[:, :, 0:64], src, reads=[bvb], writes=[bv])
                    self.cp("act", v3[:, :, 128:192], src, reads=[bvb], writes=[bv])
                    self.dma(self.VAB.ap()[t * 128:(t + 1) * 128, :], v, reads=[bv])
                else:
                    v3 = v.rearrange("p (g s) -> p g s", g=8)
                    for half in range(2):
                        vb_i = 5 + state["vb"] % 3
                        state["vb"] += 1
                        vbk, bvk = self.bank[vb_i], self.bb[vb_i]
                        for kc in range(8):
                            self.mm(vbk, ht[:, kc, j * 128:(j + 1) * 128],
                                    W[:, kc, vcol + half * 512:vcol + (half + 1) * 512],
                                    kc == 0, kc == 7, reads=[bW, bh], writes=[bvk])
                        src = vbk.rearrange("p (g e d) -> p g e d", g=4, e=2)
                        self.cp("dve", v3[:, half * 4:(half + 1) * 4, 0:64], src[:, :, 0, :], reads=[bvk], writes=[bv])
                        self.cp("act", v3[:, half * 4:(half + 1) * 4, 128:192], src[:, :, 1, :], reads=[bvk], writes=[bv])
                    self.dma(self.VC.ap()[t * 128:(t + 1) * 128, :], v, reads=[bv])

        chunks = []
        for (name, col, nch, dst, kind) in groups:
            for ci in range(nch):
                chunks.append((name, col, ci, nch, dst, kind))
        if layer == 0:
            rope = [ch for ch in chunks if ch[5].startswith("rope")]
            plain = [ch for ch in chunks if not ch[5].startswith("rope")]
            order = []
            while rope or plain:
                if rope:
                    order.append(rope.pop(0))
                for _ in range(3):
                    if plain:
                        order.append(plain.pop(0))
            chunks = order
        load_x(0)
        norm_tr(0)
        for b in range(NB):
            if layer == 0:
                cosb, sinb, bcs = cs[b % 2]
                self.dma(cosb, self.c_cos.ap()[:, b * 512:(b + 1) * 512], writes=[bcs])
                self.dma(sinb, self.c_sin.ap()[:, b * 512:(b + 1) * 512], writes=[bcs])
            if b + 1 < NB:
                load_x(b + 1)
            half_n = len(chunks) // 2
            done = {}
            stores = []
            for idx, (name, col, ci, nch, dst, kind) in enumerate(chunks):
                if idx == half_n and b + 1 < NB:
                    norm_tr(b + 1)
                sbuf, bs = stg[name][b % 2]
                do_chunk(b, name, col, ci, kind, sbuf, bs)
                done[name] = done.get(name, 0) + 1
                if done[name] == nch:
                    stores.append((dst, sbuf, bs, kind))
                keep = []
                for (d_, sb_, bs_, k_) in stores:
                    if k_.startswith("rope") and pending:
                        keep.append((d_, sb_, bs_, k_))
                    else:
                        self.dma(d_.ap().rearrange("(c p) t -> p c t", p=128)[:, :, b * 512:(b + 1) * 512], sb_, reads=[bs_])
                stores = keep
            while pending:
                pending.pop(0)()
            for (d_, sb_, bs_, k_) in stores:
                self.dma(d_.ap().rearrange("(c p) t -> p c t", p=128)[:, :, b * 512:(b + 1) * 512], sb_, reads=[bs_])
            do_v(b)

    def phase_attn(self, layer):
        self.reset_arena()
        P = self.P
        WOUT = self.alloc(12 * 1024, BF16).rearrange("p (c n) -> p c n", c=12)
        bWOUT = Buf()
        self.dma(WOUT, self.w_out[layer].ap().rearrange("(c p) n -> p c n", p=128), writes=[bWOUT], q="pool")
        ones_bf = self.alloc(128, BF16)
        bones = Buf()
        self.memset("pool", ones_bf, 1.0, writes=[bones])
        MKT = self.alloc(4 * 256, BF16).rearrange("p (c t) -> p c t", c=4)
        MVs = self.alloc(2 * 512, BF16).rearrange("p (c n) -> p c n", c=2)
        bMK, bMV = Buf(), Buf()
        self.dma(MKT, self.MK_T.ap()[layer].rearrange("(c p) t -> p c t", p=128), writes=[bMK])
        self.dma(MVs, self.MV.ap()[layer].rearrange("(c p) n -> p c n", p=128), writes=[bMV])
        QM = self.alloc(4 * 512, BF16).rearrange("p (c t) -> p c t", c=4)
        G = self.alloc(12 * 512, BF16).rearrange("p (c t) -> p c t", c=12)
        YT = self.alloc(12 * 512, BF16).rearrange("p (c t) -> p c t", c=12)
        bQM, bG = Buf(), Buf()
        bYT = [Buf() for _ in range(12)]
        nxb = 2 if layer == 0 else 1
        XR = [(self.alloc(1024), Buf()) for _ in range(nxb)]
        OUTT = [(self.alloc(1024), Buf()) for _ in range(nxb)]
        PT = [(self.alloc(640, BF16), Buf()) for _ in range(6)]
        RD = [(self.alloc(512), Buf()) for _ in range(2)]
        TN = [(self.alloc(512), Buf()) for _ in range(2)]
        st = {"s": 0, "pt": 0, "o": 0, "rd": 0, "x": 0}

        SB = [0, 1, 2, 3] if layer == 0 else [5]
        OB = [4, 5, 6, 7] if layer == 0 else [6, 7]

        def sbank():
            i = SB[st["s"] % len(SB)]
            st["s"] += 1
            return self.bank[i], self.bb[i]

        def obank(n=1):
            if n == 2 and st["o"] % 2 == 1:
                st["o"] += 1
            r = []
            for _ in range(n):
                i = OB[st["o"] % len(OB)]
                st["o"] += 1
                r.append((self.bank[i], self.bb[i]))
            return r

        def ptbuf():
            i = st["pt"] % 6
            st["pt"] += 1
            return PT[i]

        def normalize(ob, bob, o_rows, d_rows, ci, add_scalar=None, mode="act", yeng="dve"):
            i = st["rd"] % 2
            st["rd"] += 1
            rd, brd = RD[i]
            tn, btn = TN[i]
            o0, o1 = o_rows
            d0, d1 = d_rows
            if mode == "act":
                if add_scalar is not None:
                    self.act(rd[d0:d1, :], ob[d0:d1, :], AF.Ln, reads=[bob, add_scalar[1]], writes=[brd], bias=add_scalar[0])
                else:
                    self.act(rd[d0:d1, :], ob[d0:d1, :], AF.Ln, reads=[bob], writes=[brd])
                self.act(rd[d0:d1, :], rd[d0:d1, :], AF.Exp, reads=[brd], writes=[brd], scale=-1.0)
            else:
                if add_scalar is not None:
                    self.ts("dve", rd[d0:d1, :], ob[d0:d1, :], add_scalar[0], ALU.add, reads=[bob, add_scalar[1]], writes=[brd])
                    self.recip(rd[d0:d1, :], rd[d0:d1, :], reads=[brd], writes=[brd])
                else:
                    self.recip(rd[d0:d1, :], ob[d0:d1, :], reads=[bob], writes=[brd])
            self.tt("dve", tn[o0:o1, :], ob[o0:o1, :], rd[d0:d1, :], ALU.mult, reads=[bob, brd], writes=[btn])
            self.tt(yeng, YT[o0:o1, ci, :], tn[o0:o1, :], G[o0:o1, ci, :], ALU.mult, reads=[btn, bG], writes=[bYT[ci]])

        def mem_attn(b):
            sc = 128.0 ** -0.5
            for hm in range(4):
                (ob, bob), (db, bdb) = obank(2)
                pts = []
                for mt in range(2):
                    sb, bsb = sbank()
                    self.mm(sb, MKT[:, hm, mt * 128:(mt + 1) * 128], QM[:, hm, :], True, True, reads=[bMK, bQM], writes=[bsb])
                    pt, bpt = ptbuf()
                    self.act(pt[:, 0:512], sb, AF.Exp, reads=[bsb], writes=[bpt], scale=sc)
                    pts.append((pt, bpt))
                for mt in range(2):
                    pt, bpt = pts[mt]
                    self.mm(ob, MVs[:, mt, hm * 128:(hm + 1) * 128], pt[:, 0:512], mt == 0, mt == 1, reads=[bMV, bpt], writes=[bob])
                for mt in range(2):
                    pt, bpt = pts[mt]
                    self.mm(db, ones_bf, pt[:, 0:512], mt == 0, mt == 1, reads=[bones, bpt], writes=[bdb])
                i = st["rd"] % 2
                st["rd"] += 1
                rd, brd = RD[i]
                tn, btn = TN[i]
                self.act(rd, db, AF.Ln, reads=[bdb], writes=[brd])
                self.act(rd, rd, AF.Exp, reads=[brd], writes=[brd], scale=-1.0)
                self.tt("dve", tn, ob, rd, ALU.mult, reads=[bob, brd], writes=[btn])
                self.tt("dve" if layer == 0 else "pool", YT[:, 8 + hm, :], tn, G[:, 8 + hm, :], ALU.mult,
                        reads=[btn, bG], writes=[bYT[8 + hm]])

        def out_proj(b):
            for jt in range(4):
                t = b * 4 + jt
                xr, bxr = XR[st["x"] % nxb]
                ot, bot = OUTT[st["x"] % nxb]
                st["x"] += 1
                src = self.x if layer == 0 else self.X1
                self.dma(xr, src.ap()[t * 128:(t + 1) * 128, :], writes=[bxr])
                obs = obank(2)
                for nh in range(2):
                    ob, bob = obs[nh]
                    for cc in range(12):
                        self.mm(ob, YT[:, cc, jt * 128:(jt + 1) * 128], WOUT[:, cc, nh * 512:(nh + 1) * 512],
                                cc == 0, cc == 11, reads=[bYT[cc], bWOUT], writes=[bob])
                    self.tt("dve", ot[:, nh * 512:(nh + 1) * 512], ob, xr[:, nh * 512:(nh + 1) * 512], ALU.add,
                            reads=[bob, bxr], writes=[bot])
                if layer == 0:
                    self.dma(self.X1.ap()[t * 128:(t + 1) * 128, :], ot, reads=[bot])
                else:
                    self.act(fjunk, ot, AF.Square, reads=[bot], writes=[bfj, bfs], accum_out=fstat[:, 0:1])
                    self.ts("dve", fstat[:, 1:2], fstat[:, 0:1], 1.0 / D, ALU.mult, reads=[bfs], writes=[bfs], s2=EPS, op1=ALU.add)
                    if False:
                        self.act(fstat[:, 2:3], fstat[:, 1:2], AF.Ln, reads=[bfs], writes=[bfs])
                        self.act(fstat[:, 3:4], fstat[:, 2:3], AF.Exp, reads=[bfs], writes=[bfs], scale=-0.5)
                    else:
                        self.act(fstat[:, 2:3], fstat[:, 1:2], AF.Sqrt, reads=[bfs], writes=[bfs])
                        self.recip(fstat[:, 3:4], fstat[:, 2:3], reads=[bfs], writes=[bfs])
                    P.add("dve", lambda e, ot=ot: e.scalar_tensor_tensor(out=ot, in0=ot, scalar=fstat[:, 3:4], in1=FG,
                                                                       op0=ALU.mult, op1=ALU.mult),
                          reads=[bot, bfs, bFG], writes=[bot])
                    self.dma(self.out.ap()[t * 128:(t + 1) * 128, :], ot, reads=[bot])

        if layer == 0:
            self.attn_layer0(locals())
        else:
            FG = self.alloc(1024)
            bFG = Buf()
            self.dma(FG, self.fgain.ap().broadcast_to([128, D]), writes=[bFG])
            fjunk = self.alloc(1024, BF16)
            fstat = self.alloc(4)
            bfj, bfs = Buf(), Buf()
            self.attn_layer1(locals())

    def attn_layer0(self, L):
        P = self.P
        QM, G, YT, bQM, bG, bYT = L["QM"], L["G"], L["YT"], L["bQM"], L["bG"], L["bYT"]
        sbank, obank, ptbuf, normalize, mem_attn, out_proj = (L["sbank"], L["obank"], L["ptbuf"], L["normalize"],
                                                              L["mem_attn"], L["out_proj"])
        KA = self.alloc(2 * S, BF16).rearrange("p (j t) -> p j t", j=2)
        KB = self.alloc(2 * S, BF16).rearrange("p (j t) -> p j t", j=2)
        VA = self.alloc(NT * 768, BF16).rearrange("p (t n) -> p t n", t=NT)
        bKA, bKB, bVA = Buf(), Buf(), Buf()
        self.dma(KA, self.KA_T.ap().rearrange("(j p) t -> p j t", p=128), writes=[bKA])
        for a in range(0, NT, 8):
            self.dma(VA[:, a:a + 8, :], self.VAB.ap().rearrange("(t p) n -> p t n", p=128)[:, a:a + 8, :], writes=[bVA])
        self.dma(KB, self.KB_T.ap().rearrange("(j p) t -> p j t", p=128), writes=[bKB])
        QA = self.alloc(4 * 2 * 512, BF16).rearrange("p (c e t) -> p c e t", c=4, e=2)
        QB = self.alloc(4 * 2 * 512, BF16).rearrange("p (c e t) -> p c e t", c=4, e=2)
        bQA, bQB = Buf(), Buf()
        for (qz, bq) in ((QA, bQA), (QB, bQB)):
            self.memset("pool", qz[64:128, :, 0, :], 0.0, writes=[bq])
            self.memset("pool", qz[0:64, :, 1, :], 0.0, writes=[bq])
        PTF = [(self.alloc(384), Buf()) for _ in range(2)]
        EB = self.alloc(8 * 384, BF16).rearrange("p (h n) -> p h n", h=8)
        bEB = Buf()
        oh1 = self.alloc(640)
        erb = self.alloc(8)
        esink = self.alloc(8)
        boh, berb, bes = Buf(), Buf(), Buf()
        oh1b = self.alloc(640, BF16)
        erbb = self.alloc(8, BF16)
        boh2, berb2 = Buf(), Buf()
        self.dma(oh1[0:32, :], self.c_oh1.ap(), writes=[boh])
        self.dma(erb[0:32, :], self.relb.ap(), writes=[berb])
        self.dma(esink, self.sink.ap().broadcast_to([128, 8]), writes=[bes])
        self.cp("dve", oh1b[0:32, :], oh1[0:32, :], reads=[boh], writes=[boh2])
        self.act(erbb[0:32, :], erb[0:32, :], AF.Exp, reads=[berb], writes=[berb2])
        self.act(esink, esink, AF.Exp, reads=[bes], writes=[bes])
        ps6 = self.PS[:, 0:3072]
        for g in range(3):
            for qq in range(128):
                u0 = (g - 1) * 128 - qq + 256
                idx = g * 128 + qq
                bk = idx // 64
                self.mm(ps6[:, idx * 8:(idx + 1) * 8], oh1b[0:32, u0:u0 + 128], erbb[0:32, 0:8], True, True,
                        reads=[boh2, berb2], writes=[self.bb[bk]])
        ps6v = ps6.rearrange("p (n h) -> p n h", h=8)
        for h in range(8):
            self.cp("dve" if h % 2 == 0 else "act", EB[:, h, :], ps6v[:, :, h], reads=self.bb[0:6], writes=[bEB])

        def normalize0(*a, **k):
            return normalize(*a, **k)

        def load_block(b, what):
            sl = slice(b * 512, (b + 1) * 512)
            if what == "QA":
                v = self.QA_T.ap().rearrange("(c p) t -> p c t", p=128)
                self.dma(QA[0:64, :, 0, :], v[0:64, :, sl], writes=[bQA])
                self.dma(QA[64:128, :, 1, :], v[64:128, :, sl], writes=[bQA])
            elif what == "QB":
                v = self.QB_T.ap().rearrange("(c p) t -> p c t", p=128)
                self.dma(QB[0:64, :, 0, :], v[0:64, :, sl], writes=[bQB])
                self.dma(QB[64:128, :, 1, :], v[64:128, :, sl], writes=[bQB])
            elif what == "QM":
                self.dma(QM, self.QM_T.ap().rearrange("(c p) t -> p c t", p=128)[:, :, sl], writes=[bQM])
            else:
                self.dma(G, self.G_T.ap().rearrange("(c p) t -> p c t", p=128)[:, :, sl], writes=[bG])

        for w in ("QA", "G", "QB", "QM"):
            load_block(0, w)
        LA = 2
        for b in range(NB):
            for c in range(4):
                j = c // 2
                obs = obank(2)
                units = [(kt, hh) for kt in range(NT) for hh in range(2)]
                sbs = {}

                def qk(u):
                    kt, hh = u
                    sb, bsb = sbank()
                    self.mm(sb, KA[:, j, kt * 128:(kt + 1) * 128], QA[:, c, hh, :], True, True,
                            reads=[bKA, bQA], writes=[bsb])
                    sbs[u] = (sb, bsb)

                for u in units[:LA]:
                    qk(u)
                for idx, u in enumerate(units):
                    if idx + LA < len(units):
                        qk(units[idx + LA])
                    kt, hh = u
                    sb, bsb = sbs.pop(u)
                    pt, bpt = ptbuf()
                    self.act(pt[:, 0:512], sb, AF.Exp, reads=[bsb], writes=[bpt], scale=0.125)
                    ob, bob = obs[hh]
                    v0 = j * 192 + hh * 64
                    self.mm(ob, VA[:, kt, v0:v0 + 128], pt[:, 0:512], kt == 0, kt == NT - 1, reads=[bVA, bpt], writes=[bob])
                for hh in range(2):
                    ob, bob = obs[hh]
                    normalize(ob, bob, (hh * 64, hh * 64 + 64), ((1 - hh) * 64, (1 - hh) * 64 + 64), c, mode="dve", yeng="pool")
            if b + 1 < NB:
                load_block(b + 1, "QA")
            bunits = [(h, ql) for h in range(8) for ql in range(4)]
            bsbs, bobs = {}, {}

            def b_stage1(u):
                h, ql = u
                c, hh, j = h // 2, h % 2, h // 4
                i = b * 4 + ql
                gs = [g for g in range(3) if 0 <= i + g - 1 < NT]
                sb, bsb = sbank()
                for g in gs:
                    kt = i + g - 1
                    self.mm(sb[:, g * 128:(g + 1) * 128], KB[:, j, kt * 128:(kt + 1) * 128],
                            QB[:, c, hh, ql * 128:(ql + 1) * 128], True, True, reads=[bKB, bQB], writes=[bsb])
                bsbs[u] = (sb, bsb, gs)

            def b_stage2(u, n):
                h, ql = u
                c, hh, j = h // 2, h % 2, h // 4
                r0, r1 = hh * 64, hh * 64 + 64
                i = b * 4 + ql
                if ql == 0:
                    bobs[h] = obank(1)[0]
                ob, bob = bobs[h]
                sb, bsb, gs = bsbs.pop(u)
                c0, c1 = gs[0] * 128, (gs[-1] + 1) * 128
                pf, bpf = PTF[n % 2]
                self.act(pf[:, c0:c1], sb[:, c0:c1], AF.Exp, reads=[bsb], writes=[bpf], scale=0.125)
                pt, bpt = ptbuf()
                self.tt("dve", pt[:, c0:c1], pf[:, c0:c1], EB[:, h, c0:c1], ALU.mult, reads=[bpf, bEB], writes=[bpt])
                v0 = (2 + j) * 192 + hh * 64
                for g in gs:
                    kt = i + g - 1
                    self.mm(ob[:, ql * 128:(ql + 1) * 128], VA[:, kt, v0:v0 + 128], pt[:, g * 128:(g + 1) * 128],
                            g == gs[0], g == gs[-1], reads=[bVA, bpt], writes=[bob])
                if ql == 3:
                    d0 = (1 - hh) * 64
                    pend.append((n + 2, lambda: normalize(ob, bob, (r0, r1), (d0, d0 + 64), 4 + c,
                                                          add_scalar=(esink[d0:d0 + 64, h:h + 1], bes))))

            LB = 3
            pend = []
            for u in bunits[:LB]:
                b_stage1(u)
            for n, u in enumerate(bunits):
                if n + LB < len(bunits):
                    b_stage1(bunits[n + LB])
                b_stage2(u, n)
                while pend and pend[0][0] <= n:
                    pend.pop(0)[1]()
            while pend:
                pend.pop(0)[1]()
            if b + 1 < NB:
                load_block(b + 1, "QB")
            mem_attn(b)
            if b + 1 < NB:
                load_block(b + 1, "QM")
            out_proj(b)
            if b + 1 < NB:
                load_block(b + 1, "G")

    def attn_layer1(self, L):
        P = self.P
        QM, G, YT, bQM, bG, bYT = L["QM"], L["G"], L["YT"], L["bQM"], L["bG"], L["bYT"]
        sbank, obank, ptbuf, normalize, mem_attn, out_proj = (L["sbank"], L["obank"], L["ptbuf"], L["normalize"],
                                                              L["mem_attn"], L["out_proj"])
        st = L["st"]
        E = self.alloc(16 * 9 * 128, BF16).rearrange("p (h t q) -> p h t q", h=16, t=9)
        bE = Buf()
        save = self.aoff
        erT = self.alloc(240)
        berT = Buf()
        self.dma(erT[0:31, :], self.rpbT.ap(), writes=[berT])
        self.act(erT[0:31, :], erT[0:31, :], AF.Exp, reads=[berT], writes=[berT])
        self.memset("pool", E.rearrange("p h t q -> p (h t q)"), 0.0, writes=[bE])
        ohp = [(self.alloc(1024), Buf()) for _ in range(2)]
        tiles = [(-3, False), (-2, False), (-2, True), (-1, False), (0, False), (1, False), (2, True), (2, False), (3, False)]
        ps4 = self.PS[:, 0:2048].rearrange("p (q n) -> p q n", n=256)
        cnt = 0
        for r in range(8):
            oh, boh = ohp[r % 2]
            self.dma(oh[0:31, :], self.c_ohc.ap()[:, r * 1024:(r + 1) * 1024], writes=[boh])
            for i in range(8):
                self.mm(self.PS[:, i * 256:i * 256 + 240], oh[0:31, i * 128:(i + 1) * 128], erT[0:31, 0:240], True, True,
                        reads=[boh, berT], writes=[self.bb[i // 2]])
            for ti, (c, masked) in enumerate(tiles):
                for kl in range(2):
                    for ql in range(2):
                        dkr = 2 * c + kl - ql
                        if abs(dkr) > 7:
                            continue
                        if masked and not (-4 <= dkr <= 3):
                            continue
                        dr = dkr + 7
                        src = ps4[kl * 64:(kl + 1) * 64, :, 0:240].rearrange("p q (h r) -> p h q r", r=15)[:, :, :, dr]
                        dst = E[kl * 64:(kl + 1) * 64, :, ti, ql * 64 + r * 8:ql * 64 + r * 8 + 8]
                        self.cp("dve" if cnt % 2 == 0 else "act", dst, src, reads=self.bb[0:4], writes=[bE])
                        cnt += 1
        self.P.barrier()
        self.aoff = save
        R = 12
        KR = self.alloc(8 * R * 128, BF16).rearrange("p (c t) -> p c t", c=8)
        VR = self.alloc(R * 1536, BF16).rearrange("p (s n) -> p s n", s=R)
        bKR = [Buf() for _ in range(R)]
        bVR = [Buf() for _ in range(R)]
        QC = self.alloc(8 * 2 * 512, BF16).rearrange("p (c e t) -> p c e t", c=8, e=2)
        bQC = Buf()
        self.memset("pool", QC[64:128, :, 0, :], 0.0, writes=[bQC])
        self.memset("pool", QC[0:64, :, 1, :], 0.0, writes=[bQC])
        sslot = [Buf() for _ in range(20)]
        PTF = [(self.alloc(640), Buf()) for _ in range(2)]
        kview = self.KC_T.ap().rearrange("(c p) t -> p c t", p=128)

        def load_tile(kt):
            s_ = kt % R
            self.dma(KR[:, :, s_ * 128:(s_ + 1) * 128], kview[:, :, kt * 128:(kt + 1) * 128], writes=[bKR[s_]])
            self.dma(VR[:, s_, :], self.VC.ap()[kt * 128:(kt + 1) * 128, :], writes=[bVR[s_]])

        def load_block(b, what):
            sl = slice(b * 512, (b + 1) * 512)
            if what == "QC":
                v = self.QC_T.ap().rearrange("(c p) t -> p c t", p=128)
                self.dma(QC[0:64, :, 0, :], v[0:64, :, sl], writes=[bQC])
                self.dma(QC[64:128, :, 1, :], v[64:128, :, sl], writes=[bQC])
            elif what == "QM":
                self.dma(QM, self.QM_T.ap().rearrange("(c p) t -> p c t", p=128)[:, :, sl], writes=[bQM])
            else:
                self.dma(G, self.G_T.ap().rearrange("(c p) t -> p c t", p=128)[:, :, sl], writes=[bG])

        def window(qt):
            if qt == 0:
                return [(0, 4), (1, 5), (2, 7), (3, 8)]
            if qt == 1:
                return [(0, 3), (1, 4), (2, 5), (3, 7)]
            if qt == 30:
                return [(28, 1), (29, 3), (30, 4), (31, 5)]
            if qt == 31:
                return [(28, 0), (29, 1), (30, 3), (31, 4)]
            return [(qt - 2 + i, 2 + i) for i in range(5)]

        for w in ("QC", "G", "QM"):
            load_block(0, w)
        for kt in range(0, 6):
            load_tile(kt)
        for b in range(NB):
            if b + 1 < NB:
                for kt in range(4 * b + 6, min(4 * b + 10, NT)):
                    load_tile(kt)
            cunits = [(h, ql) for h in range(16) for ql in range(4)]
            cst, cobs = {}, {}

            def c_stage1(u, n):
                h, ql = u
                c, hh = h // 2, h % 2
                win = window(b * 4 + ql)
                base = (n % 4) * 5
                for i, (kt, ti) in enumerate(win):
                    s_ = kt % R
                    col = (base + i) * 128
                    self.mm(self.PS[:, col:col + 128], KR[:, c, s_ * 128:(s_ + 1) * 128],
                            QC[:, c, hh, ql * 128:(ql + 1) * 128], True, True,
                            reads=[bKR[s_], bQC], writes=[sslot[base + i]])
                cst[u] = (win, base)

            def c_stage2(u, n):
                h, ql = u
                c, hh = h // 2, h % 2
                r0, r1 = hh * 64, hh * 64 + 64
                if ql == 0:
                    cobs[h] = obank(1)[0]
                ob, bob = cobs[h]
                win, base = cst.pop(u)
                nw = len(win)
                sb2 = self.PS[:, base * 128:(base + nw) * 128]
                pf, bpf = PTF[n % 2]
                self.act(pf[:, 0:nw * 128], sb2, AF.Exp, reads=sslot[base:base + nw], writes=[bpf], scale=0.125)
                pt, bpt = ptbuf()
                i0 = 0
                while i0 < nw:
                    i1 = i0
                    while i1 + 1 < nw and win[i1 + 1][1] == win[i1][1] + 1:
                        i1 += 1
                    t0, t1 = win[i0][1], win[i1][1]
                    ev = E[:, h, t0:t1 + 1, :].rearrange("p t q -> p (t q)")
                    self.tt("dve", pt[:, i0 * 128:(i1 + 1) * 128], pf[:, i0 * 128:(i1 + 1) * 128], ev, ALU.mult,
                            reads=[bpf, bE], writes=[bpt])
                    i0 = i1 + 1
                v0 = c * 192 + hh * 64
                for i, (kt, ti) in enumerate(win):
                    s_ = kt % R
                    self.mm(ob[:, ql * 128:(ql + 1) * 128], VR[:, s_, v0:v0 + 128], pt[:, i * 128:(i + 1) * 128],
                            i == 0, i == nw - 1, reads=[bVR[s_], bpt], writes=[bob])
                if ql == 3:
                    d0 = (1 - hh) * 64
                    pend.append((n + 0,
                                 lambda: normalize(ob, bob, (r0, r1), (d0, d0 + 64), c, yeng="pool")))

            LC = 3
            pend = []
            for n, u in enumerate(cunits[:LC]):
                c_stage1(u, n)
            for n, u in enumerate(cunits):
                if n + LC < len(cunits):
                    c_stage1(cunits[n + LC], n + LC)
                c_stage2(u, n)
                while pend and pend[0][0] <= n:
                    pend.pop(0)[1]()
            while pend:
                pend.pop(0)[1]()
            if b + 1 < NB:
                load_block(b + 1, "QC")
            mem_attn(b)
            if b + 1 < NB:
                load_block(b + 1, "QM")
            out_proj(b)
            if b + 1 < NB:
                load_block(b + 1, "G")


def prep_inputs(inputs):
    f = lambda a: np.ascontiguousarray(np.asarray(a, dtype=np.float32))
    x = f(inputs["x"])
    mem = f(inputs["mem"])
    ng = f(inputs["norm_gain"])
    mg = f(inputs["mem_norm_gain"])
    gains = np.concatenate([ng[0].reshape(8, 128).T, ng[1].reshape(8, 128).T, mg.reshape(8, 128).T], axis=1)
    qkg = np.stack([np.tile(f(inputs["q_norm_a"])[0], 2), np.tile(f(inputs["k_norm_a"])[0], 2)], axis=1)
    shared = {
        "w_in_even": f(inputs["w_in_even"])[0], "w_in_odd": f(inputs["w_in_odd"])[0],
        "w_out_even": f(inputs["w_out_even"])[0], "w_out_odd": f(inputs["w_out_odd"])[0],
        "w_mem_kv": f(inputs["w_mem_kv"]),
        "gains_pp": f(gains), "qk_gain": f(qkg),
        "final_gain": f(inputs["final_norm_gain"]).reshape(1, D),
        "sink_b": f(inputs["sink_b"]).reshape(1, 8),
        "rel_bias": f(inputs["rel_bias"]),
        "rpbT": f(np.transpose(f(inputs["rpb_c"])[0], (2, 0, 1)).reshape(31, 240)),
    }
    shared.update(host_consts())
    in_maps = []
    for b in range(8):
        m = dict(shared)
        m["x"] = x[b]
        m["mem"] = mem[b]
        in_maps.append(m)
    return in_maps


def kernel(**inputs):
    bld = Builder()
    nc = bld.build()
    in_maps = prep_inputs(inputs)
    res = run_bass_kernel_spmd(nc, in_maps, core_ids=list(range(8)))
    return np.stack([np.asarray(r["out"]) for r in res.results], axis=0).astype(np.float32)
```

```python
from contextlib import ExitStack
import math
import numpy as np
import concourse.bass as bass
import concourse.mybir as mybir
from concourse.bass_utils import run_bass_kernel_spmd

F32 = mybir.dt.float32
BF16 = mybir.dt.bfloat16
ALU = mybir.AluOpType
AF = mybir.ActivationFunctionType
AX = mybir.AxisListType

ENGS = ("pe", "act", "dve", "pool", "sp")
DMA_SLOTS = 12

S = 4096
D = 1024
NT = 32
NB = 8
EPS = 1e-6


class Buf:
    __slots__ = ("w", "r", "name")

    def __init__(self, name=""):
        self.w = None
        self.r = []
        self.name = name


class Op:
    __slots__ = ("eng", "fn", "deps", "needs_inc", "inc_val", "dma", "slot", "slot_val", "id")

    def __init__(self, eng, fn, dma):
        self.eng = eng
        self.fn = fn
        self.dma = dma
        self.deps = set()
        self.needs_inc = False
        self.inc_val = None
        self.slot = None
        self.slot_val = None
        self.id = None


class Prog:
    def __init__(self, nc):
        self.nc = nc
        self.ops = []
        self.dma_count = {e: 0 for e in ENGS}
        self.dma_hist = {e: [] for e in ENGS}
        self.last_op = {e: None for e in ENGS}

    def add(self, eng, fn, reads=(), writes=(), dma=False):
        op = Op(eng, fn, dma)
        op.id = len(self.ops)
        deps = set()
        for b in reads:
            if b.w is not None:
                deps.add(b.w)
        for b in writes:
            if b.w is not None:
                deps.add(b.w)
            for r in b.r:
                deps.add(r)
        for b in reads:
            b.r.append(op)
        for b in writes:
            b.w = op
            b.r = []
        for d in deps:
            if d is op:
                continue
            if (not d.dma) and (not dma) and d.eng == "pe" and eng == "pe":
                continue
            op.deps.add(d)
            if not d.dma:
                d.needs_inc = True
        if dma:
            n = self.dma_count[eng]
            op.slot = n % DMA_SLOTS
            op.slot_val = 16 * (n // DMA_SLOTS + 1)
            hist = self.dma_hist[eng]
            if n >= DMA_SLOTS:
                op.deps.add(hist[n - DMA_SLOTS])
            hist.append(op)
            self.dma_count[eng] = n + 1
        else:
            self.last_op[eng] = op
        self.ops.append(op)
        return op

    def barrier(self):
        lasts = [self.last_op[e] for e in ENGS if self.last_op[e] is not None]
        dmas = []
        for e in ENGS:
            dmas += self.dma_hist[e][-DMA_SLOTS:]
        for e in ENGS:
            op = Op(e, None, False)
            op.id = len(self.ops)
            for d in lasts:
                if d.eng != e:
                    op.deps.add(d)
                    d.needs_inc = True
            for d in dmas:
                op.deps.add(d)
            self.ops.append(op)

    def emit(self):
        nc = self.nc
        with ExitStack() as st:
            sem = {e: st.enter_context(nc.semaphore("c_" + e)) for e in ENGS}
            dsem = {}
            for e in ENGS:
                if self.dma_count[e] > 0:
                    dsem[e] = [st.enter_context(nc.semaphore("d_%s_%d" % (e, i))) for i in range(DMA_SLOTS)]
            cnt = {e: 0 for e in ENGS}
            for op in self.ops:
                if op.dma or op.fn is None:
                    continue
                if op.needs_inc:
                    cnt[op.eng] += 1
                    op.inc_val = cnt[op.eng]
            per_eng = {e: [] for e in ENGS}
            seen = {e: {} for e in ENGS}
            for op in self.ops:
                waits = {}
                for d in op.deps:
                    if d.dma:
                        key = ("d", d.eng, d.slot)
                        val = d.slot_val
                    else:
                        key = ("c", d.eng)
                        val = d.inc_val
                    if waits.get(key, 0) < val:
                        waits[key] = val
                wl = []
                s = seen[op.eng]
                for key, val in waits.items():
                    if s.get(key, 0) >= val:
                        continue
                    s[key] = val
                    wl.append((key, val))
                per_eng[op.eng].append((op, wl))
            block = st.enter_context(nc.Block())

            def run(engname, e):
                for op, wl in per_eng[engname]:
                    for key, val in wl:
                        if key[0] == "d":
                            e.wait_ge(dsem[key[1]][key[2]], val)
                        else:
                            e.wait_ge(sem[key[1]], val)
                    if op.fn is None:
                        continue
                    ins = op.fn(e)
                    if op.dma:
                        ins.then_inc(dsem[engname][op.slot], 16)
                    elif op.needs_inc:
                        ins.then_inc(sem[engname], 1)

            @block.tensor
            def _(e):
                run("pe", e)

            @block.scalar
            def _(e):
                run("act", e)

            @block.vector
            def _(e):
                run("dve", e)

            @block.gpsimd
            def _(e):
                run("pool", e)

            @block.sync
            def _(e):
                run("sp", e)


def _t5_bucket(rel):
    nb = 16
    max_exact = 8
    ret = np.where(rel > 0, nb, 0)
    n = np.abs(rel)
    nf = np.maximum(n, 1).astype(np.float32)
    large = max_exact + (np.log(nf / np.float32(max_exact)) / np.float32(math.log(128 / max_exact))
                         * np.float32(nb - max_exact)).astype(np.int32)
    large = np.minimum(large, nb - 1)
    return ret + np.where(n < max_exact, n, large)


def host_consts():
    c = {}
    c["ident"] = np.eye(128, dtype=np.float32)
    R = np.zeros((128, 128), np.float32)
    for p in range(128):
        sub = p % 32
        if sub < 16:
            R[p, p + 16] = -1.0
        else:
            R[p, p - 16] = 1.0
    c["rotT"] = np.ascontiguousarray(R.T)
    bd = np.zeros((128, 128), np.float32)
    bd[0:64, 0:64] = 1.0
    bd[64:128, 64:128] = 1.0
    c["bd"] = bd
    t = np.arange(S)
    row = (t // 64).astype(np.float32)
    col = (t % 64).astype(np.float32)
    freqs = np.power(np.float32(10000.0), -np.arange(16, dtype=np.float32) / np.float32(16)).astype(np.float32)
    cosT = np.zeros((128, S), np.float32)
    sinT = np.zeros((128, S), np.float32)
    for p in range(128):
        dh = p % 64
        pos = row if dh < 32 else col
        ang = (pos * freqs[dh % 16]).astype(np.float32)
        cosT[p] = np.cos(ang)
        sinT[p] = np.sin(ang)
    c["cosT"] = cosT
    c["sinT"] = sinT
    rel = np.arange(-256, 384)
    bk = _t5_bucket(rel)
    oh1 = np.zeros((32, 640), np.float32)
    for u, r in enumerate(rel):
        if abs(r) <= 128:
            oh1[bk[u], u] = 1.0
    c["oh1"] = oh1
    ohc = np.zeros((31, 64, 128), np.float32)
    for qc in range(64):
        cs = min(max(qc - 8, 0), 48)
        for kc in range(cs, cs + 16):
            dc = kc - qc + 15
            ohc[dc, qc, kc] = 1.0
            ohc[dc, qc, 64 + kc] = 1.0
    c["ohc"] = ohc.reshape(31, 64 * 128)
    return c


class Builder:
    def __init__(self, debug=None, stop_after=None):
        self.debug = debug or ()
        self.stop_after = stop_after
        self.nc = bass.Bass("TRN2", target_bir_lowering=False)
        self.P = Prog(self.nc)
        self.dram = {}

    def din(self, name, shape, dt=F32):
        t = self.nc.dram_tensor(name, list(shape), dt, kind="ExternalInput")
        self.dram[name] = t
        return t

    def dscratch(self, name, shape, dt):
        kind = "ExternalOutput" if name in self.debug else "Internal"
        t = self.nc.dram_tensor(name, list(shape), dt, kind=kind)
        self.dram[name] = t
        return t

    def reset_arena(self):
        self.aoff = 0

    def alloc(self, ncols, dt=F32):
        nbytes = ncols * (4 if dt == F32 else 2)
        n32 = (nbytes + 3) // 4
        n32 = (n32 + 1) // 2 * 2
        a = self.ARENA[:, self.aoff:self.aoff + n32]
        self.aoff += n32
        assert self.aoff <= self.ARENA_N, "arena overflow %d > %d" % (self.aoff, self.ARENA_N)
        if dt == F32:
            return a[:, 0:ncols]
        return a.bitcast(BF16)[:, 0:ncols]

    def dma(self, out, in_, reads=(), writes=(), q="sp", **kw):
        return self.P.add(q, lambda e: e.dma_start(out=out, in_=in_, **kw), reads, writes, dma=True)

    def mm(self, out, lhsT, rhs, start, stop, reads=(), writes=()):
        return self.P.add("pe", lambda e: e.matmul(out, lhsT=lhsT, rhs=rhs, start=start, stop=stop), reads, writes)

    def tr(self, out, in_, ident, reads=(), writes=()):
        return self.P.add("pe", lambda e: e.transpose(out, in_, ident), reads, writes)

    def act(self, out, in_, func, reads=(), writes=(), **kw):
        return self.P.add("act", lambda e: e.activation(out=out, in_=in_, func=func, **kw), reads, writes)

    def tt(self, eng, out, in0, in1, op, reads=(), writes=()):
        return self.P.add(eng, lambda e: e.tensor_tensor(out=out, in0=in0, in1=in1, op=op), reads, writes)

    def ts(self, eng, out, in0, s1, op0, reads=(), writes=(), s2=None, op1=None):
        if op1 is None:
            return self.P.add(eng, lambda e: e.tensor_scalar(out=out, in0=in0, scalar1=s1, scalar2=None, op0=op0), reads, writes)
        return self.P.add(eng, lambda e: e.tensor_scalar(out=out, in0=in0, scalar1=s1, scalar2=s2, op0=op0, op1=op1), reads, writes)

    def cp(self, eng, out, in_, reads=(), writes=()):
        if eng == "act":
            return self.P.add("act", lambda e: e.copy(out=out, in_=in_), reads, writes)
        return self.P.add(eng, lambda e: e.tensor_copy(out=out, in_=in_), reads, writes)

    def memset(self, eng, ap, val, writes=()):
        return self.P.add(eng, lambda e: e.memset(ap, val), (), writes)

    def recip(self, out, in_, reads=(), writes=()):
        return self.P.add("dve", lambda e: e.reciprocal(out=out, in_=in_), reads, writes)

    def build(self):
        nc = self.nc
        self.x = self.din("x", [S, D])
        self.mem = self.din("mem", [256, D])
        self.w_in = [self.din("w_in_even", [D, 3584]), self.din("w_in_odd", [D, 5120])]
        self.w_out = [self.din("w_out_even", [1536, D]), self.din("w_out_odd", [1536, D])]
        self.w_mem = self.din("w_mem_kv", [2, D, 1024])
        self.gains = self.din("gains_pp", [128, 24])
        self.qkg = self.din("qk_gain", [128, 2])
        self.fgain = self.din("final_gain", [1, D])
        self.sink = self.din("sink_b", [1, 8])
        self.relb = self.din("rel_bias", [32, 8])
        self.rpbT = self.din("rpbT", [31, 240])
        self.c_ident = self.din("ident", [128, 128])
        self.c_rotT = self.din("rotT", [128, 128])
        self.c_bd = self.din("bd", [128, 128])
        self.c_cos = self.din("cosT", [128, S])
        self.c_sin = self.din("sinT", [128, S])
        self.c_oh1 = self.din("oh1", [32, 640])
        self.c_ohc = self.din("ohc", [31, 64 * 128])
        self.out = nc.dram_tensor("out", [S, D], F32, kind="ExternalOutput")
        self.QA_T = self.dscratch("QA_T", [512, S], BF16)
        self.KA_T = self.dscratch("KA_T", [256, S], BF16)
        self.QB_T = self.dscratch("QB_T", [512, S], BF16)
        self.KB_T = self.dscratch("KB_T", [256, S], BF16)
        self.QM_T = self.dscratch("QM_T", [512, S], BF16)
        self.G_T = self.dscratch("G_T", [1536, S], BF16)
        self.VAB = self.dscratch("VAB", [S, 768], BF16)
        self.X1 = self.dscratch("X1", [S, D], F32)
        self.QC_T = self.dscratch("QC_T", [1024, S], BF16)
        self.KC_T = self.dscratch("KC_T", [1024, S], BF16)
        self.VC = self.dscratch("VC", [S, 1536], BF16)
        self.MK_T = self.dscratch("MK_T", [2, 512, 256], BF16)
        self.MV = self.dscratch("MV", [2, 256, 512], BF16)

        with ExitStack() as st:
            self.ARENA_N = 52000
            self.ARENA = st.enter_context(nc.sbuf_tensor("arena", [128, self.ARENA_N], F32))
            self.PS = st.enter_context(nc.psum_tensor("ps", [128, 4096], F32))
            self.bank = [self.PS[:, i * 512:(i + 1) * 512] for i in range(8)]
            self.bb = [Buf("bank%d" % i) for i in range(8)]
            self.phase_mem()
            self.P.barrier()
            for layer in range(2):
                self.phase_proj(layer)
                self.P.barrier()
                if self.stop_after == ("proj", layer):
                    break
                self.phase_attn(layer)
                self.P.barrier()
                if self.stop_after == ("attn", layer):
                    break
            self.P.emit()
        return nc

    def load_consts(self):
        c = {}
        c["ident"] = self.alloc(128)
        c["b_ident"] = Buf()
        self.dma(c["ident"], self.c_ident.ap(), writes=[c["b_ident"]])
        c["gains"] = self.alloc(24)
        c["b_gains"] = Buf()
        self.dma(c["gains"], self.gains.ap(), writes=[c["b_gains"]])
        return c

    def norm_transpose(self, c, xt, bx, ht3, bht, j, gcol, tp_banks, btp, scr):
        junk, bjunk, stat, bstat = scr
        self.act(junk, xt, AF.Square, reads=[bx], writes=[bjunk, bstat], accum_out=stat[:, 0:1])
        self.ts("dve", stat[:, 1:2], stat[:, 0:1], 1.0 / D, ALU.mult, reads=[bstat], writes=[bstat], s2=EPS, op1=ALU.add)
        if False:
            self.act(stat[:, 2:3], stat[:, 1:2], AF.Ln, reads=[bstat], writes=[bstat])
            self.act(stat[:, 3:4], stat[:, 2:3], AF.Exp, reads=[bstat], writes=[bstat], scale=-0.5)
        else:
            self.act(stat[:, 2:3], stat[:, 1:2], AF.Sqrt, reads=[bstat], writes=[bstat])
            self.recip(stat[:, 3:4], stat[:, 2:3], reads=[bstat], writes=[bstat])
        self.ts("dve", xt, xt, stat[:, 3:4], ALU.mult, reads=[bx, bstat], writes=[bx])
        tp = tp_banks
        for kc in range(8):
            self.tr(tp[:, kc * 128:(kc + 1) * 128], xt[:, kc * 128:(kc + 1) * 128], c["ident"],
                    reads=[bx, c["b_ident"]], writes=list(btp))
        g = c["gains"][:, gcol:gcol + 8]
        gb = g.unsqueeze(2).broadcast_to([128, 8, 128])
        tp3 = tp.rearrange("p (a b) -> p a b", a=8)
        self.tt("dve", ht3[:, :, j * 128:(j + 1) * 128], tp3, gb, ALU.mult,
                reads=list(btp) + [c["b_gains"]], writes=[bht])

    def phase_mem(self):
        self.reset_arena()
        c = self.load_consts()
        W = self.alloc(2 * 8 * 1024, BF16).rearrange("p (l k n) -> p l k n", l=2, k=8)
        bW = Buf()
        for l in range(2):
            self.dma(W[:, l], self.w_mem.ap()[l].rearrange("(k p) n -> p k n", p=128), writes=[bW], q="pool")
        ht3 = self.alloc(8 * 256, BF16).rearrange("p (k t) -> p k t", k=8)
        bht = Buf()
        junk = self.alloc(1024, BF16)
        scr = (junk, Buf(), self.alloc(4), Buf())
        tpb = self.PS[:, 0:1024]
        btp = [self.bb[0], self.bb[1]]
        for j in range(2):
            xt = self.alloc(1024)
            bx = Buf()
            self.dma(xt, self.mem.ap()[j * 128:(j + 1) * 128, :], writes=[bx])
            self.norm_transpose(c, xt, bx, ht3, bht, j, 16, tpb, btp, scr)
        stg = self.alloc(4 * 256, BF16).rearrange("p (c t) -> p c t", c=4)
        stv = self.alloc(2 * 512, BF16).rearrange("p (c t) -> p c t", c=2)
        bst, bsv = Buf(), Buf()
        for l in range(2):
            for hm in range(4):
                pb = self.bank[2 + hm % 2]
                bpb = self.bb[2 + hm % 2]
                for kc in range(8):
                    self.mm(pb[:, 0:256], W[:, l, kc, hm * 128:(hm + 1) * 128], ht3[:, kc, :], kc == 0, kc == 7,
                            reads=[bW, bht], writes=[bpb])
                self.cp("dve", stg[:, hm, :], pb[:, 0:256], reads=[bpb], writes=[bst])
            self.dma(self.MK_T.ap()[l].rearrange("(c p) t -> p c t", p=128), stg, reads=[bst])
            for mt in range(2):
                pb = self.bank[4 + mt]
                bpb = self.bb[4 + mt]
                for kc in range(8):
                    self.mm(pb, ht3[:, kc, mt * 128:(mt + 1) * 128], W[:, l, kc, 512:1024], kc == 0, kc == 7,
                            reads=[bW, bht], writes=[bpb])
                self.cp("act", stv[:, mt, :], pb, reads=[bpb], writes=[bsv])
            self.dma(self.MV.ap()[l].rearrange("(c p) n -> p c n", p=128), stv, reads=[bsv])

    def phase_proj(self, layer):
        self.reset_arena()
        c = self.load_consts()
        if layer == 0:
            wl = [(0, 512, 0)]
            wl += [(512, 64, 512), (512, 64, 576), (576, 64, 640), (576, 64, 704)]
            wl += [(768, 512, 768)]
            wl += [(1280, 64, 1280), (1280, 64, 1344), (1344, 64, 1408), (1344, 64, 1472)]
            wl += [(1536, 512, 1536), (2048, 1536, 2048), (640, 128, 3584), (1408, 128, 3712)]
            NC = 3840
            groups = [("qa", 0, 4, self.QA_T, "rope_q"), ("ka", 512, 2, self.KA_T, "rope_k"),
                      ("qb", 768, 4, self.QB_T, "copy"), ("kb", 1280, 2, self.KB_T, "copy"),
                      ("qm", 1536, 4, self.QM_T, "copy"), ("gate", 2048, 12, self.G_T, "silu")]
            vcol, vn = 3584, 256
        else:
            wl = [(0, 2048, 0), (3072, 2048, 2048), (2048, 1024, 4096)]
            NC = 5120
            groups = [("qc", 0, 8, self.QC_T, "copy"), ("kc", 1024, 8, self.KC_T, "copy"),
                      ("qm", 2048, 4, self.QM_T, "copy"), ("gate", 2560, 12, self.G_T, "silu")]
            vcol, vn = 4096, 1024
        W = self.alloc(8 * NC, BF16).rearrange("p (k n) -> p k n", k=8)
        wsrc = self.w_in[layer].ap().rearrange("(k p) n -> p k n", p=128)
        wpieces = []
        for (s0, n, d0) in wl:
            for a in range(0, n, 1024):
                m = min(1024, n - a)
                pb_ = Buf()
                self.dma(W[:, :, d0 + a:d0 + a + m], wsrc[:, :, s0 + a:s0 + a + m], writes=[pb_], q="pool")
                wpieces.append((d0 + a, d0 + a + m, pb_))

        def wbuf(c0, c1):
            return [b_ for (p0, p1, b_) in wpieces if p0 < c1 and c0 < p1]
        if layer == 0:
            rotT = self.alloc(128)
            bd = self.alloc(128)
            qkg = self.alloc(2)
            bcst = Buf()
            self.dma(rotT, self.c_rotT.ap(), writes=[bcst])
            self.dma(bd, self.c_bd.ap(), writes=[bcst])
            self.dma(qkg, self.qkg.ap(), writes=[bcst])
            epsT = self.alloc(2)
            self.memset("pool", epsT, EPS, writes=[bcst])
            cs = [(self.alloc(512), self.alloc(512), Buf()) for _ in range(2)]
            ropeT = [(self.alloc(512), self.alloc(512), self.alloc(512), self.alloc(512), Buf(), Buf(), Buf(), Buf())
                     for _ in range(2)]
        XT = [self.alloc(1024) for _ in range(4)]
        bXT = [Buf() for _ in range(4)]
        HT = [self.alloc(8 * 512, BF16).rearrange("p (k t) -> p k t", k=8) for _ in range(2)]
        bHT = [Buf() for _ in range(2)]
        junk = self.alloc(1024, BF16)
        scr = (junk, Buf(), self.alloc(4), Buf())
        tpb = self.PS[:, 0:1024]
        btp = [self.bb[0], self.bb[1]]
        stg = {}
        for (name, col, nch, dst, kind) in groups:
            stg[name] = [(self.alloc(nch * 512, BF16).rearrange("p (c t) -> p c t", c=nch), Buf()) for _ in range(2)]
        if layer == 0:
            VST = [(self.alloc(768, BF16), Buf()) for _ in range(2)]
        else:
            VST = [(self.alloc(1536, BF16), Buf()) for _ in range(2)]
        for (v, bv) in VST:
            self.memset("pool", v, 1.0, writes=[bv])
        pbanks = [self.bank[2], self.bank[3], self.bank[4]]
        bpb = [self.bb[2], self.bb[3], self.bb[4]]
        aux = [self.bank[5], self.bank[6]]
        baux = [self.bb[5], self.bb[6]]
        vbank = self.bank[7]
        bvb = self.bb[7]
        xsrc = self.x if layer == 0 else self.X1
        gcol = 0 if layer == 0 else 8

        def load_x(b):
            for j in range(4):
                t = b * 4 + j
                self.dma(XT[j], xsrc.ap()[t * 128:(t + 1) * 128, :], writes=[bXT[j]])

        def norm_tr(b):
            for j in range(4):
                self.norm_transpose(c, XT[j], bXT[j], HT[b % 2], bHT[b % 2], j, gcol, tpb, btp, scr)

        state = {"pi": 0, "vi": 0, "ri": 0, "vb": 0}
        pending = []

        def do_chunk(b, name, col, ci, kind, sbuf, bs):
            i = state["pi"] % 3
            state["pi"] += 1
            pb, bp = pbanks[i], bpb[i]
            ht, bh = HT[b % 2], bHT[b % 2]
            for kc in range(8):
                self.mm(pb, W[:, kc, col + ci * 128:col + (ci + 1) * 128], ht[:, kc, :], kc == 0, kc == 7,
                        reads=wbuf(col + ci * 128, col + (ci + 1) * 128) + [bh], writes=[bp])
            while pending:
                pending.pop(0)()
            dst = sbuf[:, ci, :]
            if kind == "copy":
                self.cp("act" if (state["pi"] % 2 == 0) else "dve", dst, pb, reads=[bp], writes=[bs])
            elif kind == "silu":
                self.act(dst, pb, AF.Silu, reads=[bp], writes=[bs])
            else:
                gi = 0 if kind == "rope_q" else 1
                cosb, sinb, bcs = cs[b % 2]
                tq, tsq, t1, trs, btq, btsq, bt1, btrs = ropeT[state["ri"] % 2]
                state["ri"] += 1
                self.act(tq, pb, AF.Copy, reads=[bp, bcst], writes=[btq], scale=qkg[:, gi:gi + 1])
                self.act(tsq, pb, AF.Square, reads=[bp], writes=[btsq])

                def tail():
                    self.mm(aux[0], bd, tsq, True, True, reads=[bcst, btsq], writes=[baux[0]])
                    self.mm(aux[1], rotT, tq, True, True, reads=[bcst, btq], writes=[baux[1]])
                    if True:
                        self.act(trs, aux[0], AF.Ln, reads=[baux[0], bcst], writes=[btrs], scale=1.0 / 64, bias=epsT[:, 0:1])
                        self.act(trs, trs, AF.Exp, reads=[btrs], writes=[btrs], scale=-0.5)
                    else:
                        self.ts("dve", trs, aux[0], 1.0 / 64, ALU.mult, reads=[baux[0]], writes=[btrs], s2=EPS, op1=ALU.add)
                        self.act(trs, trs, AF.Sqrt, reads=[btrs], writes=[btrs])
                        self.recip(trs, trs, reads=[btrs], writes=[btrs])
                    self.tt("dve", t1, tq, cosb, ALU.mult, reads=[btq, bcs], writes=[bt1])
                    self.tt("dve", tsq, aux[1], sinb, ALU.mult, reads=[baux[1], bcs, btsq], writes=[btsq])
                    self.tt("dve", t1, t1, tsq, ALU.add, reads=[bt1, btsq], writes=[bt1])
                    self.tt("dve", dst, t1, trs, ALU.mult, reads=[bt1, btrs], writes=[bs])

                pending.append(tail)

        def do_v(b):
            ht, bh = HT[b % 2], bHT[b % 2]
            for j in range(4):
                t = b * 4 + j
                v, bv = VST[state["vi"] % 2]
                state["vi"] += 1
                if layer == 0:
                    for kc in range(8):
                        self.mm(vbank[:, 0:256], ht[:, kc, j * 128:(j + 1) * 128], W[:, kc, vcol:vcol + 256],
                                kc == 0, kc == 7, reads=wbuf(vcol, vcol + 256) + [bh], writes=[bvb])
                    v3 = v.rearrange("p (g s) -> p g s", g=4)
                    src = vbank[:, 0:256].rearrange("p (g d) -> p g d", g=4)
                    self.cp("dve", v3[:, :, 0:64], src, reads=[bvb], writes=[bv])
                    self.cp("act", v3[:, :, 128:192], src, reads=[bvb], writes=[bv])
                    self.dma(self.VAB.ap()[t * 128:(t + 1) * 128, :], v, reads=[bv])
                else:
                    v3 = v.rearrange("p (g s) -> p g s", g=8)
                    for half in range(2):
                        vb_i = 5 + state["vb"] % 3
                        state["vb"] += 1
                        vbk, bvk = self.bank[vb_i], self.bb[vb_i]
                        for kc in range(8):
                            self.mm(vbk, ht[:, kc, j * 128:(j + 1) * 128],
                                    W[:, kc, vcol + half * 512:vcol + (half + 1) * 512],
                                    kc == 0, kc == 7, reads=wbuf(vcol + half * 512, vcol + (half + 1) * 512) + [bh], writes=[bvk])
                        src = vbk.rearrange("p (g e d) -> p g e d", g=4, e=2)
                        self.cp("dve", v3[:, half * 4:(half + 1) * 4, 0:64], src[:, :, 0, :], reads=[bvk], writes=[bv])
                        self.cp("act", v3[:, half * 4:(half + 1) * 4, 128:192], src[:, :, 1, :], reads=[bvk], writes=[bv])
                    self.dma(self.VC.ap()[t * 128:(t + 1) * 128, :], v, reads=[bv])

        chunks = []
        for (name, col, nch, dst, kind) in groups:
            for ci in range(nch):
                chunks.append((name, col, ci, nch, dst, kind))
        if layer == 0:
            rope = [ch for ch in chunks if ch[5].startswith("rope")]
            plain = [ch for ch in chunks if not ch[5].startswith("rope")]
            order = []
            while rope or plain:
                if rope:
                    order.append(rope.pop(0))
                for _ in range(3):
                    if plain:
                        order.append(plain.pop(0))
            chunks = order
        load_x(0)
        norm_tr(0)
        for b in range(NB):
            if layer == 0:
                cosb, sinb, bcs = cs[b % 2]
                self.dma(cosb, self.c_cos.ap()[:, b * 512:(b + 1) * 512], writes=[bcs])
                self.dma(sinb, self.c_sin.ap()[:, b * 512:(b + 1) * 512], writes=[bcs])
            if b + 1 < NB:
                load_x(b + 1)
            half_n = len(chunks) // 2
            done = {}
            stores = []
            for idx, (name, col, ci, nch, dst, kind) in enumerate(chunks):
                if idx == half_n and b + 1 < NB:
                    norm_tr(b + 1)
                sbuf, bs = stg[name][b % 2]
                do_chunk(b, name, col, ci, kind, sbuf, bs)
                done[name] = done.get(name, 0) + 1
                if done[name] == nch:
                    stores.append((dst, sbuf, bs, kind))
                keep = []
                for (d_, sb_, bs_, k_) in stores:
                    if k_.startswith("rope") and pending:
                        keep.append((d_, sb_, bs_, k_))
                    else:
                        self.dma(d_.ap().rearrange("(c p) t -> p c t", p=128)[:, :, b * 512:(b + 1) * 512], sb_, reads=[bs_])
                stores = keep
            while pending:
                pending.pop(0)()
            for (d_, sb_, bs_, k_) in stores:
                self.dma(d_.ap().rearrange("(c p) t -> p c t", p=128)[:, :, b * 512:(b + 1) * 512], sb_, reads=[bs_])
            do_v(b)

    def phase_attn(self, layer):
        self.reset_arena()
        P = self.P
        WOUT = self.alloc(12 * 1024, BF16).rearrange("p (c n) -> p c n", c=12)
        bWOUT = Buf()
        self.dma(WOUT, self.w_out[layer].ap().rearrange("(c p) n -> p c n", p=128), writes=[bWOUT], q="pool")
        ones_bf = self.alloc(128, BF16)
        bones = Buf()
        self.memset("pool", ones_bf, 1.0, writes=[bones])
        MKT = self.alloc(4 * 256, BF16).rearrange("p (c t) -> p c t", c=4)
        MVs = self.alloc(2 * 512, BF16).rearrange("p (c n) -> p c n", c=2)
        bMK, bMV = Buf(), Buf()
        self.dma(MKT, self.MK_T.ap()[layer].rearrange("(c p) t -> p c t", p=128), writes=[bMK])
        self.dma(MVs, self.MV.ap()[layer].rearrange("(c p) n -> p c n", p=128), writes=[bMV])
        QM = self.alloc(4 * 512, BF16).rearrange("p (c t) -> p c t", c=4)
        G = self.alloc(12 * 512, BF16).rearrange("p (c t) -> p c t", c=12)
        YT = self.alloc(12 * 512, BF16).rearrange("p (c t) -> p c t", c=12)
        bQM, bG = Buf(), Buf()
        bYT = [Buf() for _ in range(12)]
        nxb = 2 if layer == 0 else 1
        XR = [(self.alloc(1024), Buf()) for _ in range(nxb)]
        OUTT = [(self.alloc(1024), Buf()) for _ in range(nxb)]
        PT = [(self.alloc(640, BF16), Buf()) for _ in range(6)]
        RD = [(self.alloc(512), Buf()) for _ in range(2)]
        TN = [(self.alloc(512), Buf()) for _ in range(2)]
        st = {"s": 0, "pt": 0, "o": 0, "rd": 0, "x": 0}

        SB = [0, 1, 2, 3] if layer == 0 else [5]
        OB = [4, 5, 6, 7] if layer == 0 else [6, 7]

        def sbank():
            i = SB[st["s"] % len(SB)]
            st["s"] += 1
            return self.bank[i], self.bb[i]

        def obank(n=1):
            if n == 2 and st["o"] % 2 == 1:
                st["o"] += 1
            r = []
            for _ in range(n):
                i = OB[st["o"] % len(OB)]
                st["o"] += 1
                r.append((self.bank[i], self.bb[i]))
            return r

        def ptbuf():
            i = st["pt"] % 6
            st["pt"] += 1
            return PT[i]

        def normalize(ob, bob, o_rows, d_rows, ci, add_scalar=None, mode="act", yeng="dve"):
            i = st["rd"] % 2
            st["rd"] += 1
            rd, brd = RD[i]
            tn, btn = TN[i]
            o0, o1 = o_rows
            d0, d1 = d_rows
            if mode == "act":
                if add_scalar is not None:
                    self.act(rd[d0:d1, :], ob[d0:d1, :], AF.Ln, reads=[bob, add_scalar[1]], writes=[brd], bias=add_scalar[0])
                else:
                    self.act(rd[d0:d1, :], ob[d0:d1, :], AF.Ln, reads=[bob], writes=[brd])
                self.act(rd[d0:d1, :], rd[d0:d1, :], AF.Exp, reads=[brd], writes=[brd], scale=-1.0)
            else:
                if add_scalar is not None:
                    self.ts("dve", rd[d0:d1, :], ob[d0:d1, :], add_scalar[0], ALU.add, reads=[bob, add_scalar[1]], writes=[brd])
                    self.recip(rd[d0:d1, :], rd[d0:d1, :], reads=[brd], writes=[brd])
                else:
                    self.recip(rd[d0:d1, :], ob[d0:d1, :], reads=[bob], writes=[brd])
            self.tt("dve", tn[o0:o1, :], ob[o0:o1, :], rd[d0:d1, :], ALU.mult, reads=[bob, brd], writes=[btn])
            self.tt(yeng, YT[o0:o1, ci, :], tn[o0:o1, :], G[o0:o1, ci, :], ALU.mult, reads=[btn, bG], writes=[bYT[ci]])

        def mem_attn(b):
            sc = 128.0 ** -0.5
            for hm in range(4):
                (ob, bob), (db, bdb) = obank(2)
                pts = []
                for mt in range(2):
                    sb, bsb = sbank()
                    self.mm(sb, MKT[:, hm, mt * 128:(mt + 1) * 128], QM[:, hm, :], True, True, reads=[bMK, bQM], writes=[bsb])
                    pt, bpt = ptbuf()
                    self.act(pt[:, 0:512], sb, AF.Exp, reads=[bsb], writes=[bpt], scale=sc)
                    pts.append((pt, bpt))
                for mt in range(2):
                    pt, bpt = pts[mt]
                    self.mm(ob, MVs[:, mt, hm * 128:(hm + 1) * 128], pt[:, 0:512], mt == 0, mt == 1, reads=[bMV, bpt], writes=[bob])
                for mt in range(2):
                    pt, bpt = pts[mt]
                    self.mm(db, ones_bf, pt[:, 0:512], mt == 0, mt == 1, reads=[bones, bpt], writes=[bdb])
                i = st["rd"] % 2
                st["rd"] += 1
                rd, brd = RD[i]
                tn, btn = TN[i]
                self.act(rd, db, AF.Ln, reads=[bdb], writes=[brd])
                self.act(rd, rd, AF.Exp, reads=[brd], writes=[brd], scale=-1.0)
                self.tt("dve", tn, ob, rd, ALU.mult, reads=[bob, brd], writes=[btn])
                self.tt("dve" if layer == 0 else "pool", YT[:, 8 + hm, :], tn, G[:, 8 + hm, :], ALU.mult,
                        reads=[btn, bG], writes=[bYT[8 + hm]])

        def out_proj(b):
            for jt in range(4):
                t = b * 4 + jt
                xr, bxr = XR[st["x"] % nxb]
                ot, bot = OUTT[st["x"] % nxb]
                st["x"] += 1
                src = self.x if layer == 0 else self.X1
                self.dma(xr, src.ap()[t * 128:(t + 1) * 128, :], writes=[bxr])
                obs = obank(2)
                for nh in range(2):
                    ob, bob = obs[nh]
                    for cc in range(12):
                        self.mm(ob, YT[:, cc, jt * 128:(jt + 1) * 128], WOUT[:, cc, nh * 512:(nh + 1) * 512],
                                cc == 0, cc == 11, reads=[bYT[cc], bWOUT], writes=[bob])
                    self.tt("dve", ot[:, nh * 512:(nh + 1) * 512], ob, xr[:, nh * 512:(nh + 1) * 512], ALU.add,
                            reads=[bob, bxr], writes=[bot])
                if layer == 0:
                    self.dma(self.X1.ap()[t * 128:(t + 1) * 128, :], ot, reads=[bot])
                else:
                    self.act(fjunk, ot, AF.Square, reads=[bot], writes=[bfj, bfs], accum_out=fstat[:, 0:1])
                    self.ts("dve", fstat[:, 1:2], fstat[:, 0:1], 1.0 / D, ALU.mult, reads=[bfs], writes=[bfs], s2=EPS, op1=ALU.add)
                    if False:
                        self.act(fstat[:, 2:3], fstat[:, 1:2], AF.Ln, reads=[bfs], writes=[bfs])
                        self.act(fstat[:, 3:4], fstat[:, 2:3], AF.Exp, reads=[bfs], writes=[bfs], scale=-0.5)
                    else:
                        self.act(fstat[:, 2:3], fstat[:, 1:2], AF.Sqrt, reads=[bfs], writes=[bfs])
                        self.recip(fstat[:, 3:4], fstat[:, 2:3], reads=[bfs], writes=[bfs])
                    P.add("dve", lambda e, ot=ot: e.scalar_tensor_tensor(out=ot, in0=ot, scalar=fstat[:, 3:4], in1=FG,
                                                                       op0=ALU.mult, op1=ALU.mult),
                          reads=[bot, bfs, bFG], writes=[bot])
                    self.dma(self.out.ap()[t * 128:(t + 1) * 128, :], ot, reads=[bot])

        if layer == 0:
            self.attn_layer0(locals())
        else:
            FG = self.alloc(1024)
            bFG = Buf()
            self.dma(FG, self.fgain.ap().broadcast_to([128, D]), writes=[bFG])
            fjunk = self.alloc(1024, BF16)
            fstat = self.alloc(4)
            bfj, bfs = Buf(), Buf()
            self.attn_layer1(locals())

    def attn_layer0(self, L):
        P = self.P
        QM, G, YT, bQM, bG, bYT = L["QM"], L["G"], L["YT"], L["bQM"], L["bG"], L["bYT"]
        sbank, obank, ptbuf, normalize, mem_attn, out_proj = (L["sbank"], L["obank"], L["ptbuf"], L["normalize"],
                                                              L["mem_attn"], L["out_proj"])
        KA = self.alloc(2 * S, BF16).rearrange("p (j t) -> p j t", j=2)
        KB = self.alloc(2 * S, BF16).rearrange("p (j t) -> p j t", j=2)
        VA = self.alloc(NT * 768, BF16).rearrange("p (t n) -> p t n", t=NT)
        bKA, bKB, bVA = Buf(), Buf(), Buf()
        self.dma(KA, self.KA_T.ap().rearrange("(j p) t -> p j t", p=128), writes=[bKA])
        for a in range(0, NT, 8):
            self.dma(VA[:, a:a + 8, :], self.VAB.ap().rearrange("(t p) n -> p t n", p=128)[:, a:a + 8, :], writes=[bVA])
        self.dma(KB, self.KB_T.ap().rearrange("(j p) t -> p j t", p=128), writes=[bKB])
        QA = self.alloc(4 * 2 * 512, BF16).rearrange("p (c e t) -> p c e t", c=4, e=2)
        QB = self.alloc(4 * 2 * 512, BF16).rearrange("p (c e t) -> p c e t", c=4, e=2)
        bQA, bQB = Buf(), Buf()
        for (qz, bq) in ((QA, bQA), (QB, bQB)):
            self.memset("pool", qz[64:128, :, 0, :], 0.0, writes=[bq])
            self.memset("pool", qz[0:64, :, 1, :], 0.0, writes=[bq])
        PTF = [(self.alloc(384), Buf()) for _ in range(2)]
        EB = self.alloc(8 * 384, BF16).rearrange("p (h n) -> p h n", h=8)
        bEB = Buf()
        oh1 = self.alloc(640)
        erb = self.alloc(8)
        esink = self.alloc(8)
        boh, berb, bes = Buf(), Buf(), Buf()
        oh1b = self.alloc(640, BF16)
        erbb = self.alloc(8, BF16)
        boh2, berb2 = Buf(), Buf()
        self.dma(oh1[0:32, :], self.c_oh1.ap(), writes=[boh])
        self.dma(erb[0:32, :], self.relb.ap(), writes=[berb])
        self.dma(esink, self.sink.ap().broadcast_to([128, 8]), writes=[bes])
        self.cp("dve", oh1b[0:32, :], oh1[0:32, :], reads=[boh], writes=[boh2])
        self.act(erbb[0:32, :], erb[0:32, :], AF.Exp, reads=[berb], writes=[berb2])
        self.act(esink, esink, AF.Exp, reads=[bes], writes=[bes])
        ps6 = self.PS[:, 0:3072]
        for g in range(3):
            for qq in range(128):
                u0 = (g - 1) * 128 - qq + 256
                idx = g * 128 + qq
                bk = idx // 64
                self.mm(ps6[:, idx * 8:(idx + 1) * 8], oh1b[0:32, u0:u0 + 128], erbb[0:32, 0:8], True, True,
                        reads=[boh2, berb2], writes=[self.bb[bk]])
        ps6v = ps6.rearrange("p (n h) -> p n h", h=8)
        for h in range(8):
            self.cp("dve" if h % 2 == 0 else "act", EB[:, h, :], ps6v[:, :, h], reads=self.bb[0:6], writes=[bEB])

        def normalize0(*a, **k):
            return normalize(*a, **k)

        def load_block(b, what):
            sl = slice(b * 512, (b + 1) * 512)
            if what == "QA":
                v = self.QA_T.ap().rearrange("(c p) t -> p c t", p=128)
                self.dma(QA[0:64, :, 0, :], v[0:64, :, sl], writes=[bQA])
                self.dma(QA[64:128, :, 1, :], v[64:128, :, sl], writes=[bQA])
            elif what == "QB":
                v = self.QB_T.ap().rearrange("(c p) t -> p c t", p=128)
                self.dma(QB[0:64, :, 0, :], v[0:64, :, sl], writes=[bQB])
                self.dma(QB[64:128, :, 1, :], v[64:128, :, sl], writes=[bQB])
            elif what == "QM":
                self.dma(QM, self.QM_T.ap().rearrange("(c p) t -> p c t", p=128)[:, :, sl], writes=[bQM])
            else:
                self.dma(G, self.G_T.ap().rearrange("(c p) t -> p c t", p=128)[:, :, sl], writes=[bG])

        for w in ("QA", "G", "QB", "QM"):
            load_block(0, w)
        LA = 2
        for b in range(NB):
            for c in range(4):
                j = c // 2
                obs = obank(2)
                units = [(kt, hh) for kt in range(NT) for hh in range(2)]
                sbs = {}

                def qk(u):
                    kt, hh = u
                    sb, bsb = sbank()
                    self.mm(sb, KA[:, j, kt * 128:(kt + 1) * 128], QA[:, c, hh, :], True, True,
                            reads=[bKA, bQA], writes=[bsb])
                    sbs[u] = (sb, bsb)

                for u in units[:LA]:
                    qk(u)
                for idx, u in enumerate(units):
                    if idx + LA < len(units):
                        qk(units[idx + LA])
                    kt, hh = u
                    sb, bsb = sbs.pop(u)
                    pt, bpt = ptbuf()
                    self.act(pt[:, 0:512], sb, AF.Exp, reads=[bsb], writes=[bpt], scale=0.125)
                    ob, bob = obs[hh]
                    v0 = j * 192 + hh * 64
                    self.mm(ob, VA[:, kt, v0:v0 + 128], pt[:, 0:512], kt == 0, kt == NT - 1, reads=[bVA, bpt], writes=[bob])
                for hh in range(2):
                    ob, bob = obs[hh]
                    normalize(ob, bob, (hh * 64, hh * 64 + 64), ((1 - hh) * 64, (1 - hh) * 64 + 64), c, mode="dve", yeng="pool")
            if b + 1 < NB:
                load_block(b + 1, "QA")
            bunits = [(h, ql) for h in range(8) for ql in range(4)]
            bsbs, bobs = {}, {}

            def b_stage1(u):
                h, ql = u
                c, hh, j = h // 2, h % 2, h // 4
                i = b * 4 + ql
                gs = [g for g in range(3) if 0 <= i + g - 1 < NT]
                sb, bsb = sbank()
                for g in gs:
                    kt = i + g - 1
                    self.mm(sb[:, g * 128:(g + 1) * 128], KB[:, j, kt * 128:(kt + 1) * 128],
                            QB[:, c, hh, ql * 128:(ql + 1) * 128], True, True, reads=[bKB, bQB], writes=[bsb])
                bsbs[u] = (sb, bsb, gs)

            def b_stage2(u, n):
                h, ql = u
                c, hh, j = h // 2, h % 2, h // 4
                r0, r1 = hh * 64, hh * 64 + 64
                i = b * 4 + ql
                if ql == 0:
                    bobs[h] = obank(1)[0]
                ob, bob = bobs[h]
                sb, bsb, gs = bsbs.pop(u)
                c0, c1 = gs[0] * 128, (gs[-1] + 1) * 128
                pf, bpf = PTF[n % 2]
                self.act(pf[:, c0:c1], sb[:, c0:c1], AF.Exp, reads=[bsb], writes=[bpf], scale=0.125)
                pt, bpt = ptbuf()
                self.tt("dve", pt[:, c0:c1], pf[:, c0:c1], EB[:, h, c0:c1], ALU.mult, reads=[bpf, bEB], writes=[bpt])
                v0 = (2 + j) * 192 + hh * 64
                for g in gs:
                    kt = i + g - 1
                    self.mm(ob[:, ql * 128:(ql + 1) * 128], VA[:, kt, v0:v0 + 128], pt[:, g * 128:(g + 1) * 128],
                            g == gs[0], g == gs[-1], reads=[bVA, bpt], writes=[bob])
                if ql == 3:
                    d0 = (1 - hh) * 64
                    pend.append((n + 2, lambda: normalize(ob, bob, (r0, r1), (d0, d0 + 64), 4 + c,
                                                          add_scalar=(esink[d0:d0 + 64, h:h + 1], bes))))

            LB = 3
            pend = []
            for u in bunits[:LB]:
                b_stage1(u)
            for n, u in enumerate(bunits):
                if n + LB < len(bunits):
                    b_stage1(bunits[n + LB])
                b_stage2(u, n)
                while pend and pend[0][0] <= n:
                    pend.pop(0)[1]()
            while pend:
                pend.pop(0)[1]()
            if b + 1 < NB:
                load_block(b + 1, "QB")
            mem_attn(b)
            if b + 1 < NB:
                load_block(b + 1, "QM")
            out_proj(b)
            if b + 1 < NB:
                load_block(b + 1, "G")

    def attn_layer1(self, L):
        P = self.P
        QM, G, YT, bQM, bG, bYT = L["QM"], L["G"], L["YT"], L["bQM"], L["bG"], L["bYT"]
        sbank, obank, ptbuf, normalize, mem_attn, out_proj = (L["sbank"], L["obank"], L["ptbuf"], L["normalize"],
                                                              L["mem_attn"], L["out_proj"])
        st = L["st"]
        E = self.alloc(16 * 9 * 128, BF16).rearrange("p (h t q) -> p h t q", h=16, t=9)
        bE = Buf()
        save = self.aoff
        erT = self.alloc(240)
        berT = Buf()
        self.dma(erT[0:31, :], self.rpbT.ap(), writes=[berT])
        self.act(erT[0:31, :], erT[0:31, :], AF.Exp, reads=[berT], writes=[berT])
        self.memset("pool", E.rearrange("p h t q -> p (h t q)"), 0.0, writes=[bE])
        ohp = [(self.alloc(1024), Buf()) for _ in range(2)]
        tiles = [(-3, False), (-2, False), (-2, True), (-1, False), (0, False), (1, False), (2, True), (2, False), (3, False)]
        ps4 = self.PS[:, 0:2048].rearrange("p (q n) -> p q n", n=256)
        cnt = 0
        for r in range(8):
            oh, boh = ohp[r % 2]
            self.dma(oh[0:31, :], self.c_ohc.ap()[:, r * 1024:(r + 1) * 1024], writes=[boh])
            for i in range(8):
                self.mm(self.PS[:, i * 256:i * 256 + 240], oh[0:31, i * 128:(i + 1) * 128], erT[0:31, 0:240], True, True,
                        reads=[boh, berT], writes=[self.bb[i // 2]])
            for ti, (c, masked) in enumerate(tiles):
                for kl in range(2):
                    for ql in range(2):
                        dkr = 2 * c + kl - ql
                        if abs(dkr) > 7:
                            continue
                        if masked and not (-4 <= dkr <= 3):
                            continue
                        dr = dkr + 7
                        src = ps4[kl * 64:(kl + 1) * 64, :, 0:240].rearrange("p q (h r) -> p h q r", r=15)[:, :, :, dr]
                        dst = E[kl * 64:(kl + 1) * 64, :, ti, ql * 64 + r * 8:ql * 64 + r * 8 + 8]
                        self.cp("dve" if cnt % 2 == 0 else "act", dst, src, reads=self.bb[0:4], writes=[bE])
                        cnt += 1
        self.P.barrier()
        self.aoff = save
        R = 12
        KR = self.alloc(8 * R * 128, BF16).rearrange("p (c t) -> p c t", c=8)
        VR = self.alloc(R * 1536, BF16).rearrange("p (s n) -> p s n", s=R)
        bKR = [Buf() for _ in range(R)]
        bVR = [Buf() for _ in range(R)]
        QC = self.alloc(8 * 2 * 512, BF16).rearrange("p (c e t) -> p c e t", c=8, e=2)
        bQC = Buf()
        self.memset("pool", QC[64:128, :, 0, :], 0.0, writes=[bQC])
        self.memset("pool", QC[0:64, :, 1, :], 0.0, writes=[bQC])
        sslot = [Buf() for _ in range(20)]
        PTF = [(self.alloc(640), Buf()) for _ in range(2)]
        kview = self.KC_T.ap().rearrange("(c p) t -> p c t", p=128)

        def load_tile(kt):
            s_ = kt % R
            self.dma(KR[:, :, s_ * 128:(s_ + 1) * 128], kview[:, :, kt * 128:(kt + 1) * 128], writes=[bKR[s_]])
            self.dma(VR[:, s_, :], self.VC.ap()[kt * 128:(kt + 1) * 128, :], writes=[bVR[s_]])

        def load_block(b, what):
            sl = slice(b * 512, (b + 1) * 512)
            if what == "QC":
                v = self.QC_T.ap().rearrange("(c p) t -> p c t", p=128)
                self.dma(QC[0:64, :, 0, :], v[0:64, :, sl], writes=[bQC])
                self.dma(QC[64:128, :, 1, :], v[64:128, :, sl], writes=[bQC])
            elif what == "QM":
                self.dma(QM, self.QM_T.ap().rearrange("(c p) t -> p c t", p=128)[:, :, sl], writes=[bQM])
            else:
                self.dma(G, self.G_T.ap().rearrange("(c p) t -> p c t", p=128)[:, :, sl], writes=[bG])

        def window(qt):
            if qt == 0:
                return [(0, 4), (1, 5), (2, 7), (3, 8)]
            if qt == 1:
                return [(0, 3), (1, 4), (2, 5), (3, 7)]
            if qt == 30:
                return [(28, 1), (29, 3), (30, 4), (31, 5)]
            if qt == 31:
                return [(28, 0), (29, 1), (30, 3), (31, 4)]
            return [(qt - 2 + i, 2 + i) for i in range(5)]

        for w in ("QC", "G", "QM"):
            load_block(0, w)
        for kt in range(0, 6):
            load_tile(kt)
        for b in range(NB):
            if b + 1 < NB:
                for kt in range(4 * b + 6, min(4 * b + 10, NT)):
                    load_tile(kt)
            cunits = [(h, ql) for h in range(16) for ql in range(4)]
            cst, cobs = {}, {}

            def c_stage1(u, n):
                h, ql = u
                c, hh = h // 2, h % 2
                win = window(b * 4 + ql)
                base = (n % 4) * 5
                for i, (kt, ti) in enumerate(win):
                    s_ = kt % R
                    col = (base + i) * 128
                    self.mm(self.PS[:, col:col + 128], KR[:, c, s_ * 128:(s_ + 1) * 128],
                            QC[:, c, hh, ql * 128:(ql + 1) * 128], True, True,
                            reads=[bKR[s_], bQC], writes=[sslot[base + i]])
                cst[u] = (win, base)

            def c_stage2(u, n):
                h, ql = u
                c, hh = h // 2, h % 2
                r0, r1 = hh * 64, hh * 64 + 64
                if ql == 0:
                    cobs[h] = obank(1)[0]
                ob, bob = cobs[h]
                win, base = cst.pop(u)
                nw = len(win)
                sb2 = self.PS[:, base * 128:(base + nw) * 128]
                pf, bpf = PTF[n % 2]
                self.act(pf[:, 0:nw * 128], sb2, AF.Exp, reads=sslot[base:base + nw], writes=[bpf], scale=0.125)
                pt, bpt = ptbuf()
                i0 = 0
                while i0 < nw:
                    i1 = i0
                    while i1 + 1 < nw and win[i1 + 1][1] == win[i1][1] + 1:
                        i1 += 1
                    t0, t1 = win[i0][1], win[i1][1]
                    ev = E[:, h, t0:t1 + 1, :].rearrange("p t q -> p (t q)")
                    self.tt("dve", pt[:, i0 * 128:(i1 + 1) * 128], pf[:, i0 * 128:(i1 + 1) * 128], ev, ALU.mult,
                            reads=[bpf, bE], writes=[bpt])
                    i0 = i1 + 1
                v0 = c * 192 + hh * 64
                for i, (kt, ti) in enumerate(win):
                    s_ = kt % R
                    self.mm(ob[:, ql * 128:(ql + 1) * 128], VR[:, s_, v0:v0 + 128], pt[:, i * 128:(i + 1) * 128],
                            i == 0, i == nw - 1, reads=[bVR[s_], bpt], writes=[bob])
                if ql == 3:
                    d0 = (1 - hh) * 64
                    pend.append((n + 0,
                                 lambda: normalize(ob, bob, (r0, r1), (d0, d0 + 64), c, yeng="pool")))

            LC = 3
            pend = []
            for n, u in enumerate(cunits[:LC]):
                c_stage1(u, n)
            for n, u in enumerate(cunits):
                if n + LC < len(cunits):
                    c_stage1(cunits[n + LC], n + LC)
                c_stage2(u, n)
                while pend and pend[0][0] <= n:
                    pend.pop(0)[1]()
            while pend:
                pend.pop(0)[1]()
            if b + 1 < NB:
                load_block(b + 1, "QC")
            mem_attn(b)
            if b + 1 < NB:
                load_block(b + 1, "QM")
            out_proj(b)
            if b + 1 < NB:
                load_block(b + 1, "G")


def prep_inputs(inputs):
    f = lambda a: np.ascontiguousarray(np.asarray(a, dtype=np.float32))
    x = f(inputs["x"])
    mem = f(inputs["mem"])
    ng = f(inputs["norm_gain"])
    mg = f(inputs["mem_norm_gain"])
    gains = np.concatenate([ng[0].reshape(8, 128).T, ng[1].reshape(8, 128).T, mg.reshape(8, 128).T], axis=1)
    qkg = np.stack([np.tile(f(inputs["q_norm_a"])[0], 2), np.tile(f(inputs["k_norm_a"])[0], 2)], axis=1)
    shared = {
        "w_in_even": f(inputs["w_in_even"])[0], "w_in_odd": f(inputs["w_in_odd"])[0],
        "w_out_even": f(inputs["w_out_even"])[0], "w_out_odd": f(inputs["w_out_odd"])[0],
        "w_mem_kv": f(inputs["w_mem_kv"]),
        "gains_pp": f(gains), "qk_gain": f(qkg),
        "final_gain": f(inputs["final_norm_gain"]).reshape(1, D),
        "sink_b": f(inputs["sink_b"]).reshape(1, 8),
        "rel_bias": f(inputs["rel_bias"]),
        "rpbT": f(np.transpose(f(inputs["rpb_c"])[0], (2, 0, 1)).reshape(31, 240)),
    }
    shared.update(host_consts())
    in_maps = []
    for b in range(8):
        m = dict(shared)
        m["x"] = x[b]
        m["mem"] = mem[b]
        in_maps.append(m)
    return in_maps


def kernel(**inputs):
    bld = Builder()
    nc = bld.build()
    in_maps = prep_inputs(inputs)
    res = run_bass_kernel_spmd(nc, in_maps, core_ids=list(range(8)))
    return np.stack([np.asarray(r["out"]) for r in res.results], axis=0).astype(np.float32)
```

```python
from contextlib import ExitStack
import math
import numpy as np
import concourse.bass as bass
import concourse.mybir as mybir
from concourse.bass_utils import run_bass_kernel_spmd

F32 = mybir.dt.float32
BF16 = mybir.dt.bfloat16
ALU = mybir.AluOpType
AF = mybir.ActivationFunctionType
AX = mybir.AxisListType

ENGS = ("pe", "act", "dve", "pool", "sp")
DMA_SLOTS = 12

S = 4096
D = 1024
NT = 32
NB = 8
EPS = 1e-6


class Buf:
    __slots__ = ("w", "r", "name")

    def __init__(self, name=""):
        self.w = None
        self.r = []
        self.name = name


class Op:
    __slots__ = ("eng", "fn", "deps", "needs_inc", "inc_val", "dma", "slot", "slot_val", "id")

    def __init__(self, eng, fn, dma):
        self.eng = eng
        self.fn = fn
        self.dma = dma
        self.deps = set()
        self.needs_inc = False
        self.inc_val = None
        self.slot = None
        self.slot_val = None
        self.id = None


class Prog:
    def __init__(self, nc):
        self.nc = nc
        self.ops = []
        self.dma_count = {e: 0 for e in ENGS}
        self.dma_hist = {e: [] for e in ENGS}
        self.last_op = {e: None for e in ENGS}

    def add(self, eng, fn, reads=(), writes=(), dma=False):
        op = Op(eng, fn, dma)
        op.id = len(self.ops)
        deps = set()
        for b in reads:
            if b.w is not None:
                deps.add(b.w)
        for b in writes:
            if b.w is not None:
                deps.add(b.w)
            for r in b.r:
                deps.add(r)
        for b in reads:
            b.r.append(op)
        for b in writes:
            b.w = op
            b.r = []
        for d in deps:
            if d is op:
                continue
            if (not d.dma) and (not dma) and d.eng == "pe" and eng == "pe":
                continue
            op.deps.add(d)
            if not d.dma:
                d.needs_inc = True
        if dma:
            n = self.dma_count[eng]
            op.slot = n % DMA_SLOTS
            op.slot_val = 16 * (n // DMA_SLOTS + 1)
            hist = self.dma_hist[eng]
            if n >= DMA_SLOTS:
                op.deps.add(hist[n - DMA_SLOTS])
            hist.append(op)
            self.dma_count[eng] = n + 1
        else:
            self.last_op[eng] = op
        self.ops.append(op)
        return op

    def barrier(self):
        lasts = [self.last_op[e] for e in ENGS if self.last_op[e] is not None]
        dmas = []
        for e in ENGS:
            dmas += self.dma_hist[e][-DMA_SLOTS:]
        for e in ENGS:
            op = Op(e, None, False)
            op.id = len(self.ops)
            for d in lasts:
                if d.eng != e:
                    op.deps.add(d)
                    d.needs_inc = True
            for d in dmas:
                op.deps.add(d)
            self.ops.append(op)

    def emit(self):
        nc = self.nc
        with ExitStack() as st:
            sem = {e: st.enter_context(nc.semaphore("c_" + e)) for e in ENGS}
            dsem = {}
            for e in ENGS:
                if self.dma_count[e] > 0:
                    dsem[e] = [st.enter_context(nc.semaphore("d_%s_%d" % (e, i))) for i in range(DMA_SLOTS)]
            cnt = {e: 0 for e in ENGS}
            for op in self.ops:
                if op.dma or op.fn is None:
                    continue
                if op.needs_inc:
                    cnt[op.eng] += 1
                    op.inc_val = cnt[op.eng]
            per_eng = {e: [] for e in ENGS}
            seen = {e: {} for e in ENGS}
            for op in self.ops:
                waits = {}
                for d in op.deps:
                    if d.dma:
                        key = ("d", d.eng, d.slot)
                        val = d.slot_val
                    else:
                        key = ("c", d.eng)
                        val = d.inc_val
                    if waits.get(key, 0) < val:
                        waits[key] = val
                wl = []
                s = seen[op.eng]
                for key, val in waits.items():
                    if s.get(key, 0) >= val:
                        continue
                    s[key] = val
                    wl.append((key, val))
                per_eng[op.eng].append((op, wl))
            block = st.enter_context(nc.Block())

            def run(engname, e):
                for op, wl in per_eng[engname]:
                    for key, val in wl:
                        if key[0] == "d":
                            e.wait_ge(dsem[key[1]][key[2]], val)
                        else:
                            e.wait_ge(sem[key[1]], val)
                    if op.fn is None:
                        continue
                    ins = op.fn(e)
                    if op.dma:
                        ins.then_inc(dsem[engname][op.slot], 16)
                    elif op.needs_inc:
                        ins.then_inc(sem[engname], 1)

            @block.tensor
            def _(e):
                run("pe", e)

            @block.scalar
            def _(e):
                run("act", e)

            @block.vector
            def _(e):
                run("dve", e)

            @block.gpsimd
            def _(e):
                run("pool", e)

            @block.sync
            def _(e):
                run("sp", e)


def _t5_bucket(rel):
    nb = 16
    max_exact = 8
    ret = np.where(rel > 0, nb, 0)
    n = np.abs(rel)
    nf = np.maximum(n, 1).astype(np.float32)
    large = max_exact + (np.log(nf / np.float32(max_exact)) / np.float32(math.log(128 / max_exact))
                         * np.float32(nb - max_exact)).astype(np.int32)
    large = np.minimum(large, nb - 1)
    return ret + np.where(n < max_exact, n, large)


def host_consts():
    c = {}
    c["ident"] = np.eye(128, dtype=np.float32)
    R = np.zeros((128, 128), np.float32)
    for p in range(128):
        sub = p % 32
        if sub < 16:
            R[p, p + 16] = -1.0
        else:
            R[p, p - 16] = 1.0
    c["rotT"] = np.ascontiguousarray(R.T)
    bd = np.zeros((128, 128), np.float32)
    bd[0:64, 0:64] = 1.0
    bd[64:128, 64:128] = 1.0
    c["bd"] = bd
    t = np.arange(S)
    row = (t // 64).astype(np.float32)
    col = (t % 64).astype(np.float32)
    freqs = np.power(np.float32(10000.0), -np.arange(16, dtype=np.float32) / np.float32(16)).astype(np.float32)
    cosT = np.zeros((128, S), np.float32)
    sinT = np.zeros((128, S), np.float32)
    for p in range(128):
        dh = p % 64
        pos = row if dh < 32 else col
        ang = (pos * freqs[dh % 16]).astype(np.float32)
        cosT[p] = np.cos(ang)
        sinT[p] = np.sin(ang)
    c["cosT"] = cosT
    c["sinT"] = sinT
    rel = np.arange(-256, 384)
    bk = _t5_bucket(rel)
    oh1 = np.zeros((32, 640), np.float32)
    for u, r in enumerate(rel):
        if abs(r) <= 128:
            oh1[bk[u], u] = 1.0
    c["oh1"] = oh1
    ohc = np.zeros((31, 64, 128), np.float32)
    for qc in range(64):
        cs = min(max(qc - 8, 0), 48)
        for kc in range(cs, cs + 16):
            dc = kc - qc + 15
            ohc[dc, qc, kc] = 1.0
            ohc[dc, qc, 64 + kc] = 1.0
    c["ohc"] = ohc.reshape(31, 64 * 128)
    return c


class Builder:
    def __init__(self, debug=None, stop_after=None):
        self.debug = debug or ()
        self.stop_after = stop_after
        self.nc = bass.Bass("TRN2", target_bir_lowering=False)
        self.P = Prog(self.nc)
        self.dram = {}

    def din(self, name, shape, dt=F32):
        t = self.nc.dram_tensor(name, list(shape), dt, kind="ExternalInput")
        self.dram[name] = t
        return t

    def dscratch(self, name, shape, dt):
        kind = "ExternalOutput" if name in self.debug else "Internal"
        t = self.nc.dram_tensor(name, list(shape), dt, kind=kind)
        self.dram[name] = t
        return t

    def reset_arena(self):
        self.aoff = 0

    def alloc(self, ncols, dt=F32):
        nbytes = ncols * (4 if dt == F32 else 2)
        n32 = (nbytes + 3) // 4
        n32 = (n32 + 1) // 2 * 2
        a = self.ARENA[:, self.aoff:self.aoff + n32]
        self.aoff += n32
        assert self.aoff <= self.ARENA_N, "arena overflow %d > %d" % (self.aoff, self.ARENA_N)
        if dt == F32:
            return a[:, 0:ncols]
        return a.bitcast(BF16)[:, 0:ncols]

    def dma(self, out, in_, reads=(), writes=(), q="sp", **kw):
        return self.P.add(q, lambda e: e.dma_start(out=out, in_=in_, **kw), reads, writes, dma=True)

    def mm(self, out, lhsT, rhs, start, stop, reads=(), writes=()):
        return self.P.add("pe", lambda e: e.matmul(out, lhsT=lhsT, rhs=rhs, start=start, stop=stop), reads, writes)

    def tr(self, out, in_, ident, reads=(), writes=()):
        return self.P.add("pe", lambda e: e.transpose(out, in_, ident), reads, writes)

    def act(self, out, in_, func, reads=(), writes=(), **kw):
        return self.P.add("act", lambda e: e.activation(out=out, in_=in_, func=func, **kw), reads, writes)

    def tt(self, eng, out, in0, in1, op, reads=(), writes=()):
        return self.P.add(eng, lambda e: e.tensor_tensor(out=out, in0=in0, in1=in1, op=op), reads, writes)

    def ts(self, eng, out, in0, s1, op0, reads=(), writes=(), s2=None, op1=None):
        if op1 is None:
            return self.P.add(eng, lambda e: e.tensor_scalar(out=out, in0=in0, scalar1=s1, scalar2=None, op0=op0), reads, writes)
        return self.P.add(eng, lambda e: e.tensor_scalar(out=out, in0=in0, scalar1=s1, scalar2=s2, op0=op0, op1=op1), reads, writes)

    def cp(self, eng, out, in_, reads=(), writes=()):
        if eng == "act":
            return self.P.add("act", lambda e: e.copy(out=out, in_=in_), reads, writes)
        return self.P.add(eng, lambda e: e.tensor_copy(out=out, in_=in_), reads, writes)

    def memset(self, eng, ap, val, writes=()):
        return self.P.add(eng, lambda e: e.memset(ap, val), (), writes)

    def recip(self, out, in_, reads=(), writes=()):
        return self.P.add("dve", lambda e: e.reciprocal(out=out, in_=in_), reads, writes)

    def build(self):
        nc = self.nc
        self.x = self.din("x", [S, D])
        self.mem = self.din("mem", [256, D])
        self.w_in = [self.din("w_in_even", [D, 3584]), self.din("w_in_odd", [D, 5120])]
        self.w_out = [self.din("w_out_even", [1536, D]), self.din("w_out_odd", [1536, D])]
        self.w_mem = self.din("w_mem_kv", [2, D, 1024])
        self.gains = self.din("gains_pp", [128, 24])
        self.qkg = self.din("qk_gain", [128, 2])
        self.fgain = self.din("final_gain", [1, D])
        self.sink = self.din("sink_b", [1, 8])
        self.relb = self.din("rel_bias", [32, 8])
        self.rpbT = self.din("rpbT", [31, 240])
        self.c_ident = self.din("ident", [128, 128])
        self.c_rotT = self.din("rotT", [128, 128])
        self.c_bd = self.din("bd", [128, 128])
        self.c_cos = self.din("cosT", [128, S])
        self.c_sin = self.din("sinT", [128, S])
        self.c_oh1 = self.din("oh1", [32, 640])
        self.c_ohc = self.din("ohc", [31, 64 * 128])
        self.out = nc.dram_tensor("out", [S, D], F32, kind="ExternalOutput")
        self.QA_T = self.dscratch("QA_T", [512, S], BF16)
        self.KA_T = self.dscratch("KA_T", [256, S], BF16)
        self.QB_T = self.dscratch("QB_T", [512, S], BF16)
        self.KB_T = self.dscratch("KB_T", [256, S], BF16)
        self.QM_T = self.dscratch("QM_T", [512, S], BF16)
        self.G_T = self.dscratch("G_T", [1536, S], BF16)
        self.VAB = self.dscratch("VAB", [S, 768], BF16)
        self.X1 = self.dscratch("X1", [S, D], F32)
        self.QC_T = self.dscratch("QC_T", [1024, S], BF16)
        self.KC_T = self.dscratch("KC_T", [1024, S], BF16)
        self.VC = self.dscratch("VC", [S, 1536], BF16)
        self.MK_T = self.dscratch("MK_T", [2, 512, 256], BF16)
        self.MV = self.dscratch("MV", [2, 256, 512], BF16)

        with ExitStack() as st:
            self.ARENA_N = 52000
            self.ARENA = st.enter_context(nc.sbuf_tensor("arena", [128, self.ARENA_N], F32))
            self.PS = st.enter_context(nc.psum_tensor("ps", [128, 4096], F32))
            self.bank = [self.PS[:, i * 512:(i + 1) * 512] for i in range(8)]
            self.bb = [Buf("bank%d" % i) for i in range(8)]
            self.phase_mem()
            self.P.barrier()
            for layer in range(2):
                self.phase_proj(layer)
                self.P.barrier()
                if self.stop_after == ("proj", layer):
                    break
                self.phase_attn(layer)
                self.P.barrier()
                if self.stop_after == ("attn", layer):
                    break
            self.P.emit()
        return nc

    def load_consts(self):
        c = {}
        c["ident"] = self.alloc(128)
        c["b_ident"] = Buf()
        self.dma(c["ident"], self.c_ident.ap(), writes=[c["b_ident"]])
        c["gains"] = self.alloc(24)
        c["b_gains"] = Buf()
        self.dma(c["gains"], self.gains.ap(), writes=[c["b_gains"]])
        return c

    def norm_transpose(self, c, xt, bx, ht3, bht, j, gcol, tp_banks, btp, scr):
        junk, bjunk, stat, bstat = scr
        self.act(junk, xt, AF.Square, reads=[bx], writes=[bjunk, bstat], accum_out=stat[:, 0:1])
        self.ts("dve", stat[:, 1:2], stat[:, 0:1], 1.0 / D, ALU.mult, reads=[bstat], writes=[bstat], s2=EPS, op1=ALU.add)
        if False:
            self.act(stat[:, 2:3], stat[:, 1:2], AF.Ln, reads=[bstat], writes=[bstat])
            self.act(stat[:, 3:4], stat[:, 2:3], AF.Exp, reads=[bstat], writes=[bstat], scale=-0.5)
        else:
            self.act(stat[:, 2:3], stat[:, 1:2], AF.Sqrt, reads=[bstat], writes=[bstat])
            self.recip(stat[:, 3:4], stat[:, 2:3], reads=[bstat], writes=[bstat])
        self.ts("dve", xt, xt, stat[:, 3:4], ALU.mult, reads=[bx, bstat], writes=[bx])
        tp = tp_banks
        for kc in range(8):
            self.tr(tp[:, kc * 128:(kc + 1) * 128], xt[:, kc * 128:(kc + 1) * 128], c["ident"],
                    reads=[bx, c["b_ident"]], writes=list(btp))
        g = c["gains"][:, gcol:gcol + 8]
        gb = g.unsqueeze(2).broadcast_to([128, 8, 128])
        tp3 = tp.rearrange("p (a b) -> p a b", a=8)
        self.tt("dve", ht3[:, :, j * 128:(j + 1) * 128], tp3, gb, ALU.mult,
                reads=list(btp) + [c["b_gains"]], writes=[bht])

    def phase_mem(self):
        self.reset_arena()
        c = self.load_consts()
        W = self.alloc(2 * 8 * 1024, BF16).rearrange("p (l k n) -> p l k n", l=2, k=8)
        bW = Buf()
        for l in range(2):
            self.dma(W[:, l], self.w_mem.ap()[l].rearrange("(k p) n -> p k n", p=128), writes=[bW], q="pool")
        ht3 = self.alloc(8 * 256, BF16).rearrange("p (k t) -> p k t", k=8)
        bht = Buf()
        junk = self.alloc(1024, BF16)
        scr = (junk, Buf(), self.alloc(4), Buf())
        tpb = self.PS[:, 0:1024]
        btp = [self.bb[0], self.bb[1]]
        for j in range(2):
            xt = self.alloc(1024)
            bx = Buf()
            self.dma(xt, self.mem.ap()[j * 128:(j + 1) * 128, :], writes=[bx])
            self.norm_transpose(c, xt, bx, ht3, bht, j, 16, tpb, btp, scr)
        stg = self.alloc(4 * 256, BF16).rearrange("p (c t) -> p c t", c=4)
        stv = self.alloc(2 * 512, BF16).rearrange("p (c t) -> p c t", c=2)
        bst, bsv = Buf(), Buf()
        for l in range(2):
            for hm in range(4):
                pb = self.bank[2 + hm % 2]
                bpb = self.bb[2 + hm % 2]
                for kc in range(8):
                    self.mm(pb[:, 0:256], W[:, l, kc, hm * 128:(hm + 1) * 128], ht3[:, kc, :], kc == 0, kc == 7,
                            reads=[bW, bht], writes=[bpb])
                self.cp("dve", stg[:, hm, :], pb[:, 0:256], reads=[bpb], writes=[bst])
            self.dma(self.MK_T.ap()[l].rearrange("(c p) t -> p c t", p=128), stg, reads=[bst])
            for mt in range(2):
                pb = self.bank[4 + mt]
                bpb = self.bb[4 + mt]
                for kc in range(8):
                    self.mm(pb, ht3[:, kc, mt * 128:(mt + 1) * 128], W[:, l, kc, 512:1024], kc == 0, kc == 7,
                            reads=[bW, bht], writes=[bpb])
                self.cp("act", stv[:, mt, :], pb, reads=[bpb], writes=[bsv])
            self.dma(self.MV.ap()[l].rearrange("(c p) n -> p c n", p=128), stv, reads=[bsv])

    def phase_proj(self, layer):
        self.reset_arena()
        c = self.load_consts()
        if layer == 0:
            wl = [(0, 512, 0)]
            wl += [(512, 64, 512), (512, 64, 576), (576, 64, 640), (576, 64, 704)]
            wl += [(768, 512, 768)]
            wl += [(1280, 64, 1280), (1280, 64, 1344), (1344, 64, 1408), (1344, 64, 1472)]
            wl += [(1536, 512, 1536), (2048, 1536, 2048), (640, 128, 3584), (1408, 128, 3712)]
            NC = 3840
            groups = [("qa", 0, 4, self.QA_T, "rope_q"), ("ka", 512, 2, self.KA_T, "rope_k"),
                      ("qb", 768, 4, self.QB_T, "copy"), ("kb", 1280, 2, self.KB_T, "copy"),
                      ("qm", 1536, 4, self.QM_T, "copy"), ("gate", 2048, 12, self.G_T, "silu")]
            vcol, vn = 3584, 256
        else:
            wl = [(0, 2048, 0), (3072, 2048, 2048), (2048, 1024, 4096)]
            NC = 5120
            groups = [("qc", 0, 8, self.QC_T, "copy"), ("kc", 1024, 8, self.KC_T, "copy"),
                      ("qm", 2048, 4, self.QM_T, "copy"), ("gate", 2560, 12, self.G_T, "silu")]
            vcol, vn = 4096, 1024
        W = self.alloc(8 * NC, BF16).rearrange("p (k n) -> p k n", k=8)
        wsrc = self.w_in[layer].ap().rearrange("(k p) n -> p k n", p=128)
        wpieces = []
        for (s0, n, d0) in wl:
            for a in range(0, n, 1024):
                m = min(1024, n - a)
                pb_ = Buf()
                self.dma(W[:, :, d0 + a:d0 + a + m], wsrc[:, :, s0 + a:s0 + a + m], writes=[pb_], q="pool")
                wpieces.append((d0 + a, d0 + a + m, pb_))

        def wbuf(c0, c1):
            return [b_ for (p0, p1, b_) in wpieces if p0 < c1 and c0 < p1]
        if layer == 0:
            rotT = self.alloc(128)
            bd = self.alloc(128)
            qkg = self.alloc(2)
            bcst = Buf()
            self.dma(rotT, self.c_rotT.ap(), writes=[bcst])
            self.dma(bd, self.c_bd.ap(), writes=[bcst])
            self.dma(qkg, self.qkg.ap(), writes=[bcst])
            epsT = self.alloc(2)
            self.memset("pool", epsT, EPS, writes=[bcst])
            cs = [(self.alloc(512), self.alloc(512), Buf()) for _ in range(2)]
            ropeT = [(self.alloc(512), self.alloc(512), self.alloc(512), self.alloc(512), Buf(), Buf(), Buf(), Buf())
                     for _ in range(2)]
        XT = [self.alloc(1024) for _ in range(4)]
        bXT = [Buf() for _ in range(4)]
        HT = [self.alloc(8 * 512, BF16).rearrange("p (k t) -> p k t", k=8) for _ in range(2)]
        bHT = [Buf() for _ in range(2)]
        junk = self.alloc(1024, BF16)
        scr = (junk, Buf(), self.alloc(4), Buf())
        tpb = self.PS[:, 0:1024]
        btp = [self.bb[0], self.bb[1]]
        stg = {}
        for (name, col, nch, dst, kind) in groups:
            stg[name] = [(self.alloc(nch * 512, BF16).rearrange("p (c t) -> p c t", c=nch), Buf()) for _ in range(2)]
        if layer == 0:
            VST = [(self.alloc(768, BF16), Buf()) for _ in range(2)]
        else:
            VST = [(self.alloc(1536, BF16), Buf()) for _ in range(2)]
        for (v, bv) in VST:
            self.memset("pool", v, 1.0, writes=[bv])
        pbanks = [self.bank[2], self.bank[3], self.bank[4]]
        bpb = [self.bb[2], self.bb[3], self.bb[4]]
        aux = [self.bank[5], self.bank[6]]
        baux = [self.bb[5], self.bb[6]]
        vbank = self.bank[7]
        bvb = self.bb[7]
        xsrc = self.x if layer == 0 else self.X1
        gcol = 0 if layer == 0 else 8

        def load_x(b):
            for j in range(4):
                t = b * 4 + j
                self.dma(XT[j], xsrc.ap()[t * 128:(t + 1) * 128, :], writes=[bXT[j]])

        def norm_tr(b):
            for j in range(4):
                self.norm_transpose(c, XT[j], bXT[j], HT[b % 2], bHT[b % 2], j, gcol, tpb, btp, scr)

        state = {"pi": 0, "vi": 0, "ri": 0, "vb": 0}
        pending = []

        def do_chunk(b, name, col, ci, kind, sbuf, bs):
            i = state["pi"] % 3
            state["pi"] += 1
            pb, bp = pbanks[i], bpb[i]
            ht, bh = HT[b % 2], bHT[b % 2]
            for kc in range(8):
                self.mm(pb, W[:, kc, col + ci * 128:col + (ci + 1) * 128], ht[:, kc, :], kc == 0, kc == 7,
                        reads=wbuf(col + ci * 128, col + (ci + 1) * 128) + [bh], writes=[bp])
            while pending:
                pending.pop(0)()
            dst = sbuf[:, ci, :]
            if kind == "copy":
                self.cp("act" if (state["pi"] % 2 == 0) else "dve", dst, pb, reads=[bp], writes=[bs])
            elif kind == "silu":
                self.act(dst, pb, AF.Silu, reads=[bp], writes=[bs])
            else:
                gi = 0 if kind == "rope_q" else 1
                cosb, sinb, bcs = cs[b % 2]
                tq, tsq, t1, trs, btq, btsq, bt1, btrs = ropeT[state["ri"] % 2]
                state["ri"] += 1
                self.act(tq, pb, AF.Copy, reads=[bp, bcst], writes=[btq], scale=qkg[:, gi:gi + 1])
                self.act(tsq, pb, AF.Square, reads=[bp], writes=[btsq])

                def tail():
                    self.mm(aux[0], bd, tsq, True, True, reads=[bcst, btsq], writes=[baux[0]])
                    self.mm(aux[1], rotT, tq, True, True, reads=[bcst, btq], writes=[baux[1]])
                    if True:
                        self.act(trs, aux[0], AF.Ln, reads=[baux[0], bcst], writes=[btrs], scale=1.0 / 64, bias=epsT[:, 0:1])
                        self.act(trs, trs, AF.Exp, reads=[btrs], writes=[btrs], scale=-0.5)
                    else:
                        self.ts("dve", trs, aux[0], 1.0 / 64, ALU.mult, reads=[baux[0]], writes=[btrs], s2=EPS, op1=ALU.add)
                        self.act(trs, trs, AF.Sqrt, reads=[btrs], writes=[btrs])
                        self.recip(trs, trs, reads=[btrs], writes=[btrs])
                    self.tt("dve", t1, tq, cosb, ALU.mult, reads=[btq, bcs], writes=[bt1])
                    self.tt("dve", tsq, aux[1], sinb, ALU.mult, reads=[baux[1], bcs, btsq], writes=[btsq])
                    self.tt("dve", t1, t1, tsq, ALU.add, reads=[bt1, btsq], writes=[bt1])
                    self.tt("dve", dst, t1, trs, ALU.mult, reads=[bt1, btrs], writes=[bs])

                pending.append(tail)

        def do_v(b):
            ht, bh = HT[b % 2], bHT[b % 2]
            for j in range(4):
                t = b * 4 + j
                v, bv = VST[state["vi"] % 2]
                state["vi"] += 1
                if layer == 0:
                    for kc in range(8):
                        self.mm(vbank[:, 0:256], ht[:, kc, j * 128:(j + 1) * 128], W[:, kc, vcol:vcol + 256],
                                kc == 0, kc == 7, reads=wbuf(vcol, vcol + 256) + [bh], writes=[bvb])
                    v3 = v.rearrange("p (g s) -> p g s", g=4)
                    src = vbank[:, 0:256].rearrange("p (g d) -> p g d", g=4)
                    self.cp("dve", v3[:, :, 0:64], src, reads=[bvb], writes=[bv])
                    self.cp("act", v3[:, :, 128:192], src, reads=[bvb], writes=[bv])
                    self.dma(self.VAB.ap()[t * 128:(t + 1) * 128, :], v, reads=[bv])
                else:
                    v3 = v.rearrange("p (g s) -> p g s", g=8)
                    for half in range(2):
                        vb_i = 5 + state["vb"] % 3
                        state["vb"] += 1
                        vbk, bvk = self.bank[vb_i], self.bb[vb_i]
                        for kc in range(8):
                            self.mm(vbk, ht[:, kc, j * 128:(j + 1) * 128],
                                    W[:, kc, vcol + half * 512:vcol + (half + 1) * 512],
                                    kc == 0, kc == 7, reads=wbuf(vcol + half * 512, vcol + (half + 1) * 512) + [bh], writes=[bvk])
                        src = vbk.rearrange("p (g e d) -> p g e d", g=4, e=2)
                        self.cp("dve", v3[:, half * 4:(half + 1) * 4, 0:64], src[:, :, 0, :], reads=[bvk], writes=[bv])
                        self.cp("act", v3[:, half * 4:(half + 1) * 4, 128:192], src[:, :, 1, :], reads=[bvk], writes=[bv])
                    self.dma(self.VC.ap()[t * 128:(t + 1) * 128, :], v, reads=[bv])

        chunks = []
        for (name, col, nch, dst, kind) in groups:
            for ci in range(nch):
                chunks.append((name, col, ci, nch, dst, kind))
        if layer == 0:
            rope = [ch for ch in chunks if ch[5].startswith("rope")]
            plain = [ch for ch in chunks if not ch[5].startswith("rope")]
            order = []
            while rope or plain:
                if rope:
                    order.append(rope.pop(0))
                for _ in range(3):
                    if plain:
                        order.append(plain.pop(0))
            chunks = order
        load_x(0)
        norm_tr(0)
        for b in range(NB):
            if layer == 0:
                cosb, sinb, bcs = cs[b % 2]
                self.dma(cosb, self.c_cos.ap()[:, b * 512:(b + 1) * 512], writes=[bcs])
                self.dma(sinb, self.c_sin.ap()[:, b * 512:(b + 1) * 512], writes=[bcs])
            if b + 1 < NB:
                load_x(b + 1)
            half_n = len(chunks) // 2
            done = {}
            stores = []
            for idx, (name, col, ci, nch, dst, kind) in enumerate(chunks):
                if idx == half_n and b + 1 < NB:
                    norm_tr(b + 1)
                sbuf, bs = stg[name][b % 2]
                do_chunk(b, name, col, ci, kind, sbuf, bs)
                done[name] = done.get(name, 0) + 1
                if done[name] == nch:
                    stores.append((dst, sbuf, bs, kind))
                keep = []
                for (d_, sb_, bs_, k_) in stores:
                    if k_.startswith("rope") and pending:
                        keep.append((d_, sb_, bs_, k_))
                    else:
                        self.dma(d_.ap().rearrange("(c p) t -> p c t", p=128)[:, :, b * 512:(b + 1) * 512], sb_, reads=[bs_])
                stores = keep
            while pending:
                pending.pop(0)()
            for (d_, sb_, bs_, k_) in stores:
                self.dma(d_.ap().rearrange("(c p) t -> p c t", p=128)[:, :, b * 512:(b + 1) * 512], sb_, reads=[bs_])
            do_v(b)

    def phase_attn(self, layer):
        self.reset_arena()
        P = self.P
        WOUT = self.alloc(12 * 1024, BF16).rearrange("p (c n) -> p c n", c=12)
        bWOUT = Buf()
        self.dma(WOUT, self.w_out[layer].ap().rearrange("(c p) n -> p c n", p=128), writes=[bWOUT], q="pool")
        ones_bf = self.alloc(128, BF16)
        bones = Buf()
        self.memset("pool", ones_bf, 1.0, writes=[bones])
        MKT = self.alloc(4 * 256, BF16).rearrange("p (c t) -> p c t", c=4)
        MVs = self.alloc(2 * 512, BF16).rearrange("p (c n) -> p c n", c=2)
        bMK, bMV = Buf(), Buf()
        self.dma(MKT, self.MK_T.ap()[layer].rearrange("(c p) t -> p c t", p=128), writes=[bMK])
        self.dma(MVs, self.MV.ap()[layer].rearrange("(c p) n -> p c n", p=128), writes=[bMV])
        QM = self.alloc(4 * 512, BF16).rearrange("p (c t) -> p c t", c=4)
        G = self.alloc(12 * 512, BF16).rearrange("p (c t) -> p c t", c=12)
        YT = self.alloc(12 * 512, BF16).rearrange("p (c t) -> p c t", c=12)
        bQM, bG = Buf(), Buf()
        bYT = [Buf() for _ in range(12)]
        nxb = 2 if layer == 0 else 1
        XR = [(self.alloc(1024), Buf()) for _ in range(nxb)]
        OUTT = [(self.alloc(1024), Buf()) for _ in range(nxb)]
        PT = [(self.alloc(640, BF16), Buf()) for _ in range(6)]
        RD = [(self.alloc(512), Buf()) for _ in range(2)]
        TN = [(self.alloc(512), Buf()) for _ in range(2)]
        st = {"s": 0, "pt": 0, "o": 0, "rd": 0, "x": 0}

        SB = [0, 1, 2, 3] if layer == 0 else [5]
        OB = [4, 5, 6, 7] if layer == 0 else [6, 7]

        def sbank():
            i = SB[st["s"] % len(SB)]
            st["s"] += 1
            return self.bank[i], self.bb[i]

        def obank(n=1):
            if n == 2 and st["o"] % 2 == 1:
                st["o"] += 1
            r = []
            for _ in range(n):
                i = OB[st["o"] % len(OB)]
                st["o"] += 1
                r.append((self.bank[i], self.bb[i]))
            return r

        def ptbuf():
            i = st["pt"] % 6
            st["pt"] += 1
            return PT[i]

        def normalize(ob, bob, o_rows, d_rows, ci, add_scalar=None, mode="act", yeng="dve"):
            i = st["rd"] % 2
            st["rd"] += 1
            rd, brd = RD[i]
            tn, btn = TN[i]
            o0, o1 = o_rows
            d0, d1 = d_rows
            if mode == "act":
                if add_scalar is not None:
                    self.act(rd[d0:d1, :], ob[d0:d1, :], AF.Ln, reads=[bob, add_scalar[1]], writes=[brd], bias=add_scalar[0])
                else:
                    self.act(rd[d0:d1, :], ob[d0:d1, :], AF.Ln, reads=[bob], writes=[brd])
                self.act(rd[d0:d1, :], rd[d0:d1, :], AF.Exp, reads=[brd], writes=[brd], scale=-1.0)
            else:
                if add_scalar is not None:
                    self.ts("dve", rd[d0:d1, :], ob[d0:d1, :], add_scalar[0], ALU.add, reads=[bob, add_scalar[1]], writes=[brd])
                    self.recip(rd[d0:d1, :], rd[d0:d1, :], reads=[brd], writes=[brd])
                else:
                    self.recip(rd[d0:d1, :], ob[d0:d1, :], reads=[bob], writes=[brd])
            self.tt("dve", tn[o0:o1, :], ob[o0:o1, :], rd[d0:d1, :], ALU.mult, reads=[bob, brd], writes=[btn])
            self.tt(yeng, YT[o0:o1, ci, :], tn[o0:o1, :], G[o0:o1, ci, :], ALU.mult, reads=[btn, bG], writes=[bYT[ci]])

        def mem_attn(b):
            sc = 128.0 ** -0.5
            for hm in range(4):
                (ob, bob), (db, bdb) = obank(2)
                pts = []
                for mt in range(2):
                    sb, bsb = sbank()
                    self.mm(sb, MKT[:, hm, mt * 128:(mt + 1) * 128], QM[:, hm, :], True, True, reads=[bMK, bQM], writes=[bsb])
                    pt, bpt = ptbuf()
                    self.act(pt[:, 0:512], sb, AF.Exp, reads=[bsb], writes=[bpt], scale=sc)
                    pts.append((pt, bpt))
                for mt in range(2):
                    pt, bpt = pts[mt]
                    self.mm(ob, MVs[:, mt, hm * 128:(hm + 1) * 128], pt[:, 0:512], mt == 0, mt == 1, reads=[bMV, bpt], writes=[bob])
                for mt in range(2):
                    pt, bpt = pts[mt]
                    self.mm(db, ones_bf, pt[:, 0:512], mt == 0, mt == 1, reads=[bones, bpt], writes=[bdb])
                i = st["rd"] % 2
                st["rd"] += 1
                rd, brd = RD[i]
                tn, btn = TN[i]
                self.act(rd, db, AF.Ln, reads=[bdb], writes=[brd])
                self.act(rd, rd, AF.Exp, reads=[brd], writes=[brd], scale=-1.0)
                self.tt("dve", tn, ob, rd, ALU.mult, reads=[bob, brd], writes=[btn])
                self.tt("dve" if layer == 0 else "pool", YT[:, 8 + hm, :], tn, G[:, 8 + hm, :], ALU.mult,
                        reads=[btn, bG], writes=[bYT[8 + hm]])

        def out_proj(b):
            for jt in range(4):
                t = b * 4 + jt
                xr, bxr = XR[st["x"] % nxb]
                ot, bot = OUTT[st["x"] % nxb]
                st["x"] += 1
                src = self.x if layer == 0 else self.X1
                self.dma(xr, src.ap()[t * 128:(t + 1) * 128, :], writes=[bxr])
                obs = obank(2)
                for nh in range(2):
                    ob, bob = obs[nh]
                    for cc in range(12):
                        self.mm(ob, YT[:, cc, jt * 128:(jt + 1) * 128], WOUT[:, cc, nh * 512:(nh + 1) * 512],
                                cc == 0, cc == 11, reads=[bYT[cc], bWOUT], writes=[bob])
                    self.tt("dve", ot[:, nh * 512:(nh + 1) * 512], ob, xr[:, nh * 512:(nh + 1) * 512], ALU.add,
                            reads=[bob, bxr], writes=[bot])
                if layer == 0:
                    self.dma(self.X1.ap()[t * 128:(t + 1) * 128, :], ot, reads=[bot])
                else:
                    self.act(fjunk, ot, AF.Square, reads=[bot], writes=[bfj, bfs], accum_out=fstat[:, 0:1])
                    self.ts("dve", fstat[:, 1:2], fstat[:, 0:1], 1.0 / D, ALU.mult, reads=[bfs], writes=[bfs], s2=EPS, op1=ALU.add)
                    if False:
                        self.act(fstat[:, 2:3], fstat[:, 1:2], AF.Ln, reads=[bfs], writes=[bfs])
                        self.act(fstat[:, 3:4], fstat[:, 2:3], AF.Exp, reads=[bfs], writes=[bfs], scale=-0.5)
                    else:
                        self.act(fstat[:, 2:3], fstat[:, 1:2], AF.Sqrt, reads=[bfs], writes=[bfs])
                        self.recip(fstat[:, 3:4], fstat[:, 2:3], reads=[bfs], writes=[bfs])
                    P.add("dve", lambda e, ot=ot: e.scalar_tensor_tensor(out=ot, in0=ot, scalar=fstat[:, 3:4], in1=FG,
                                                                       op0=ALU.mult, op1=ALU.mult),
                          reads=[bot, bfs, bFG], writes=[bot])
                    self.dma(self.out.ap()[t * 128:(t + 1) * 128, :], ot, reads=[bot])

        if layer == 0:
            self.attn_layer0(locals())
        else:
            FG = self.alloc(1024)
            bFG = Buf()
            self.dma(FG, self.fgain.ap().broadcast_to([128, D]), writes=[bFG])
            fjunk = self.alloc(1024, BF16)
            fstat = self.alloc(4)
            bfj, bfs = Buf(), Buf()
            self.attn_layer1(locals())

    def attn_layer0(self, L):
        P = self.P
        QM, G, YT, bQM, bG, bYT = L["QM"], L["G"], L["YT"], L["bQM"], L["bG"], L["bYT"]
        sbank, obank, ptbuf, normalize, mem_attn, out_proj = (L["sbank"], L["obank"], L["ptbuf"], L["normalize"],
                                                              L["mem_attn"], L["out_proj"])
        KA = self.alloc(2 * S, BF16).rearrange("p (j t) -> p j t", j=2)
        KB = self.alloc(2 * S, BF16).rearrange("p (j t) -> p j t", j=2)
        VA = self.alloc(NT * 768, BF16).rearrange("p (t n) -> p t n", t=NT)
        bKA, bKB, bVA = Buf(), Buf(), Buf()
        self.dma(KA, self.KA_T.ap().rearrange("(j p) t -> p j t", p=128), writes=[bKA])
        for a in range(0, NT, 8):
            self.dma(VA[:, a:a + 8, :], self.VAB.ap().rearrange("(t p) n -> p t n", p=128)[:, a:a + 8, :], writes=[bVA])
        self.dma(KB, self.KB_T.ap().rearrange("(j p) t -> p j t", p=128), writes=[bKB])
        QA = self.alloc(4 * 2 * 512, BF16).rearrange("p (c e t) -> p c e t", c=4, e=2)
        QB = self.alloc(4 * 2 * 512, BF16).rearrange("p (c e t) -> p c e t", c=4, e=2)
        bQA, bQB = Buf(), Buf()
        for (qz, bq) in ((QA, bQA), (QB, bQB)):
            self.memset("pool", qz[64:128, :, 0, :], 0.0, writes=[bq])
            self.memset("pool", qz[0:64, :, 1, :], 0.0, writes=[bq])
        PTF = [(self.alloc(384), Buf()) for _ in range(2)]
        EB = self.alloc(8 * 384, BF16).rearrange("p (h n) -> p h n", h=8)
        bEB = Buf()
        oh1 = self.alloc(640)
        erb = self.alloc(8)
        esink = self.alloc(8)
        boh, berb, bes = Buf(), Buf(), Buf()
        oh1b = self.alloc(640, BF16)
        erbb = self.alloc(8, BF16)
        boh2, berb2 = Buf(), Buf()
        self.dma(oh1[0:32, :], self.c_oh1.ap(), writes=[boh])
        self.dma(erb[0:32, :], self.relb.ap(), writes=[berb])
        self.dma(esink, self.sink.ap().broadcast_to([128, 8]), writes=[bes])
        self.cp("dve", oh1b[0:32, :], oh1[0:32, :], reads=[boh], writes=[boh2])
        self.act(erbb[0:32, :], erb[0:32, :], AF.Exp, reads=[berb], writes=[berb2])
        self.act(esink, esink, AF.Exp, reads=[bes], writes=[bes])
        ps6 = self.PS[:, 0:3072]
        for g in range(3):
            for qq in range(128):
                u0 = (g - 1) * 128 - qq + 256
                idx = g * 128 + qq
                bk = idx // 64
                self.mm(ps6[:, idx * 8:(idx + 1) * 8], oh1b[0:32, u0:u0 + 128], erbb[0:32, 0:8], True, True,
                        reads=[boh2, berb2], writes=[self.bb[bk]])
        ps6v = ps6.rearrange("p (n h) -> p n h", h=8)
        for h in range(8):
            self.cp("dve" if h % 2 == 0 else "act", EB[:, h, :], ps6v[:, :, h], reads=self.bb[0:6], writes=[bEB])

        def normalize0(*a, **k):
            return normalize(*a, **k)

        def load_block(b, what):
            sl = slice(b * 512, (b + 1) * 512)
            if what == "QA":
                v = self.QA_T.ap().rearrange("(c p) t -> p c t", p=128)
                self.dma(QA[0:64, :, 0, :], v[0:64, :, sl], writes=[bQA])
                self.dma(QA[64:128, :, 1, :], v[64:128, :, sl], writes=[bQA])
            elif what == "QB":
                v = self.QB_T.ap().rearrange("(c p) t -> p c t", p=128)
                self.dma(QB[0:64, :, 0, :], v[0:64, :, sl], writes=[bQB])
                self.dma(QB[64:128, :, 1, :], v[64:128, :, sl], writes=[bQB])
            elif what == "QM":
                self.dma(QM, self.QM_T.ap().rearrange("(c p) t -> p c t", p=128)[:, :, sl], writes=[bQM])
            else:
                self.dma(G, self.G_T.ap().rearrange("(c p) t -> p c t", p=128)[:, :, sl], writes=[bG])

        for w in ("QA", "G", "QB", "QM"):
            load_block(0, w)
        LA = 2
        for b in range(NB):
            for c in range(4):
                j = c // 2
                obs = obank(2)
                units = [(kt, hh) for kt in range(NT) for hh in range(2)]
                sbs = {}

                def qk(u):
                    kt, hh = u
                    sb, bsb = sbank()
                    self.mm(sb, KA[:, j, kt * 128:(kt + 1) * 128], QA[:, c, hh, :], True, True,
                            reads=[bKA, bQA], writes=[bsb])
                    sbs[u] = (sb, bsb)

                for u in units[:LA]:
                    qk(u)
                for idx, u in enumerate(units):
                    if idx + LA < len(units):
                        qk(units[idx + LA])
                    kt, hh = u
                    sb, bsb = sbs.pop(u)
                    pt, bpt = ptbuf()
                    self.act(pt[:, 0:512], sb, AF.Exp, reads=[bsb], writes=[bpt], scale=0.125)
                    ob, bob = obs[hh]
                    v0 = j * 192 + hh * 64
                    self.mm(ob, VA[:, kt, v0:v0 + 128], pt[:, 0:512], kt == 0, kt == NT - 1, reads=[bVA, bpt], writes=[bob])
                for hh in range(2):
                    ob, bob = obs[hh]
                    normalize(ob, bob, (hh * 64, hh * 64 + 64), ((1 - hh) * 64, (1 - hh) * 64 + 64), c, mode="dve", yeng="pool")
            if b + 1 < NB:
                load_block(b + 1, "QA")
            bunits = [(h, ql) for h in range(8) for ql in range(4)]
            bsbs, bobs = {}, {}

            def b_stage1(u):
                h, ql = u
                c, hh, j = h // 2, h % 2, h // 4
                i = b * 4 + ql
                gs = [g for g in range(3) if 0 <= i + g - 1 < NT]
                sb, bsb = sbank()
                for g in gs:
                    kt = i + g - 1
                    self.mm(sb[:, g * 128:(g + 1) * 128], KB[:, j, kt * 128:(kt + 1) * 128],
                            QB[:, c, hh, ql * 128:(ql + 1) * 128], True, True, reads=[bKB, bQB], writes=[bsb])
                bsbs[u] = (sb, bsb, gs)

            def b_stage2(u, n):
                h, ql = u
                c, hh, j = h // 2, h % 2, h // 4
                r0, r1 = hh * 64, hh * 64 + 64
                i = b * 4 + ql
                if ql == 0:
                    bobs[h] = obank(1)[0]
                ob, bob = bobs[h]
                sb, bsb, gs = bsbs.pop(u)
                c0, c1 = gs[0] * 128, (gs[-1] + 1) * 128
                pf, bpf = PTF[n % 2]
                self.act(pf[:, c0:c1], sb[:, c0:c1], AF.Exp, reads=[bsb], writes=[bpf], scale=0.125)
                pt, bpt = ptbuf()
                self.tt("dve", pt[:, c0:c1], pf[:, c0:c1], EB[:, h, c0:c1], ALU.mult, reads=[bpf, bEB], writes=[bpt])
                v0 = (2 + j) * 192 + hh * 64
                for g in gs:
                    kt = i + g - 1
                    self.mm(ob[:, ql * 128:(ql + 1) * 128], VA[:, kt, v0:v0 + 128], pt[:, g * 128:(g + 1) * 128],
                            g == gs[0], g == gs[-1], reads=[bVA, bpt], writes=[bob])
                if ql == 3:
                    d0 = (1 - hh) * 64
                    pend.append((n + 2, lambda: normalize(ob, bob, (r0, r1), (d0, d0 + 64), 4 + c,
                                                          add_scalar=(esink[d0:d0 + 64, h:h + 1], bes))))

            LB = 3
            pend = []
            for u in bunits[:LB]:
                b_stage1(u)
            for n, u in enumerate(bunits):
                if n + LB < len(bunits):
                    b_stage1(bunits[n + LB])
                b_stage2(u, n)
                while pend and pend[0][0] <= n:
                    pend.pop(0)[1]()
            while pend:
                pend.pop(0)[1]()
            if b + 1 < NB:
                load_block(b + 1, "QB")
            mem_attn(b)
            if b + 1 < NB:
                load_block(b + 1, "QM")
            out_proj(b)
            if b + 1 < NB:
                load_block(b + 1, "G")

    def attn_layer1(self, L):
        P = self.P
        QM, G, YT, bQM, bG, bYT = L["QM"], L["G"], L["YT"], L["bQM"], L["bG"], L["bYT"]
        sbank, obank, ptbuf, normalize, mem_attn, out_proj = (L["sbank"], L["obank"], L["ptbuf"], L["normalize"],
                                                              L["mem_attn"], L["out_proj"])
        st = L["st"]
        E = self.alloc(16 * 9 * 128, BF16).rearrange("p (h t q) -> p h t q", h=16, t=9)
        bE = Buf()
        save = self.aoff
        erT = self.alloc(240)
        berT = Buf()
        self.dma(erT[0:31, :], self.rpbT.ap(), writes=[berT])
        self.act(erT[0:31, :], erT[0:31, :], AF.Exp, reads=[berT], writes=[berT])
        self.memset("pool", E.rearrange("p h t q -> p (h t q)"), 0.0, writes=[bE])
        ohp = [(self.alloc(1024), Buf()) for _ in range(2)]
        tiles = [(-3, False), (-2, False), (-2, True), (-1, False), (0, False), (1, False), (2, True), (2, False), (3, False)]
        ps4 = self.PS[:, 0:2048].rearrange("p (q n) -> p q n", n=256)
        cnt = 0
        for r in range(8):
            oh, boh = ohp[r % 2]
            self.dma(oh[0:31, :], self.c_ohc.ap()[:, r * 1024:(r + 1) * 1024], writes=[boh])
            for i in range(8):
                self.mm(self.PS[:, i * 256:i * 256 + 240], oh[0:31, i * 128:(i + 1) * 128], erT[0:31, 0:240], True, True,
                        reads=[boh, berT], writes=[self.bb[i // 2]])
            for ti, (c, masked) in enumerate(tiles):
                for kl in range(2):
                    for ql in range(2):
                        dkr = 2 * c + kl - ql
                        if abs(dkr) > 7:
                            continue
                        if masked and not (-4 <= dkr <= 3):
                            continue
                        dr = dkr + 7
                        src = ps4[kl * 64:(kl + 1) * 64, :, 0:240].rearrange("p q (h r) -> p h q r", r=15)[:, :, :, dr]
                        dst = E[kl * 64:(kl + 1) * 64, :, ti, ql * 64 + r * 8:ql * 64 + r * 8 + 8]
                        self.cp("dve" if cnt % 2 == 0 else "act", dst, src, reads=self.bb[0:4], writes=[bE])
                        cnt += 1
        self.P.barrier()
        self.aoff = save
        R = 12
        KR = self.alloc(8 * R * 128, BF16).rearrange("p (c t) -> p c t", c=8)
        VR = self.alloc(R * 1536, BF16).rearrange("p (s n) -> p s n", s=R)
        bKR = [Buf() for _ in range(R)]
        bVR = [Buf() for _ in range(R)]
        QC = self.alloc(8 * 2 * 512, BF16).rearrange("p (c e t) -> p c e t", c=8, e=2)
        bQC = Buf()
        self.memset("pool", QC[64:128, :, 0, :], 0.0, writes=[bQC])
        self.memset("pool", QC[0:64, :, 1, :], 0.0, writes=[bQC])
        sslot = [Buf() for _ in range(20)]
        PTF = [(self.alloc(640), Buf()) for _ in range(2)]
        kview = self.KC_T.ap().rearrange("(c p) t -> p c t", p=128)

        def load_tile(kt):
            s_ = kt % R
            self.dma(KR[:, :, s_ * 128:(s_ + 1) * 128], kview[:, :, kt * 128:(kt + 1) * 128], writes=[bKR[s_]])
            self.dma(VR[:, s_, :], self.VC.ap()[kt * 128:(kt + 1) * 128, :], writes=[bVR[s_]])

        def load_block(b, what):
            sl = slice(b * 512, (b + 1) * 512)
            if what == "QC":
                v = self.QC_T.ap().rearrange("(c p) t -> p c t", p=128)
                self.dma(QC[0:64, :, 0, :], v[0:64, :, sl], writes=[bQC])
                self.dma(QC[64:128, :, 1, :], v[64:128, :, sl], writes=[bQC])
            elif what == "QM":
                self.dma(QM, self.QM_T.ap().rearrange("(c p) t -> p c t", p=128)[:, :, sl], writes=[bQM])
            else:
                self.dma(G, self.G_T.ap().rearrange("(c p) t -> p c t", p=128)[:, :, sl], writes=[bG])

        def window(qt):
            if qt == 0:
                return [(0, 4), (1, 5), (2, 7), (3, 8)]
            if qt == 1:
                return [(0, 3), (1, 4), (2, 5), (3, 7)]
            if qt == 30:
                return [(28, 1), (29, 3), (30, 4), (31, 5)]
            if qt == 31:
                return [(28, 0), (29, 1), (30, 3), (31, 4)]
            return [(qt - 2 + i, 2 + i) for i in range(5)]

        for w in ("QC", "G", "QM"):
            load_block(0, w)
        for kt in range(0, 6):
            load_tile(kt)
        for b in range(NB):
            if b + 1 < NB:
                for kt in range(4 * b + 6, min(4 * b + 10, NT)):
                    load_tile(kt)
            cunits = [(h, ql) for h in range(16) for ql in range(4)]
            cst, cobs = {}, {}

            def c_stage1(u, n):
                h, ql = u
                c, hh = h // 2, h % 2
                win = window(b * 4 + ql)
                base = (n % 3) * 8
                for i, (kt, ti) in enumerate(win):
                    s_ = kt % R
                    col = (base + i) * 128
                    self.mm(self.PS[:, col:col + 128], KR[:, c, s_ * 128:(s_ + 1) * 128],
                            QC[:, c, hh, ql * 128:(ql + 1) * 128], True, True,
                            reads=[bKR[s_], bQC], writes=[self.bb[(base + i) // 4]])
                cst[u] = (win, base)

            def c_stage2(u, n):
                h, ql = u
                c, hh = h // 2, h % 2
                r0, r1 = hh * 64, hh * 64 + 64
                if ql == 0:
                    cobs[h] = obank(1)[0]
                ob, bob = cobs[h]
                win, base = cst.pop(u)
                nw = len(win)
                sb2 = self.PS[:, base * 128:(base + nw) * 128]
                pf, bpf = PTF[n % 2]
                self.act(pf[:, 0:nw * 128], sb2, AF.Exp, reads=[self.bb[base // 4], self.bb[base // 4 + 1]], writes=[bpf], scale=0.125)
                pt, bpt = ptbuf()
                i0 = 0
                while i0 < nw:
                    i1 = i0
                    while i1 + 1 < nw and win[i1 + 1][1] == win[i1][1] + 1:
                        i1 += 1
                    t0, t1 = win[i0][1], win[i1][1]
                    ev = E[:, h, t0:t1 + 1, :].rearrange("p t q -> p (t q)")
                    self.tt("dve", pt[:, i0 * 128:(i1 + 1) * 128], pf[:, i0 * 128:(i1 + 1) * 128], ev, ALU.mult,
                            reads=[bpf, bE], writes=[bpt])
                    i0 = i1 + 1
                v0 = c * 192 + hh * 64
                for i, (kt, ti) in enumerate(win):
                    s_ = kt % R
                    self.mm(ob[:, ql * 128:(ql + 1) * 128], VR[:, s_, v0:v0 + 128], pt[:, i * 128:(i + 1) * 128],
                            i == 0, i == nw - 1, reads=[bVR[s_], bpt], writes=[bob])
                if ql == 3:
                    d0 = (1 - hh) * 64
                    pend.append((n + 0,
                                 lambda: normalize(ob, bob, (r0, r1), (d0, d0 + 64), c, yeng="pool")))

            LC = 2
            pend = []
            for n, u in enumerate(cunits[:LC]):
                c_stage1(u, n)
            for n, u in enumerate(cunits):
                if n + LC < len(cunits):
                    c_stage1(cunits[n + LC], n + LC)
                c_stage2(u, n)
                while pend and pend[0][0] <= n:
                    pend.pop(0)[1]()
            while pend:
                pend.pop(0)[1]()
            if b + 1 < NB:
                load_block(b + 1, "QC")
            mem_attn(b)
            if b + 1 < NB:
                load_block(b + 1, "QM")
            out_proj(b)
            if b + 1 < NB:
                load_block(b + 1, "G")


def prep_inputs(inputs):
    f = lambda a: np.ascontiguousarray(np.asarray(a, dtype=np.float32))
    x = f(inputs["x"])
    mem = f(inputs["mem"])
    ng = f(inputs["norm_gain"])
    mg = f(inputs["mem_norm_gain"])
    gains = np.concatenate([ng[0].reshape(8, 128).T, ng[1].reshape(8, 128).T, mg.reshape(8, 128).T], axis=1)
    qkg = np.stack([np.tile(f(inputs["q_norm_a"])[0], 2), np.tile(f(inputs["k_norm_a"])[0], 2)], axis=1)
    shared = {
        "w_in_even": f(inputs["w_in_even"])[0], "w_in_odd": f(inputs["w_in_odd"])[0],
        "w_out_even": f(inputs["w_out_even"])[0], "w_out_odd": f(inputs["w_out_odd"])[0],
        "w_mem_kv": f(inputs["w_mem_kv"]),
        "gains_pp": f(gains), "qk_gain": f(qkg),
        "final_gain": f(inputs["final_norm_gain"]).reshape(1, D),
        "sink_b": f(inputs["sink_b"]).reshape(1, 8),
        "rel_bias": f(inputs["rel_bias"]),
        "rpbT": f(np.transpose(f(inputs["rpb_c"])[0], (2, 0, 1)).reshape(31, 240)),
    }
    shared.update(host_consts())
    in_maps = []
    for b in range(8):
        m = dict(shared)
        m["x"] = x[b]
        m["mem"] = mem[b]
        in_maps.append(m)
    return in_maps


def kernel(**inputs):
    bld = Builder()
    nc = bld.build()
    in_maps = prep_inputs(inputs)
    res = run_bass_kernel_spmd(nc, in_maps, core_ids=list(range(8)))
    return np.stack([np.asarray(r["out"]) for r in res.results], axis=0).astype(np.float32)
```
